# Optimizing a Trainium2 kernel written in Bass

```python
import math
import jax, jax.numpy as jnp
from jax import lax
import numpy as np

D_MODEL = 1024
BATCH = 8
SEQ = 2048
DEPTH = 4

HEAD_DIM = 64
SB_HEADS = 8
SB_WIDTH = SB_HEADS * HEAD_DIM
NSA_HEADS = 8
NSA_KV_HEADS = 2
NSA_GROUP = NSA_HEADS // NSA_KV_HEADS
NSA_WIDTH = NSA_HEADS * HEAD_DIM
NSA_KV_WIDTH = NSA_KV_HEADS * HEAD_DIM
CMP_LEN = 32
CMP_STRIDE = 16
CMP_HIDDEN = 128
SLC_BLOCK = 64
SLC_TOPK = 8
WINDOW = 256
Q_BLOCK = 128
ROPE_THETA = 10000.0
EPS = 1e-6
NEG = -1e30
FORCE = 1e4
IN_SIZES = (SB_WIDTH, SB_WIDTH, SB_WIDTH, SB_WIDTH,
            NSA_WIDTH, NSA_KV_WIDTH, NSA_KV_WIDTH, NSA_KV_WIDTH, NSA_KV_WIDTH, NSA_KV_WIDTH, NSA_KV_WIDTH,
            NSA_WIDTH, 3 * NSA_HEADS, D_MODEL, D_MODEL)
N_IN = sum(IN_SIZES)

kernel_name = "hybrid_stickbreaking_nsa_adaln"


def rms_norm(x, g):
    xf = x.astype(jnp.float32)
    y = xf * lax.rsqrt(jnp.mean(xf * xf, axis=-1, keepdims=True) + EPS)
    return (y * g.astype(jnp.float32)).astype(x.dtype)


def rope(x, pos):
    half = x.shape[-1] // 2
    freq = ROPE_THETA ** (-jnp.arange(half, dtype=jnp.float32) / half)
    ang = pos.astype(jnp.float32)[:, None] * freq[None, :]
    cos = jnp.cos(ang)[None, :, None, :]
    sin = jnp.sin(ang)[None, :, None, :]
    xf = x.astype(jnp.float32)
    x1, x2 = xf[..., :half], xf[..., half:]
    return jnp.concatenate([x1 * cos - x2 * sin, x2 * cos + x1 * sin], axis=-1).astype(x.dtype)


def stick_breaking_attention(q, k, v):
    B, S, H, d = q.shape
    scale = d ** -0.5
    qh, kh, vh = (a.transpose(0, 2, 1, 3) for a in (q, k, v))
    outs = []
    for i in range(S // Q_BLOCK):
        t0, t1 = i * Q_BLOCK, (i + 1) * Q_BLOCK
        z = jnp.einsum('bhqd,bhkd->bhqk', qh[:, :, t0:t1], kh[:, :, :t1]).astype(jnp.float32) * scale
        strict = jnp.arange(t1)[None, :] < jnp.arange(t0, t1)[:, None]
        log_keep = jnp.where(strict, jax.nn.log_sigmoid(-z), 0.0)
        between = lax.cumsum(log_keep, axis=3, reverse=True) - log_keep
        w = jnp.where(strict, jnp.exp(jax.nn.log_sigmoid(z) + between), 0.0)
        outs.append(jnp.einsum('bhqk,bhkd->bqhd', w.astype(v.dtype), vh[:, :, :t1]))
    return jnp.concatenate(outs, axis=1)


def compress(x, pe, w1, w2):
    B, S, G, d = x.shape
    n = (S - CMP_LEN) // CMP_STRIDE + 1
    idx = CMP_STRIDE * jnp.arange(n)[:, None] + jnp.arange(CMP_LEN)[None, :]
    blocks = x[:, idx] + pe[None, None, :, None, :]
    flat = blocks.transpose(0, 1, 3, 2, 4).reshape(B, n, G, CMP_LEN * d)
    hid = jax.nn.silu(jnp.einsum('bngf,fh->bngh', flat, w1))
    return jnp.einsum('bngh,hd->bngd', hid, w2)


def band_blocks(x, n_q, n_band):
    B, S, G, d = x.shape
    xp = jnp.pad(x.transpose(0, 2, 1, 3), ((0, 0), (0, 0), (WINDOW, 0), (0, 0)))
    xb = xp.reshape(B, G, n_q + n_band - 1, Q_BLOCK, d)
    return jnp.concatenate([xb[:, :, o:o + n_q] for o in range(n_band)], axis=3)


def native_sparse_attention(q, kc, vc, ks, vs, kw, vw, gate_logits):
    B, S, H, d = q.shape
    G = NSA_KV_HEADS
    hpg = NSA_GROUP
    scale = d ** -0.5
    dt = q.dtype
    qg = q.reshape(B, S, G, hpg, d)
    t_all = jnp.arange(S)

    n = kc.shape[1]
    ends = CMP_STRIDE * jnp.arange(n) + CMP_LEN - 1
    valid = ends[None, :] <= t_all[:, None]
    s_c = jnp.einsum('bsghd,bngd->bghsn', qg, kc).astype(jnp.float32) * scale
    p_c = jnp.where(valid, jax.nn.softmax(jnp.where(valid, s_c, NEG), axis=-1), 0.0)
    o_cmp = jnp.einsum('bghsn,bngd->bsghd', p_c.astype(dt), vc)

    nb = S // SLC_BLOCK
    ci = jnp.arange(n)[:, None]
    sj = jnp.arange(nb)[None, :]
    overlap = ((CMP_STRIDE * ci < SLC_BLOCK * (sj + 1)) &
               (CMP_STRIDE * ci + CMP_LEN > SLC_BLOCK * sj)).astype(jnp.float32)
    imp = jnp.einsum('bghsn,nj->bgsj', p_c, overlap)
    cur = (t_all // SLC_BLOCK)[:, None]
    jb = jnp.arange(nb)[None, :]
    imp = jnp.where(jb > cur, NEG, imp)
    imp = jnp.where((jb == 0) | (jb == cur - 1), FORCE, imp)
    imp = jnp.where(jb == cur, 2.0 * FORCE, imp)
    k_sel = min(SLC_TOPK, nb)
    _, sel = lax.top_k(imp, k_sel)

    ks_blocks = ks.transpose(0, 2, 1, 3).reshape(B, G, nb, SLC_BLOCK, d)
    vs_blocks = vs.transpose(0, 2, 1, 3).reshape(B, G, nb, SLC_BLOCK, d)
    n_c = S // Q_BLOCK
    q_chunks = qg.reshape(B, n_c, Q_BLOCK, G, hpg, d).transpose(1, 0, 2, 3, 4, 5)
    sel_chunks = sel.reshape(B, G, n_c, Q_BLOCK, k_sel).transpose(2, 0, 1, 3, 4)
    t_chunks = t_all.reshape(n_c, Q_BLOCK)
    gather = jax.vmap(jax.vmap(lambda blk, ix: blk[ix]))

    def select_chunk(args):
        qc, ic, tc = args
        kg = gather(ks_blocks, ic).reshape(B, G, Q_BLOCK, k_sel * SLC_BLOCK, d)
        vg = gather(vs_blocks, ic).reshape(B, G, Q_BLOCK, k_sel * SLC_BLOCK, d)
        pos = (ic[..., None] * SLC_BLOCK + jnp.arange(SLC_BLOCK)).reshape(B, G, Q_BLOCK, k_sel * SLC_BLOCK)
        mask = (pos <= tc[None, None, :, None])[:, :, None]
        sc = jnp.einsum('bqghd,bgqnd->bghqn', qc, kg).astype(jnp.float32) * scale
        p = jax.nn.softmax(jnp.where(mask, sc, NEG), axis=-1)
        return jnp.einsum('bghqn,bgqnd->bqghd', p.astype(dt), vg)

    o_slc = lax.map(select_chunk, (q_chunks, sel_chunks, t_chunks))
    o_slc = o_slc.transpose(1, 0, 2, 3, 4, 5).reshape(B, S, G, hpg, d)

    n_q = S // Q_BLOCK
    n_band = WINDOW // Q_BLOCK + 1
    kb = band_blocks(kw, n_q, n_band)
    vb = band_blocks(vw, n_q, n_band)
    qb = qg.reshape(B, n_q, Q_BLOCK, G, hpg, d)
    tpos = t_all.reshape(n_q, Q_BLOCK)[:, :, None]
    spos = (jnp.arange(n_q) * Q_BLOCK - WINDOW)[:, None, None] + jnp.arange(n_band * Q_BLOCK)[None, None, :]
    wmask = (spos <= tpos) & (spos > tpos - WINDOW) & (spos >= 0)
    s_w = jnp.einsum('bnqghd,bgnkd->bghnqk', qb, kb).astype(jnp.float32) * scale
    p_w = jax.nn.softmax(jnp.where(wmask, s_w, NEG), axis=-1)
    o_win = jnp.einsum('bghnqk,bgnkd->bnqghd', p_w.astype(dt), vb).reshape(B, S, G, hpg, d)

    g = jax.nn.sigmoid(gate_logits.astype(jnp.float32)).reshape(B, S, 3, G, hpg)[..., None].astype(dt)
    out = g[:, :, 0] * o_cmp + g[:, :, 1] * o_slc + g[:, :, 2] * o_win
    return out.reshape(B, S, H * d)


def hybrid_layer(x, c, ada_w, ada_b, norm_g, w_in, q_norm_g, kc_norm_g, ks_norm_g, kw_norm_g,
                 cmp_pe_k, cmp_w1_k, cmp_w2_k, cmp_pe_v, cmp_w1_v, cmp_w2_v, w_proj_a, w_proj_b, w_out):
    B, S, _ = x.shape
    mod = jax.nn.silu(c) @ ada_w + ada_b
    shift, scale, gate = jnp.split(mod, 3, axis=-1)
    h = rms_norm(x, norm_g) * (1 + scale[:, None]) + shift[:, None]
    proj = h @ w_in
    (sb_q, sb_k, sb_v, sb_z, n_q, k_c, v_c, k_s, v_s, k_w, v_w, n_z, n_g, m_a, m_b) = jnp.split(
        proj, np.cumsum(IN_SIZES)[:-1].tolist(), axis=-1)
    pos = jnp.arange(S)

    def heads(t, nh):
        return t.reshape(B, S, nh, HEAD_DIM)

    a = stick_breaking_attention(heads(sb_q, SB_HEADS), heads(sb_k, SB_HEADS), heads(sb_v, SB_HEADS))
    a = a.reshape(B, S, SB_WIDTH) * jax.nn.silu(sb_z)

    q = rope(rms_norm(heads(n_q, NSA_HEADS), q_norm_g), pos)
    kc = compress(heads(k_c, NSA_KV_HEADS), cmp_pe_k, cmp_w1_k, cmp_w2_k)
    kc = rope(rms_norm(kc, kc_norm_g), CMP_STRIDE * jnp.arange(kc.shape[1]) + CMP_LEN - 1)
    vc = compress(heads(v_c, NSA_KV_HEADS), cmp_pe_v, cmp_w1_v, cmp_w2_v)
    ks = rope(rms_norm(heads(k_s, NSA_KV_HEADS), ks_norm_g), pos)
    kw = rope(rms_norm(heads(k_w, NSA_KV_HEADS), kw_norm_g), pos)
    b = native_sparse_attention(q, kc, vc, ks, heads(v_s, NSA_KV_HEADS), kw, heads(v_w, NSA_KV_HEADS), n_g)
    b = b * jax.nn.silu(n_z)

    y = jax.nn.sigmoid(m_a) * (a @ w_proj_a) + jax.nn.sigmoid(m_b) * (b @ w_proj_b)
    return x + gate[:, None] * (y @ w_out)


def setup_inputs(seed: int = 0) -> dict:
    key = jax.random.key(seed)
    ks = jax.random.split(key, 24)
    L, D, d = DEPTH, D_MODEL, HEAD_DIM
    nrm = lambda k, shape, s: jax.random.normal(k, shape, jnp.float32) * s
    return {
        "x": nrm(ks[0], (BATCH, SEQ, D), 1.0),
        "c": nrm(ks[1], (BATCH, D), 1.0),
        "ada_w": nrm(ks[2], (L, D, 3 * D), 0.5 * D ** -0.5),
        "ada_b": nrm(ks[3], (L, 3 * D), 0.01),
        "norm_g": 1.0 + nrm(ks[4], (L, D), 0.02),
        "w_in": nrm(ks[5], (L, D, N_IN), D ** -0.5),
        "q_norm_g": 1.0 + nrm(ks[6], (L, d), 0.02),
        "kc_norm_g": 1.0 + nrm(ks[7], (L, d), 0.02),
        "ks_norm_g": 1.0 + nrm(ks[8], (L, d), 0.02),
        "kw_norm_g": 1.0 + nrm(ks[9], (L, d), 0.02),
        "cmp_pe_k": nrm(ks[10], (L, CMP_LEN, d), 0.1),
        "cmp_w1_k": nrm(ks[11], (L, CMP_LEN * d, CMP_HIDDEN), (CMP_LEN * d) ** -0.5),
        "cmp_w2_k": nrm(ks[12], (L, CMP_HIDDEN, d), CMP_HIDDEN ** -0.5),
        "cmp_pe_v": nrm(ks[13], (L, CMP_LEN, d), 0.1),
        "cmp_w1_v": nrm(ks[14], (L, CMP_LEN * d, CMP_HIDDEN), (CMP_LEN * d) ** -0.5),
        "cmp_w2_v": nrm(ks[15], (L, CMP_HIDDEN, d), CMP_HIDDEN ** -0.5),
        "w_proj_a": nrm(ks[16], (L, SB_WIDTH, D), SB_WIDTH ** -0.5),
        "w_proj_b": nrm(ks[17], (L, NSA_WIDTH, D), NSA_WIDTH ** -0.5),
        "w_out": nrm(ks[18], (L, D, D), D ** -0.5),
    }


def reference(x, c, ada_w, ada_b, norm_g, w_in, q_norm_g, kc_norm_g, ks_norm_g, kw_norm_g,
              cmp_pe_k, cmp_w1_k, cmp_w2_k, cmp_pe_v, cmp_w1_v, cmp_w2_v, w_proj_a, w_proj_b, w_out):
    for l in range(DEPTH):
        x = hybrid_layer(x, c, ada_w[l], ada_b[l], norm_g[l], w_in[l], q_norm_g[l], kc_norm_g[l],
                         ks_norm_g[l], kw_norm_g[l], cmp_pe_k[l], cmp_w1_k[l], cmp_w2_k[l],
                         cmp_pe_v[l], cmp_w1_v[l], cmp_w2_v[l], w_proj_a[l], w_proj_b[l], w_out[l])
    return x
```

```python
from contextlib import ExitStack
import numpy as np
import concourse.bass as bass
import concourse.mybir as mybir
from concourse.bass_utils import run_bass_kernel_spmd

F32 = mybir.dt.float32
BF16 = mybir.dt.bfloat16
AF = mybir.ActivationFunctionType
ALU = mybir.AluOpType

D = 1024
S = 2048
L_DEPTH = 4
NT = 16
NEGB = -32768.0
EPS = 1e-6
N_CMP = 127

import os as _os
SAME_ENGINE_SYNC = _os.environ.get("NO_SES", "") != "1"
SES_RAW_ONLY = _os.environ.get("NO_SES", "") == "2"


class Tracker:
    def __init__(self, nc, stack):
        self.nc = nc
        self.stack = stack
        self.engs = {"pe": nc.tensor, "act": nc.scalar, "dve": nc.vector,
                     "pool": nc.gpsimd, "sp": nc.sync}
        self.sems = {}
        self.count = {}
        for e in self.engs:
            self.sems[e] = stack.enter_context(nc.semaphore("s_" + e))
            self.count[e] = 0
        self.last_write = {}
        self.reads = {}
        self.seen = {e: {} for e in self.engs}
        self.n_waits = 0
        self.n_inst = 0
        self.log = {e: [] for e in self.engs}
        self._pending_waits = {e: [] for e in self.engs}

    def check_deadlock(self):
        cnt = {k: 0 for k in self.sems}
        pos = {e: 0 for e in self.engs}
        progress = True
        while progress:
            progress = False
            for e in self.engs:
                q = self.log[e]
                while pos[e] < len(q):
                    waits, inc = q[pos[e]]
                    if all(cnt[s_] >= v for s_, v in waits):
                        if inc is not None:
                            cnt[inc[0]] += inc[1]
                        pos[e] += 1
                        progress = True
                    else:
                        break
        stuck = {e: (pos[e], len(self.log[e]), self.log[e][pos[e]][0], {s_: cnt[s_] for s_, _ in self.log[e][pos[e]][0]})
                 for e in self.engs if pos[e] < len(self.log[e])}
        return stuck

    def _dma_src(self, key):
        sid = ("dma", key)
        if sid not in self.sems:
            nm = "d_" + "_".join(str(k) for k in (key if isinstance(key, tuple) else (key,)))
            self.sems[sid] = self.stack.enter_context(self.nc.semaphore(nm[:40]))
            self.count[sid] = 0
        return sid

    def _deps(self, e, reads, writes):
        need = {}

        def add(src, val):
            if src == e and (not SAME_ENGINE_SYNC or e in ("pe", "sp") or val > self.count[e]):
                return
            if need.get(src, 0) < val:
                need[src] = val

        for k in reads:
            if k in self.last_write:
                add(*self.last_write[k])
        for k in writes:
            if k in self.last_write:
                if not (SES_RAW_ONLY and self.last_write[k][0] == e):
                    add(*self.last_write[k])
            for src, val in self.reads.get(k, {}).items():
                if not (SES_RAW_ONLY and src == e):
                    add(src, val)
        out = []
        for src, val in need.items():
            if self.seen[e].get(src, 0) >= val:
                continue
            self.seen[e][src] = val
            out.append((src, val))
        return out

    def _emit_waits(self, e, deps):
        eng = self.engs[e]
        for src, val in deps:
            eng.wait_ge(self.sems[src], val)
            self.n_waits += 1
            self._pending_waits[e].append((src, val))

    def op(self, e, fn, reads=(), writes=(), inc=True):
        deps = self._deps(e, reads, writes)
        self._emit_waits(e, deps)
        ins = fn()
        self.n_inst += 1
        val = self.count[e] + 1
        self.log[e].append((self._pending_waits[e], (e, 1) if inc else None))
        self._pending_waits[e] = []
        if inc:
            ins.then_inc(self.sems[e], 1)
            self.count[e] = val
        for k in reads:
            d = self.reads.setdefault(k, {})
            d[e] = max(d.get(e, 0), val)
        for k in writes:
            self.last_write[k] = (e, val)
            self.reads[k] = {}
        return ins

    def dma(self, e, out, in_, reads=(), writes=(), **kw):
        deps = self._deps(e, reads, writes)
        self._emit_waits(e, deps)
        key = writes[0] if writes else ("rd",) + tuple(reads[:1])
        sid = self._dma_src(key)
        ins = self.engs[e].dma_start(out=out, in_=in_, **kw)
        self.log[e].append((self._pending_waits[e], (sid, 16)))
        self._pending_waits[e] = []
        self.count[sid] += 16
        val = self.count[sid]
        ins.then_inc(self.sems[sid], 16)
        self.n_inst += 1
        for k in reads:
            self.reads.setdefault(k, {})[sid] = val
        for k in writes:
            self.last_write[k] = (sid, val)
            self.reads[k] = {}
        return ins

    def barrier(self):
        for e in self.engs:
            for src, val in self.count.items():
                if src == e or val == 0:
                    continue
                if self.seen[e].get(src, 0) >= val:
                    continue
                self.seen[e][src] = val
                self.engs[e].wait_ge(self.sems[src], val)
                self.n_waits += 1
                self._pending_waits[e].append((src, val))
        self.last_write = {}
        self.reads = {}

    def finish(self, e="sp"):
        for src, val in self.count.items():
            if src == e or val == 0:
                continue
            if self.seen[e].get(src, 0) >= val:
                continue
            self.seen[e][src] = val
            self.engs[e].wait_ge(self.sems[src], val)


def bccol(ap, n):
    dims = [list(d) for d in ap.ap]
    dims[-1] = [0, n]
    return bass.AP(tensor=ap.tensor, offset=ap.offset, ap=dims)


def bcast(ap, pos, n):
    dims = [list(d) for d in ap.ap]
    dims.insert(1 + pos, [0, n])
    return bass.AP(tensor=ap.tensor, offset=ap.offset, ap=dims)


SWAP64 = np.concatenate([np.arange(32, 64), np.arange(0, 32)])


def _w_blocks():
    blks = []
    a = np.arange

    def add(name, cols):
        blks.append((name, "w_in", np.asarray(cols), 8))

    for j in range(4):
        add(f"sbv{j}", 1024 + j * 128 + a(128))
    for c in range(4):
        add(f"sbq{c}", 0 + c * 128 + a(128))
        add(f"sbk{c}", 512 + c * 128 + a(128))
        add(f"sbz{c}", 1536 + c * 128 + a(128))
    add("kc", 2560 + a(128))
    add("vc", 2688 + a(128))
    for c in range(4):
        hs = (c, c + 4)
        add(f"nq{c}", np.concatenate([2048 + h * 64 + a(64) for h in hs]))
        add(f"nqs{c}", np.concatenate([2048 + h * 64 + SWAP64 for h in hs]))
    add("ks", 2816 + a(128))
    add("kss", np.concatenate([2816 + g * 64 + SWAP64 for g in range(2)]))
    add("kw", 3072 + a(128))
    add("kws", np.concatenate([3072 + g * 64 + SWAP64 for g in range(2)]))
    add("vs", 2944 + a(128))
    add("vw", 3200 + a(128))
    add("ng", 3840 + a(24))
    for j in range(4):
        add(f"nz{j}", 3328 + j * 128 + a(128))
    for n in range(8):
        blks.append((f"wpa{n}", "w_proj_a", n * 128 + a(128), 4))
        blks.append((f"wpb{n}", "w_proj_b", n * 128 + a(128), 4))
        add(f"ma{n}", 3864 + n * 128 + a(128))
        add(f"mb{n}", 4888 + n * 128 + a(128))
    for n in range(8):
        blks.append((f"wo{n}", "w_out", n * 128 + a(128), 8))
    return blks


W_BLOCKS = _w_blocks()
W_OFF = {}
_off = 0
for _name, _src, _cols, _nk in W_BLOCKS:
    W_OFF[_name] = (_off, _nk, len(_cols))
    _off += _nk * len(_cols)
W_TOT = _off


def _consts():
    c = {}
    half = 32
    freq = (np.float32(10000.0) ** (-np.arange(half, dtype=np.float32) / np.float32(half))).astype(np.float32)
    pos = np.arange(S, dtype=np.float32)
    ang = (pos[:, None] * freq[None, :]).astype(np.float32)
    cos = np.cos(ang).astype(np.float32).T
    sin = np.sin(ang).astype(np.float32).T
    p = np.arange(128)
    sign = np.where((p % 64) < 32, -1.0, 1.0).astype(np.float32)[:, None]
    cos2 = cos[p % 32]
    sin2 = sin[p % 32] * sign
    c["cos"] = np.ascontiguousarray(cos2)
    c["sin"] = np.ascontiguousarray(sin2)
    posc = 16 * np.arange(N_CMP) + 31
    cc = np.zeros((128, 128), np.float32)
    sc = np.zeros((128, 128), np.float32)
    cc[:, :N_CMP] = cos2[:, posc]
    sc[:, :N_CMP] = sin2[:, posc]
    c["cosc"] = cc
    c["sinc"] = sc
    ident = np.eye(128, dtype=np.float32)
    ones = np.ones((128, 128), np.float32)
    jj, ss = np.meshgrid(np.arange(128), np.arange(128), indexing="ij")
    tri = (jj >= ss).astype(np.float32)
    bd = ((jj // 64) == (ss // 64)).astype(np.float32)
    w2m = np.where(jj > ss, 0.0, NEGB).astype(np.float32)
    c["cmat"] = np.concatenate([ident, ones, tri, bd, w2m, -8.0 * tri, -8.0 * ones], axis=1)
    u = np.arange(896)[None, :]
    s_ = np.arange(128)[:, None]
    c["big"] = np.where(s_ < u - 384, 0.0, NEGB).astype(np.float32)
    n_ = np.arange(128)[:, None]
    t_ = np.arange(S)[None, :]
    c["vbias"] = np.where((16 * n_ + 31 <= t_) & (n_ < N_CMP), 0.0, NEGB).astype(np.float32)
    ci = np.arange(128)[:, None]
    sj = np.arange(32)[None, :]
    ov = ((16 * ci < 64 * (sj + 1)) & (16 * ci + 32 > 64 * sj) & (ci < N_CMP)).astype(np.float32)
    c["ov1"] = np.concatenate([np.ones((128, 1), np.float32), ov], axis=1)
    t = np.arange(S)
    cur = (t // 64)[:, None]
    jb = np.arange(32)[None, :]
    m1 = np.ones((S, 32), np.float32)
    ad = np.zeros((S, 32), np.float32)
    fut = jb > cur
    m1[fut] = 0.0
    ad[fut] = -1e30
    frc = ((jb == 0) | (jb == cur - 1)) & ~fut
    m1[frc] = 0.0
    ad[frc] = 1e4
    cu = jb == cur
    m1[cu] = 0.0
    ad[cu] = 2e4
    c["impm"] = np.ascontiguousarray(m1.reshape(NT, 128, 32).transpose(1, 0, 2)).reshape(128, NT * 32)
    c["impa"] = np.ascontiguousarray(ad.reshape(NT, 128, 32).transpose(1, 0, 2)).reshape(128, NT * 32)
    return c


def prep_inputs(inp):
    f = lambda a: np.ascontiguousarray(np.asarray(a, dtype=np.float32))
    L = L_DEPTH
    shared = {}
    wst = np.empty((L, 128, W_TOT), np.float32)
    srcs = {k: f(inp[k]) for k in ("w_in", "w_proj_a", "w_proj_b", "w_out")}
    for name, src, cols, nk in W_BLOCKS:
        off, _, nc_ = W_OFF[name]
        w = srcs[src][:, :, cols]
        w = w.reshape(L, nk, 128, nc_).transpose(0, 2, 1, 3).reshape(L, 128, nk * nc_)
        wst[:, :, off:off + nk * nc_] = w
    shared["wst"] = wst
    ada_w = f(inp["ada_w"])
    shared["ada_w"] = np.ascontiguousarray(
        ada_w.reshape(L, 8, 128, 6, 512).transpose(0, 3, 2, 1, 4).reshape(L, 6, 128, 8 * 512))
    shared["ada_b"] = np.ascontiguousarray(f(inp["ada_b"]).reshape(L, 24, 128).transpose(0, 2, 1))
    shared["norm_g"] = np.ascontiguousarray(f(inp["norm_g"]).reshape(L, 8, 128).transpose(0, 2, 1))
    p = np.arange(128)
    gv = np.zeros((L, 128, 8), np.float32)
    for i, k in enumerate(("q_norm_g", "ks_norm_g", "kw_norm_g", "kc_norm_g")):
        g = f(inp[k])
        gv[:, :, 2 * i] = g[:, p % 64]
        gv[:, :, 2 * i + 1] = g[:, SWAP64[p % 64]]
    shared["gvec"] = gv
    for nm in ("k", "v"):
        w1 = f(inp[f"cmp_w1_{nm}"])
        w1 = w1.reshape(L, 32, 64, 128).transpose(0, 2, 1, 3).reshape(L, 64, 32 * 128)
        shared[f"w1{nm}"] = np.ascontiguousarray(np.concatenate([w1, w1], axis=1))
        pe = f(inp[f"cmp_pe_{nm}"]).transpose(0, 2, 1)
        shared[f"pe{nm}"] = np.ascontiguousarray(np.concatenate([pe, pe], axis=1))
    w2k = f(inp["cmp_w2_k"])
    shared["w2k"] = np.ascontiguousarray(np.concatenate([w2k, w2k[:, :, SWAP64]], axis=2))
    shared["w2v"] = f(inp["cmp_w2_v"])
    shared.update(_consts())
    x = f(inp["x"])
    c = f(inp["c"])
    maps = []
    for b in range(8):
        m = dict(shared)
        m["x"] = x[b]
        m["c"] = np.ascontiguousarray(c[b].reshape(8, 128).T)
        maps.append(m)
    return maps


IN_SHAPES = {
    "x": [S, D], "c": [128, 8], "wst": [L_DEPTH, 128, W_TOT], "ada_w": [L_DEPTH, 6, 128, 4096],
    "ada_b": [L_DEPTH, 128, 24], "norm_g": [L_DEPTH, 128, 8], "gvec": [L_DEPTH, 128, 8],
    "w1k": [L_DEPTH, 128, 4096], "w1v": [L_DEPTH, 128, 4096], "pek": [L_DEPTH, 128, 32],
    "pev": [L_DEPTH, 128, 32], "w2k": [L_DEPTH, 128, 128], "w2v": [L_DEPTH, 128, 64],
    "cos": [128, S], "sin": [128, S], "cosc": [128, 128], "sinc": [128, 128],
    "cmat": [128, 896], "big": [128, 896], "vbias": [128, S], "ov1": [128, 33],
    "impm": [128, NT * 32], "impa": [128, NT * 32],
}


def build_program(n_layers=L_DEPTH, debug=(), stop_after=None):
    nc = bass.Bass("TRN2", target_bir_lowering=False)
    I = {k: nc.dram_tensor(k, shp, F32, kind="ExternalInput").ap() for k, shp in IN_SHAPES.items()}
    out_d = nc.dram_tensor("out", [S, D], F32, kind="ExternalOutput").ap()
    dbg_out = {}

    with ExitStack() as st:
        T = Tracker(nc, st)

        uid = [0]

        def sb(stack, name, shape, dt):
            uid[0] += 1
            return stack.enter_context(nc.sbuf_tensor(f"sb{uid[0]}_{name}", shape, dt))

        PS = [st.enter_context(nc.psum_tensor(f"ps{i}", [128, 512], F32)) for i in range(8)]
        PSK = [("ps", i) for i in range(8)]

        def mm(out, lhsT, rhs, start, stop, reads, writes, inc=None):
            if inc is None:
                inc = stop
            return T.op("pe", lambda: nc.tensor.matmul(out, lhsT=lhsT, rhs=rhs, start=start, stop=stop,
                                                       skip_group_check=True),
                        reads, writes, inc=inc)

        def act(out, in_, func, reads, writes, **kw):
            return T.op("act", lambda: nc.scalar.activation(out, in_, func, **kw), reads, writes)

        def E(eng):
            return nc.vector if eng == "dve" else nc.gpsimd

        def tt(out, in0, in1, op, reads, writes, eng="dve"):
            return T.op(eng, lambda: E(eng).tensor_tensor(out, in0, in1, op), reads, writes)

        def ts(out, in0, s1, s2, op0, op1, reads, writes, eng="dve"):
            if s2 is None:
                return T.op(eng, lambda: E(eng).tensor_scalar(out, in0, s1, None, op0), reads, writes)
            return T.op(eng, lambda: E(eng).tensor_scalar(out, in0, s1, s2, op0, op1), reads, writes)

        def stt(out, in0, scalar, in1, op0, op1, reads, writes):
            return T.op("dve", lambda: nc.vector.scalar_tensor_tensor(out, in0, scalar, in1, op0, op1), reads, writes)

        def cp(out, in_, reads, writes, eng="dve"):
            if eng == "act":
                return T.op("act", lambda: nc.scalar.copy(out, in_), reads, writes)
            return T.op(eng, lambda: E(eng).tensor_copy(out, in_), reads, writes)

        def recip(out, in_, reads, writes):
            return T.op("dve", lambda: nc.vector.reciprocal(out, in_), reads, writes)

        def dump(name, ap, reads):
            if name not in debug:
                return
            shp = list(ap.shape)
            d = nc.dram_tensor("dbg_" + name, shp, ap.dtype, kind="ExternalOutput").ap()
            dbg_out[name] = d
            T.dma("sp", d, ap, reads=reads)

        xT = sb(st, "xT", [128, 8, S], F32)
        hT = sb(st, "hT", [128, 8, S], BF16)
        aT = sb(st, "aT", [128, 4, S], BF16)
        bT = sb(st, "bT", [128, 4, S], BF16)
        NSLOT = 8
        wpool = sb(st, "wpool", [128, NSLOT, 1024], BF16)
        cmat_f = sb(st, "cmat_f", [128, 128], F32)
        cmat = sb(st, "cmat", [128, 896], BF16)
        big = sb(st, "big", [128, 896], BF16)
        ov1 = sb(st, "ov1", [128, 33], BF16)
        cosc = sb(st, "cosc", [128, 128], F32)
        sinc = sb(st, "sinc", [128, 128], F32)
        c_sb = sb(st, "c_sb", [128, 8], F32)
        siluc = sb(st, "siluc", [128, 8, 2], F32)
        small = sb(st, "small", [128, 128], F32)
        siluc_bf = sb(st, "siluc_bf", [128, 8, 2], BF16)
        ident_bf = cmat[:, 0:128]
        ones_bf = cmat[:, 128:256]
        tri_bf = cmat[:, 256:384]
        bd_bf = cmat[:, 384:512]
        w2m_bf = cmat[:, 512:640]
        tri8_bf = cmat[:, 640:768]
        ones8_bf = cmat[:, 768:896]
        def smv(par):
            o = 64 * par
            return (small[:, o:o + 8], small[:, o + 8:o + 16], small[:, o + 16:o + 24], small[:, o + 24:o + 32],
                    small[:, o + 32:o + 40], small[:, o + 40:o + 64])

        def emit_mod(lm, stack):
            par = lm % 2
            gsv, shiftv, gatev, normg, gvec, modT = smv(par)
            adab = [sb(stack, f"adab{i}", [128, 8, 512], BF16) for i in range(2)]
            adab_b = sb(stack, "adab_b", [128, 24], F32)
            T.dma("sp", adab_b[:], I["ada_b"][lm], writes=["adab_b"])
            T.dma("sp", normg, I["norm_g"][lm], writes=[("small_ng", par)])
            T.dma("sp", gvec, I["gvec"][lm], writes=[("small_gv", par)])
            for nb in range(6):
                ab = adab[nb % 2]
                ak = ("adab", nb % 2)
                T.dma("pool", ab[:], I["ada_w"][lm, nb].rearrange("p (a b) -> p a b", a=8), writes=[ak], **CAST)
                for jj in range(4):
                    j = nb * 4 + jj
                    for kc in range(8):
                        mm(PS[7][:, 2 * j:2 * j + 2], ab[:, kc, jj * 128:(jj + 1) * 128], siluc_bf[:, kc, :],
                           kc == 0, kc == 7, [ak, "siluc_bf"], [PSK[7]])
            tt(modT, PS[7][:, 0:48:2], adab_b[:], ALU.add, [PSK[7], "adab_b"], [("modT", par)])
            stt(gsv, modT[:, 8:16], 1.0, normg, ALU.add, ALU.mult, [("modT", par), ("small_ng", par)], [("gsv", par)])
            cp(shiftv, modT[:, 0:8], [("modT", par)], [("shiftv", par)])
            cp(gatev, modT[:, 16:24], [("modT", par)], [("gatev", par)])
            dump(f"mod{lm}", modT, [("modT", par)])

        CAST = dict(max_dma_last_dim=4096)
        T.dma("sp", cmat_f[:], I["cmat"][:, 0:128], writes=["cmat_f"])
        T.dma("pool", cmat[:], I["cmat"], writes=["cmat"], **CAST)
        T.dma("pool", big[:], I["big"], writes=["big"], **CAST)
        T.dma("pool", ov1[:], I["ov1"], writes=["ov1"], **CAST)
        T.dma("sp", cosc[:], I["cosc"], writes=["cosc"])
        T.dma("sp", sinc[:], I["sinc"], writes=["sinc"])
        T.dma("sp", c_sb[:], I["c"], writes=["c_sb"])

        wstate = {"half": 0}

        def load_group(l, names):
            half = wstate["half"]
            wstate["half"] ^= 1
            res = {}
            for i, nm in enumerate(names):
                off, nk, ncol = W_OFF[nm]
                slot = half * 4 + i
                T.dma("pool", wpool[:, slot, 0:nk * ncol], I["wst"][l, :, off:off + nk * ncol],
                      writes=[("w", slot)], **CAST)
                res[nm] = (slot, nk, ncol)
            return res

        def wview(info, kc):
            slot, nk, ncol = info
            return wpool[:, slot, kc * ncol:(kc + 1) * ncol]

        def proj_F(psum_ap, pkey, info, tq, extra_reads=()):
            slot, nk, ncol = info
            for kc in range(8):
                mm(psum_ap, wview(info, kc), hT[:, kc, tq * 512:(tq + 1) * 512], kc == 0, kc == 7,
                   [("w", slot), "hT"] + list(extra_reads), [pkey])

        with ExitStack() as ps_:
            xin = [sb(ps_, f"xin{i}", [128, D], F32) for i in range(2)]
            for t in range(NT):
                xi = xin[t % 2]
                xk = ("xin", t % 2)
                T.dma("sp", xi[:], I["x"][t * 128:(t + 1) * 128, :], writes=[xk])
                for half in range(2):
                    pb = PS[(2 * t + half) % 4]
                    pk = PSK[(2 * t + half) % 4]
                    for j in range(4):
                        c = half * 4 + j
                        T.op("pe", lambda: nc.tensor.transpose(pb[:, j * 128:(j + 1) * 128],
                                                               xi[:, c * 128:(c + 1) * 128], cmat_f[:]),
                             [xk, "cmat_f"], [pk], inc=(j == 3))
                    cp(xT[:, half * 4:half * 4 + 4, t * 128:(t + 1) * 128],
                       pb[:].rearrange("p (a b) -> p a b", a=4), [pk], ["xT"],
                       eng="dve" if half == 0 else "act")
            act(siluc[:, :, 0], c_sb[:], AF.Exp, ["c_sb"], ["siluc"], scale=-1.0)
            ts(siluc[:, :, 0], siluc[:, :, 0], 1.0, None, ALU.add, None, ["siluc"], ["siluc"])
            recip(siluc[:, :, 0], siluc[:, :, 0], ["siluc"], ["siluc"])
            tt(siluc[:, :, 0], siluc[:, :, 0], c_sb[:], ALU.mult, ["siluc", "c_sb"], ["siluc"])
            cp(siluc[:, :, 1], siluc[:, :, 0], ["siluc"], ["siluc"])
            cp(siluc_bf[:], siluc[:], ["siluc"], ["siluc_bf"])
            if stop_after != ("pro", 0):
                emit_mod(0, ps_)
            T.barrier()

        for l in range(n_layers if stop_after != ("pro", 0) else 0):
            with ExitStack() as s1:
                par = l % 2
                gsv, shiftv, gatev, normg, gvec, modT = smv(par)
                sqt = [sb(s1, f"sqt{i}", [128, 512], BF16) for i in range(2)]
                lnv = sb(s1, "lnv", [128, 512], F32)
                rstd = sb(s1, "rstd", [128, 512], F32)
                tmpf = [sb(s1, f"tmpf{i}", [128, 512], F32) for i in range(2)]
                for tq in range(4 if stop_after != ("s1a", l) else 0):
                    tsl = slice(tq * 512, (tq + 1) * 512)
                    for c in range(8):
                        sq = sqt[c % 2]
                        act(sq[:], xT[:, c, tsl], AF.Square, ["xT"], [("sqt", c % 2)])
                        mm(PS[1][:], ones_bf, sq[:], c == 0, c == 7, [("sqt", c % 2), "cmat"], [PSK[1]], inc=True)
                    act(lnv[:], PS[1][:], AF.Ln, [PSK[1]], ["lnv"], scale=1.0 / D, bias=EPS)
                    act(rstd[:], lnv[:], AF.Exp, ["lnv"], ["rstd"], scale=-0.5)
                    for c in range(8):
                        tf = tmpf[c % 2]
                        stt(tf[:], xT[:, c, tsl], gsv[:, c:c + 1], rstd[:], ALU.mult, ALU.mult,
                            ["xT", ("gsv", par), "rstd"], [("tmpf", c % 2)])
                        act(hT[:, c, tsl], tf[:], AF.Identity, [("tmpf", c % 2), ("shiftv", par)], ["hT"],
                            bias=shiftv[:, c:c + 1], scale=1.0)
                dump(f"hT{l}", hT[:, :, 0:256], ["hT"])
                T.barrier()
            if stop_after in (("s1", l), ("s1a", l)):
                break

            with ExitStack() as s2:
                sbvm = [[sb(s2, f"sbvm{pp}{i}", [128, NT, 128], BF16) for i in range(2)] for pp in range(2)]
                qmb = [[sb(s2, f"qm{pp}{i}", [128, S], BF16) for i in range(2)] for pp in range(2)]
                kcb = [sb(s2, f"kc_{pp}", [128, S], BF16) for pp in range(2)]
                NB = 3
                e_t = [sb(s2, f"e_t{i}", [128, 512], F32) for i in range(2)]
                sp_t = [sb(s2, f"sp_t{i}", [128, 512], BF16) for i in range(NB)]
                w_t = [sb(s2, f"w_t{i}", [128, 512], BF16) for i in range(NB)]
                lacc = [sb(s2, f"lacc{i}", [128, 512], BF16) for i in range(2)]
                zr = [sb(s2, f"zr{i}", [128, 512], F32) for i in range(2)]
                for pp in range(2):
                    T.op("pool", lambda: nc.gpsimd.memset(qmb[pp][0][64:128, :], 0.0), [], [f"qm{pp}0"])
                    T.op("pool", lambda: nc.gpsimd.memset(qmb[pp][1][0:64, :], 0.0), [], [f"qm{pp}1"])
                    T.op("pool", lambda: nc.gpsimd.memset(sbvm[pp][0][:, :, 64:128], 0.0), [], [f"sbvm{pp}0"])
                    T.op("pool", lambda: nc.gpsimd.memset(sbvm[pp][1][:, :, 0:64], 0.0), [], [f"sbvm{pp}1"])
                nxt = load_group(l, ["sbq0", "sbk0", "sbz0", "sbv0"])
                qi = 0

                def v_unit(cn, gn, t):
                    pp = cn % 2
                    info = gn[f"sbv{cn}"]
                    pb, pk = PS[7], PSK[7]
                    for kc in range(8):
                        mm(pb[:, 0:128], hT[:, kc, t * 128:(t + 1) * 128], wview(info, kc), kc == 0, kc == 7,
                           ["hT", ("w", info[0])], [pk])
                    cp(sbvm[pp][0][:, t, 0:64], pb[:, 0:64], [pk], [f"sbvm{pp}0"], eng="dve")
                    cp(sbvm[pp][1][:, t, 64:128], pb[:, 64:128], [pk], [f"sbvm{pp}1"], eng="dve")

                def proj_unit(cn, gn, tq, which):
                    pp = cn % 2
                    tsl = slice(tq * 512, (tq + 1) * 512)
                    pb, pk = PS[7], PSK[7]
                    if which == "q":
                        proj_F(pb[:], pk, gn[f"sbq{cn}"], tq)
                        cp(qmb[pp][0][0:64, tsl], pb[0:64, :], [pk], [f"qm{pp}0"], eng="dve")
                        cp(qmb[pp][1][64:128, tsl], pb[64:128, :], [pk], [f"qm{pp}1"], eng="dve")
                    elif which == "k":
                        proj_F(pb[:], pk, gn[f"sbk{cn}"], tq)
                        cp(kcb[pp][:, tsl], pb[:], [pk], [f"kc_{pp}"], eng="dve")
                    else:
                        v_unit(cn, gn, tq)

                for tq in range(4):
                    proj_unit(0, nxt, tq, "q")
                    proj_unit(0, nxt, tq, "k")
                for t in range(NT):
                    v_unit(0, nxt, t)
                for c in range(4):
                    g = nxt
                    pending = []
                    if c < 3:
                        nxt = load_group(l, [f"sbq{c + 1}", f"sbk{c + 1}", f"sbz{c + 1}", f"sbv{c + 1}"])
                        pending = [(c + 1, nxt, tq, w_) for tq in range(4) for w_ in "qk"]
                        pending += [(c + 1, nxt, t, "v") for t in range(NT)]
                    qm = qmb[c % 2]
                    kc_ = kcb[c % 2]
                    qmk = [f"qm{c % 2}0", f"qm{c % 2}1"]
                    kck = f"kc_{c % 2}"
                    tiles = []
                    for Q in range(4):
                        nkb = 4 * Q + 4
                        for idx in range(nkb):
                            for hh in range(2):
                                kb = nkb - 1 - idx
                                tiles.append(dict(Q=Q, hh=hh, kb=kb, first=(idx == 0), last=(kb == 0),
                                                  c0=(128 * (kb - 4 * Q) if kb >= 4 * Q else 0),
                                                  diag=(kb >= 4 * Q)))
                    ntl = len(tiles)
                    accb = lambda Q: (PS[4 + (qi + Q) % 2], PSK[4 + (qi + Q) % 2])

                    def zgate(Q):
                        z = zr[(qi + Q) % 2]
                        zk = ("zr", (qi + Q) % 2)
                        proj_F(PS[6][:], PSK[6], g[f"sbz{c}"], Q)
                        act(z[:], PS[6][:], AF.Exp, [PSK[6]], [zk], scale=-1.0)
                        act(z[:], z[:], AF.Ln, [zk], [zk], bias=1.0)
                        act(z[:], z[:], AF.Exp, [zk], [zk], scale=-1.0)
                        tt(z[:], PS[6][:], z[:], ALU.mult, [PSK[6], zk], [zk])

                    def P1(j):
                        t_ = tiles[j]
                        Q, hh, kb, c0 = t_["Q"], t_["hh"], t_["kb"], t_["c0"]
                        zs, zsk = PS[j % 4], PSK[j % 4]
                        if t_["first"] and hh == 0:
                            zgate(Q)
                        cols = slice(Q * 512 + c0, (Q + 1) * 512)
                        mm(zs[:, c0:512], kc_[:, kb * 128:(kb + 1) * 128], qm[hh][:, cols], True, not t_["diag"],
                           [kck, qmk[hh]], [zsk], inc=True)
                        if t_["diag"]:
                            mm(zs[:, c0:512], ident_bf, big[:, 384:896 - c0], False, True, ["cmat", "big"], [zsk])

                    def A1a(j):
                        t_ = tiles[j]
                        c0 = t_["c0"]
                        zs, zsk = PS[j % 4], PSK[j % 4]
                        e, ek = e_t[j % 2], ("e_t", j % 2)
                        act(e[:, c0:512], zs[:, c0:512], AF.Exp, [zsk], [ek], scale=0.125)

                    def A1b(j):
                        t_ = tiles[j]
                        c0 = t_["c0"]
                        e, ek = e_t[j % 2], ("e_t", j % 2)
                        sp, spk = sp_t[j % NB], ("sp_t", j % NB)
                        act(sp[:, c0:512], e[:, c0:512], AF.Ln, [ek], [spk], bias=1.0)

                    def P2(j):
                        t_ = tiles[j]
                        c0, hh = t_["c0"], t_["hh"]
                        zs, zsk = PS[j % 4], PSK[j % 4]
                        sp, spk = sp_t[j % NB], ("sp_t", j % NB)
                        lk = ("lacc", hh)
                        if t_["first"]:
                            T.op("dve", lambda: nc.vector.memset(lacc[hh][:], 0.0), [], [lk])
                            mm(zs[:, c0:512], tri8_bf, sp[:, c0:512], False, True, ["cmat", spk], [zsk])
                        else:
                            mm(zs[:, c0:512], tri8_bf, sp[:, c0:512], False, False, ["cmat", spk], [zsk], inc=False)
                            mm(zs[:, c0:512], ones8_bf, lacc[hh][:, c0:512], False, True, ["cmat", lk], [zsk])
                        if not t_["last"]:
                            tt(lacc[hh][:, c0:512], lacc[hh][:, c0:512], sp[:, c0:512], ALU.add, [lk, spk], [lk])

                    def A2(j):
                        t_ = tiles[j]
                        c0 = t_["c0"]
                        zs, zsk = PS[j % 4], PSK[j % 4]
                        w, wk_ = w_t[j % NB], ("w_t", j % NB)
                        act(w[:, c0:512], zs[:, c0:512], AF.Exp, [zsk], [wk_], scale=0.125)

                    def P3(j):
                        t_ = tiles[j]
                        Q, hh, kb, c0 = t_["Q"], t_["hh"], t_["kb"], t_["c0"]
                        h = 2 * c + hh
                        w, wk_ = w_t[j % NB], ("w_t", j % NB)
                        ab, abk = accb(Q)
                        mm(ab[:, c0:512], sbvm[c % 2][hh][:, kb, :], w[:, c0:512],
                           t_["first"] and hh == 0, t_["last"] and hh == 1, [f"sbvm{c % 2}{hh}", wk_], [abk], inc=True)
                        if t_["last"] and hh == 1:
                            qs = slice(Q * 512, (Q + 1) * 512)
                            if c == 0 and Q == 0:
                                cp(e_t[0][:], ab[:], [abk], [("e_t", 0)])
                                dump(f"sba{l}", e_t[0][:], [("e_t", 0)])
                            tt(aT[:, c, qs], ab[:], zr[(qi + Q) % 2][:], ALU.mult, [abk, ("zr", (qi + Q) % 2)], ["aT"])

                    for j in range(ntl + 4):
                        if j < ntl:
                            P1(j)
                        if 0 <= j - 1 < ntl:
                            A1a(j - 1)
                        if 0 <= j - 3 < ntl:
                            A2(j - 3)
                        if 0 <= j - 1 < ntl:
                            A1b(j - 1)
                        if 0 <= j - 2 < ntl:
                            P2(j - 2)
                        if 0 <= j - 4 < ntl:
                            P3(j - 4)
                        if pending and j >= 4 and (j - 4) % 3 == 0:
                            proj_unit(*pending.pop(0))
                    while pending:
                        proj_unit(*pending.pop(0))
                    qi += 4
                dump(f"aT{l}", aT[:, :, 0:256], ["aT"])
                T.barrier()
            if stop_after == ("s2", l):
                break

            with ExitStack() as s3:
                vc1 = sb(s3, "vc1", [128, 2, 98], BF16)
                kcT = sb(s3, "kcT", [128, 128], BF16)
                sigg = sb(s3, "sigg", [128, NT, 32], F32)
                pb16 = [sb(s3, f"pb16_{i}", [128, 512], BF16) for i in range(3)]
                sml = sb(s3, "sml", [128, 256], F32)
                negsel = sb(s3, "negsel", [128, 2, 32], BF16)
                for g_ in range(2):
                    cp(vc1[:, g_, 64:97], ov1[:, :], ["ov1"], ["vc1"], eng="pool")
                with ExitStack() as s3a:
                    wk = [sb(s3a, f"wk{i}", [128, 512], F32) for i in range(4)]
                    kcr = sb(s3a, "kcr", [128, S], BF16)
                    vcr = sb(s3a, "vcr", [128, S], BF16)
                    w1k = sb(s3a, "w1k", [128, 32, 128], BF16)
                    w1v = sb(s3a, "w1v", [128, 32, 128], BF16)
                    pek = sb(s3a, "pek", [128, 32], BF16)
                    pev = sb(s3a, "pev", [128, 32], BF16)
                    w2k = sb(s3a, "w2k", [128, 128], BF16)
                    w2v = sb(s3a, "w2v", [128, 64], BF16)
                    hid = sb(s3a, "hid", [128, 128], BF16)
                    T.dma("pool", w1k[:], I["w1k"][l].rearrange("p (a b) -> p a b", a=32), writes=["w1k"], **CAST)
                    T.dma("pool", w1v[:], I["w1v"][l].rearrange("p (a b) -> p a b", a=32), writes=["w1v"], **CAST)
                    T.dma("pool", pek[:], I["pek"][l], writes=["pek"], **CAST)
                    T.dma("pool", pev[:], I["pev"][l], writes=["pev"], **CAST)
                    T.dma("pool", w2k[:], I["w2k"][l], writes=["w2k"], **CAST)
                    T.dma("pool", w2v[:], I["w2v"][l], writes=["w2v"], **CAST)
                    g = load_group(l, ["kc", "vc"])
                    for tq in range(4):
                        proj_F(PS[0][:], PSK[0], g["kc"], tq)
                        cp(kcr[:, tq * 512:(tq + 1) * 512], PS[0][:], [PSK[0]], ["kcr"], eng="dve")
                        proj_F(PS[1][:], PSK[1], g["vc"], tq)
                        cp(vcr[:, tq * 512:(tq + 1) * 512], PS[1][:], [PSK[1]], ["vcr"], eng="act")
                    for kv, raw, rawk, w1, w1key, pe_, pekey in (("k", kcr, "kcr", w1k, "w1k", pek, "pek"),
                                                                 ("v", vcr, "vcr", w1v, "w1v", pev, "pev")):
                        for g_ in range(2):
                            po = slice(64 * g_, 64 * g_ + 64)
                            pre = PS[2][:, 0:N_CMP]
                            for li in range(32):
                                mm(pre, w1[po, li, :], raw[po, li:li + 16 * (N_CMP - 1) + 1:16], li == 0, False,
                                   [w1key, rawk], [PSK[2]], inc=False)
                                mm(pre, w1[po, li, :], bccol(pe_[po, li:li + 1], N_CMP),
                                   False, li == 31, [w1key, pekey], [PSK[2]], inc=(li == 31))
                            e0 = wk[0][:, 0:N_CMP]
                            act(e0, pre, AF.Exp, [PSK[2]], [("wk", 0)], scale=-1.0)
                            ts(e0, e0, 1.0, None, ALU.add, None, [("wk", 0)], [("wk", 0)])
                            recip(e0, e0, [("wk", 0)], [("wk", 0)])
                            tt(hid[:, 0:N_CMP], pre, e0, ALU.mult, [PSK[2], ("wk", 0)], ["hid"])
                            if kv == "k":
                                pk_, pk2 = PS[3][po, 0:N_CMP], PS[3][po, 128:128 + N_CMP]
                                mm(pk_, w2k[:, 0:64], hid[:, 0:N_CMP], True, True, ["w2k", "hid"], [PSK[3]])
                                mm(pk2, w2k[:, 64:128], hid[:, 0:N_CMP], True, True, ["w2k", "hid"], [PSK[3]])
                                sq = pb16[0][po, 0:N_CMP]
                                act(sq, pk_, AF.Square, [PSK[3]], [("pb16", 0)])
                                ssq = PS[5][po, 0:N_CMP]
                                mm(ssq, ones_bf[po, 0:64], sq, True, True, ["cmat", ("pb16", 0)], [PSK[5]])
                                lv = wk[1][po, 0:N_CMP]
                                act(lv, ssq, AF.Ln, [PSK[5]], [("wk", 1)], scale=1.0 / 64, bias=EPS)
                                act(lv, lv, AF.Exp, [("wk", 1)], [("wk", 1)], scale=-0.5)
                                kn = wk[2][po, 0:N_CMP]
                                kns = wk[3][po, 0:N_CMP]
                                stt(kn, pk_, gvec[po, 6:7], lv, ALU.mult, ALU.mult, [PSK[3], ("small_gv", par), ("wk", 1)],
                                    [("wk", 2)])
                                stt(kns, pk2, gvec[po, 7:8], lv, ALU.mult, ALU.mult, [PSK[3], ("small_gv", par), ("wk", 1)],
                                    [("wk", 3)])
                                tt(kn, kn, cosc[po, 0:N_CMP], ALU.mult, [("wk", 2), "cosc"], [("wk", 2)])
                                tt(kns, kns, sinc[po, 0:N_CMP], ALU.mult, [("wk", 3), "sinc"], [("wk", 3)], eng="pool")
                                tt(kcT[po, 0:N_CMP], kn, kns, ALU.add, [("wk", 2), ("wk", 3)], ["kcT"])
                            else:
                                pv_ = PS[3][0:N_CMP, 256:320]
                                mm(pv_, hid[:, 0:N_CMP], w2v[:, :], True, True, ["hid", "w2v"], [PSK[3]])
                                cp(vc1[0:N_CMP, g_, 0:64], pv_, [PSK[3]], ["vc1"])
                    dump(f"kcT{l}", kcT[:], ["kcT"])
                    dump(f"vc1{l}", vc1[:], ["vc1"])
                    T.barrier()
                qT = sb(s3, "qT", [128, 4, S], BF16)
                ksT = sb(s3, "ksT", [128, S], BF16)
                kwT = sb(s3, "kwT", [128, S], BF16)
                v1 = sb(s3, "v1", [128, NT, 2, 2, 66], BF16)
                T.op("pool", lambda: nc.gpsimd.memset(v1[:, :, :, :, 64:66], 1.0), [], ["v1"])
                with ExitStack() as s3b:
                    wk = [sb(s3b, f"wk{i}", [128, 512], F32) for i in range(4)]
                    cos_t = sb(s3b, "cos_t", [128, 512], F32)
                    sin_t = sb(s3b, "sin_t", [128, 512], F32)
                    jobs = [(f"nq{c}", f"nqs{c}", qT[:, c, :], "qT", 0) for c in range(4)]
                    jobs += [("ks", "kss", ksT[:], "ksT", 2), ("kw", "kws", kwT[:], "kwT", 4)]
                    nxt = load_group(l, [jobs[0][0], jobs[0][1]])
                    for ji, (na, nb_, dest, dkey, gi) in enumerate(jobs):
                        g = nxt
                        if ji + 1 < len(jobs):
                            nxt = load_group(l, [jobs[ji + 1][0], jobs[ji + 1][1]])
                        else:
                            nxt = load_group(l, ["vs", "vw", "ng"])
                        for tq in range(4):
                            tsl = slice(tq * 512, (tq + 1) * 512)
                            T.dma("sp", cos_t[:], I["cos"][:, tsl], writes=["cos_t"])
                            T.dma("sp", sin_t[:], I["sin"][:, tsl], writes=["sin_t"])
                            pa, pak = PS[tq % 2], PSK[tq % 2]
                            pbb, pbk = PS[2 + tq % 2], PSK[2 + tq % 2]
                            pq, pqk = PS[4 + tq % 2], PSK[4 + tq % 2]
                            proj_F(pa[:], pak, g[na], tq)
                            proj_F(pbb[:], pbk, g[nb_], tq)
                            sq = pb16[tq % 2]
                            sqk = ("pb16", tq % 2)
                            act(sq[:], pa[:], AF.Square, [pak], [sqk])
                            mm(pq[:], bd_bf, sq[:], True, True, ["cmat", sqk], [pqk])
                            act(wk[0][:], pq[:], AF.Ln, [pqk], [("wk", 0)], scale=1.0 / 64, bias=EPS)
                            act(wk[0][:], wk[0][:], AF.Exp, [("wk", 0)], [("wk", 0)], scale=-0.5)
                            stt(wk[1][:], pa[:], gvec[:, gi:gi + 1], wk[0][:], ALU.mult, ALU.mult,
                                [pak, ("small_gv", par), ("wk", 0)], [("wk", 1)])
                            stt(wk[2][:], pbb[:], gvec[:, gi + 1:gi + 2], wk[0][:], ALU.mult, ALU.mult,
                                [pbk, ("small_gv", par), ("wk", 0)], [("wk", 2)])
                            tt(wk[1][:], wk[1][:], cos_t[:], ALU.mult, [("wk", 1), "cos_t"], [("wk", 1)],
                               eng="pool")
                            tt(wk[2][:], wk[2][:], sin_t[:], ALU.mult, [("wk", 2), "sin_t"], [("wk", 2)],
                               eng="pool")
                            tt(dest[:, tsl], wk[1][:], wk[2][:], ALU.add, [("wk", 1), ("wk", 2)], [dkey])
                    dump(f"qT{l}", qT[:, :, 0:256], ["qT"])
                    dump(f"ksT{l}", ksT[:, 0:512], ["ksT"])
                    g = nxt
                    nzg = load_group(l, [f"nz{j}" for j in range(4)])
                    slot0 = g["vs"][0]
                    for t in range(NT):
                        pb, pk = PS[t % 2], PSK[t % 2]
                        for kc in range(8):
                            mm(pb[:, 0:256], hT[:, kc, t * 128:(t + 1) * 128],
                               wpool[:, slot0:slot0 + 2, kc * 128:(kc + 1) * 128],
                               kc == 0, kc == 7, ["hT", ("w", slot0), ("w", slot0 + 1)], [pk])
                        cp(v1[:, t, :, :, 0:64], pb[:, 0:256].rearrange("p (a b c) -> p a b c", a=2, b=2), [pk], ["v1"],
                           eng="dve" if t % 2 == 0 else "act")
                    ngi = g["ng"]
                    for t in range(NT):
                        for kc in range(8):
                            mm(PS[2][:, t * 32:t * 32 + 24], hT[:, kc, t * 128:(t + 1) * 128], wview(ngi, kc),
                               kc == 0, kc == 7, ["hT", ("w", ngi[0])], [PSK[2]], inc=(kc == 7 and t == NT - 1))
                    sg = sigg[:].rearrange("p a b -> p (a b)")
                    act(sg, PS[2][:], AF.Exp, [PSK[2]], ["sigg"], scale=-1.0)
                    ts(sg, sg, 1.0, None, ALU.add, None, ["sigg"], ["sigg"])
                    recip(sg, sg, ["sigg"], ["sigg"])
                    dump(f"sigg{l}", sigg[:], ["sigg"])
                    dump(f"v1{l}", v1[:, 0:2], ["v1"])
                    T.barrier()
                nzslot = nzg["nz0"][0]
                with ExitStack() as s3c:
                    szs = [sb(s3c, "sz0", [128, 512], F32)] * 2
                    qmg = [sb(s3c, f"qmg{i}", [128, 4, 128], BF16) for i in range(2)]
                    T.op("pool", lambda: nc.gpsimd.memset(qmg[0][64:128], 0.0), [], [("qmg", 0)])
                    T.op("pool", lambda: nc.gpsimd.memset(qmg[1][0:64], 0.0), [], [("qmg", 1)])
                    negx = sb(s3c, "negx", [128, 32 * 64], BF16)
                    vbias = sb(s3c, "vbias", [128, S], BF16)
                    impm = sb(s3c, "impm", [128, NT, 32], BF16)
                    impa = sb(s3c, "impa", [128, NT, 32], BF16)
                    T.dma("pool", vbias[:], I["vbias"], writes=["vbias"], **CAST)
                    T.dma("pool", impm[:], I["impm"].rearrange("p (a b) -> p a b", a=NT), writes=["impm"], **CAST)
                    T.dma("pool", impa[:], I["impa"].rearrange("p (a b) -> p a b", a=NT), writes=["impa"], **CAST)
                    ocomb = sb(s3c, "ocomb", [128, 512], F32)
                    b16 = sb(s3c, "b16", [128, 512], BF16)
                    tiles = []
                    for tb in range(NT):
                        kbs = [kb for kb in (tb - 2, tb - 1, tb) if kb >= 0]
                        for g_ in range(2):
                            tiles.append(dict(tb=tb, g=g_, br="c", kb=0, first=True, last=True))
                        for g_ in range(2):
                            for i, kb in enumerate(kbs):
                                tiles.append(dict(tb=tb, g=g_, br="w", kb=kb, first=(i == 0), last=(i == len(kbs) - 1),
                                                  expand=(i == 0 and g_ == 1), head=(i == 0 and g_ == 0)))
                            for kb in range(tb + 1):
                                tiles.append(dict(tb=tb, g=g_, br="s", kb=kb, first=(kb == 0), last=(kb == tb),
                                                  tail=(kb == tb)))
                    ntl = len(tiles)
                    BR = {"c": 0, "w": 1, "s": 2}
                    tmpI = sml[:, 128:256].rearrange("p (h j) -> p h j", h=4)
                    tmpO = sb(s3c, "tmpO", [128, 4, 64], F32)

                    def acc_of(tb, g_, br):
                        b_ = 2 + 2 * BR[br] + g_
                        return PS[b_][:].rearrange("p (a b) -> p a b", a=4), PSK[b_]

                    def tb_head(tb):
                        tbs = slice(tb * 128, (tb + 1) * 128)
                        for kc in range(8):
                            mm(PS[0][:], hT[:, kc, tbs], wpool[:, nzslot:nzslot + 4, kc * 128:(kc + 1) * 128],
                               kc == 0, kc == 7, ["hT"] + [("w", nzslot + i) for i in range(4)], [PSK[0]])
                        sz, szk = szs[0], ("sz", 0)
                        act(sz[:], PS[0][:], AF.Exp, [PSK[0]], [szk], scale=-1.0)
                        act(sz[:], sz[:], AF.Ln, [szk], [szk], bias=1.0)
                        act(sz[:], sz[:], AF.Exp, [szk], [szk], scale=-1.0)
                        tt(sz[:], PS[0][:], sz[:], ALU.mult, [PSK[0], szk], [szk])

                    def expand(tb, g_):
                        nbk = 2 * (tb + 1)
                        cp(negx[:, 0:nbk * 64].rearrange("p (a b) -> p a b", b=64),
                           bcast(negsel[:, g_, 0:nbk], 1, 64), [("negsel", g_)], ["negx"])

                    def P1(j):
                        t_ = tiles[j]
                        tb, g_, br, kb = t_["tb"], t_["g"], t_["br"], t_["kb"]
                        tbs = slice(tb * 128, (tb + 1) * 128)
                        po = slice(64 * g_, 64 * g_ + 64)
                        qg = qmg[g_][:]
                        qk_ = ("qmg", g_)
                        if t_.get("head"):
                            tb_head(tb)
                        if t_.get("expand"):
                            expand(tb, 1)
                        zs, zsk = PS[j % 2], PSK[j % 2]
                        if br == "c":
                            cp(qmg[g_][po], qT[po, :, tbs], ["qT"], [qk_])
                            mm(zs[0:N_CMP, :], kcT[:, 0:N_CMP], qg, True, False, ["kcT", qk_], [zsk], inc=False)
                            mm(zs[0:N_CMP, :], ident_bf[0:N_CMP, 0:N_CMP], bcast(vbias[0:N_CMP, tbs], 0, 4), False, True,
                               ["cmat", "vbias"], [zsk])
                        elif br == "w":
                            nob = (kb == tb - 1)
                            mm(zs[:], kwT[:, kb * 128:(kb + 1) * 128], qg, True, nob, ["kwT", qk_], [zsk], inc=nob)
                            if kb == tb:
                                mm(zs[:], ident_bf, bcast(big[:, 385:513], 0, 4), False, True, ["cmat", "big"], [zsk])
                            elif kb == tb - 2:
                                mm(zs[:], ident_bf, bcast(w2m_bf, 0, 4), False, True, ["cmat"], [zsk])
                        else:
                            mm(zs[:], ksT[:, kb * 128:(kb + 1) * 128], qg, True, False, ["ksT", qk_], [zsk], inc=False)
                            mm(zs[:], negx[:, 2 * kb * 64:(2 * kb + 2) * 64], bcast(ident_bf, 0, 4), False, kb != tb,
                               ["negx", "cmat"], [zsk], inc=(kb != tb))
                            if kb == tb:
                                mm(zs[:], ident_bf, bcast(big[:, 385:513], 0, 4), False, True, ["cmat", "big"], [zsk])

                    def A1(j):
                        t_ = tiles[j]
                        nr = N_CMP if t_["br"] == "c" else 128
                        zs, zsk = PS[j % 2], PSK[j % 2]
                        p, pk = pb16[j % 3], ("pb16", j % 3)
                        act(p[0:nr, :], zs[0:nr, :], AF.Exp, [zsk], [pk], scale=0.125)

                    def P2(j):
                        t_ = tiles[j]
                        tb, g_, br, kb = t_["tb"], t_["g"], t_["br"], t_["kb"]
                        p, pk = pb16[j % 3], ("pb16", j % 3)
                        a3, ak = acc_of(tb, g_, br)
                        for h in range(4):
                            st_ = t_["first"] and h == 0
                            sp_ = t_["last"] and h == 3
                            if br == "c":
                                mm(a3[:, h, 0:97], p[0:N_CMP, h * 128:(h + 1) * 128], vc1[0:N_CMP, g_, 0:97],
                                   st_, sp_, [pk, "vc1"], [ak], inc=(h == 3))
                            else:
                                mm(a3[:, h, 0:65], p[:, h * 128:(h + 1) * 128],
                                   v1[:, kb, 0 if br == "s" else 1, g_, 0:65], st_, sp_, [pk, "v1"], [ak], inc=(h == 3))
                        if br == "c":
                            select(tb, g_)
                        if t_.get("tail"):
                            combine(tb, g_)

                    def select(tb, g_):
                        a3, ak = acc_of(tb, g_, "c")
                        rc = sml[:, 16 * g_:16 * g_ + 4]
                        rck = ("sml_rc", g_)
                        ts(rc, a3[:, :, 64], 1e-30, None, ALU.max, None, [ak], [rck])
                        recip(rc, rc, [rck], [rck])
                        imp = sml[:, 32 + 32 * g_:64 + 32 * g_]
                        ik = ("sml_imp", g_)
                        tt(tmpI[:], a3[:, :, 65:97], bcast(rc, 1, 32), ALU.mult, [ak, rck], ["tmpI"])
                        T.op("dve", lambda: nc.vector.tensor_reduce(out=imp, in_=tmpI[:].rearrange("p h j -> p j h"),
                                                                    axis=mybir.AxisListType.X, op=ALU.add),
                             ["tmpI"], [ik])
                        tt(imp, imp, impm[:, tb, :], ALU.mult, [ik, "impm"], [ik])
                        tt(imp, imp, impa[:, tb, :], ALU.add, [ik, "impa"], [ik])
                        top8 = sml[:, 96 + 8 * g_:104 + 8 * g_]
                        tk = ("sml_top", g_)
                        T.op("dve", lambda: nc.vector.max(out=top8, in_=imp), [ik], [tk])
                        ts(negsel[:, g_, :], imp, top8[:, 7:8], NEGB, ALU.is_lt, ALU.mult, [ik, tk], [("negsel", g_)])
                        if g_ == 0:
                            expand(tb, 0)
                        tt(rc, rc, sigg[:, tb, 0 + 4 * g_:4 + 4 * g_], ALU.mult, [rck, "sigg"], [rck])
                        ok = ("ocomb", g_)
                        tt(ocomb[:, g_ * 256:(g_ + 1) * 256].rearrange("p (h d) -> p h d", h=4), a3[:, :, 0:64],
                           bcast(rc, 1, 64), ALU.mult, [ak, rck], [ok])

                    def combine(tb, g_):
                        tbs = slice(tb * 128, (tb + 1) * 128)
                        ok = ("ocomb", g_)
                        ocv = ocomb[:, g_ * 256:(g_ + 1) * 256].rearrange("p (h d) -> p h d", h=4)
                        for br, gi, off in (("w", 16, 4), ("s", 8, 8)):
                            a3, ak = acc_of(tb, g_, br)
                            r_ = sml[:, 16 * g_ + off:16 * g_ + off + 4]
                            rk = ("sml_r" + br, g_)
                            recip(r_, a3[:, :, 64], [ak], [rk])
                            tt(r_, r_, sigg[:, tb, gi + 4 * g_:gi + 4 + 4 * g_], ALU.mult, [rk, "sigg"], [rk])
                            tt(tmpO[:], a3[:, :, 0:64], bcast(r_, 1, 64), ALU.mult, [ak, rk], ["tmpO"])
                            tt(ocv, ocv, tmpO[:], ALU.add, [ok, "tmpO"], [ok])
                        if g_ == 1:
                            if tb == 5:
                                dump(f"ocomb{l}", ocomb[:], [("ocomb", 0), ("ocomb", 1)])
                            tt(b16[:], ocomb[:], szs[0][:], ALU.mult, [("ocomb", 0), ("ocomb", 1), ("sz", 0)],
                               ["b16"])
                            for c in range(4):
                                mm(PS[1][:, c * 128:(c + 1) * 128], b16[:, c * 128:(c + 1) * 128], ident_bf, True, True,
                                   ["b16", "cmat"], [PSK[1]], inc=(c == 3))
                            cp(bT[:, :, tbs], PS[1][:].rearrange("p (a b) -> p a b", a=4), [PSK[1]], ["bT"], eng="act")

                    for j in range(ntl + 2):
                        if 0 <= j - 2 < ntl:
                            P2(j - 2)
                        if j < ntl:
                            P1(j)
                        if 0 <= j - 1 < ntl:
                            A1(j - 1)
                    dump(f"bT{l}", bT[:, :, 0:256], ["bT"])
                    T.barrier()
            if stop_after == ("s3", l):
                break

            with ExitStack() as s4:
                yT = sb(s4, "yT", [128, 8, S], BF16)
                ra = [sb(s4, f"ra{i}", [128, 512], F32) for i in range(2)]
                rb = [sb(s4, f"rb{i}", [128, 512], F32) for i in range(2)]
                nxt = load_group(l, ["wpa0", "wpb0", "ma0", "mb0"])
                it = 0
                for n in range(8):
                    g = nxt
                    if n < 7:
                        nxt = load_group(l, [f"wpa{n + 1}", f"wpb{n + 1}", f"ma{n + 1}", f"mb{n + 1}"])
                    else:
                        nxt = load_group(l, ["wo0", "wo1", "wo2", "wo3"])
                    for tq in range(4):
                        tsl = slice(tq * 512, (tq + 1) * 512)
                        r = it % 2
                        it += 1
                        p_a, p_ak = PS[0 + r], PSK[0 + r]
                        p_b, p_bk = PS[2 + r], PSK[2 + r]
                        p_ma, p_mak = PS[4 + r], PSK[4 + r]
                        p_mb, p_mbk = PS[6 + r], PSK[6 + r]
                        ia, ib = g[f"wpa{n}"], g[f"wpb{n}"]
                        for fc in range(4):
                            mm(p_a[:], wview(ia, fc), aT[:, fc, tsl], fc == 0, fc == 3, [("w", ia[0]), "aT"], [p_ak])
                        for fc in range(4):
                            mm(p_b[:], wview(ib, fc), bT[:, fc, tsl], fc == 0, fc == 3, [("w", ib[0]), "bT"], [p_bk])
                        proj_F(p_ma[:], p_mak, g[f"ma{n}"], tq)
                        proj_F(p_mb[:], p_mbk, g[f"mb{n}"], tq)
                        act(ra[r][:], p_ma[:], AF.Exp, [p_mak], [("ra", r)], scale=-1.0)
                        act(rb[r][:], p_mb[:], AF.Exp, [p_mbk], [("rb", r)], scale=-1.0)
                        act(ra[r][:], ra[r][:], AF.Ln, [("ra", r)], [("ra", r)], bias=1.0)
                        act(rb[r][:], rb[r][:], AF.Ln, [("rb", r)], [("rb", r)], bias=1.0)
                        act(ra[r][:], ra[r][:], AF.Exp, [("ra", r)], [("ra", r)], scale=-1.0)
                        act(rb[r][:], rb[r][:], AF.Exp, [("rb", r)], [("rb", r)], scale=-1.0)
                        tt(ra[r][:], p_a[:], ra[r][:], ALU.mult, [p_ak, ("ra", r)], [("ra", r)])
                        tt(rb[r][:], p_b[:], rb[r][:], ALU.mult, [p_bk, ("rb", r)], [("rb", r)])
                        tt(yT[:, n, tsl], ra[r][:], rb[r][:], ALU.add, [("ra", r), ("rb", r)], ["yT"])
                dump(f"yT{l}", yT[:, :, 0:256], ["yT"])
                if l + 1 < n_layers:
                    emit_mod(l + 1, s4)
                for half in range(2):
                    g = nxt
                    if half == 0:
                        nxt = load_group(l, ["wo4", "wo5", "wo6", "wo7"])
                    for nn in range(4):
                        n = half * 4 + nn
                        info = g[f"wo{n}"]
                        for tq in range(4):
                            tsl = slice(tq * 512, (tq + 1) * 512)
                            r = it % 2
                            it += 1
                            po_, pok = PS[r], PSK[r]
                            for kc in range(8):
                                mm(po_[:], wview(info, kc), yT[:, kc, tsl], kc == 0, kc == 7, [("w", info[0]), "yT"],
                                   [pok])
                            stt(xT[:, n, tsl], po_[:], gatev[:, n:n + 1], xT[:, n, tsl], ALU.mult, ALU.add,
                                [pok, ("gatev", par), "xT"], ["xT"])
                T.barrier()

        if stop_after is None or True:
            with ExitStack() as pe_:
                xo = [sb(pe_, f"xo{i}", [128, D], F32) for i in range(2)]
                for t in range(NT):
                    xi = xo[t % 2]
                    xk = ("xo", t % 2)
                    for half in range(2):
                        pb = PS[(2 * t + half) % 4]
                        pk = PSK[(2 * t + half) % 4]
                        for j in range(4):
                            c = half * 4 + j
                            T.op("pe", lambda: nc.tensor.transpose(pb[:, j * 128:(j + 1) * 128],
                                                                   xT[:, c, t * 128:(t + 1) * 128], cmat_f[:]),
                                 ["xT", "cmat_f"], [pk], inc=(j == 3))
                        cp(xi[:, half * 512:(half + 1) * 512], pb[:], [pk], [xk], eng="dve" if half == 0 else "act")
                    T.dma("sp", out_d[t * 128:(t + 1) * 128, :], xi[:], reads=[xk])
                T.finish("sp")
        build_program.stats = (T.n_inst, T.n_waits, len(T.sems))
        build_program.stuck = T.check_deadlock()
    return nc, dbg_out


_CACHE = {}


def kernel(**inputs):
    maps = prep_inputs(inputs)
    if "nc" not in _CACHE:
        _CACHE["nc"] = build_program()[0]
    nc = _CACHE["nc"]
    res = run_bass_kernel_spmd(nc, maps, core_ids=list(range(8)))
    out = np.stack([np.asarray(r["out"], dtype=np.float32) for r in res.results], axis=0)
    return out
```

```python
from contextlib import ExitStack
import numpy as np
import concourse.bass as bass
import concourse.mybir as mybir
from concourse.bass_utils import run_bass_kernel_spmd

F32 = mybir.dt.float32
BF16 = mybir.dt.bfloat16
AF = mybir.ActivationFunctionType
ALU = mybir.AluOpType

D = 1024
S = 2048
L_DEPTH = 4
NT = 16
NEGB = -32768.0
EPS = 1e-6
N_CMP = 127

import os as _os
SAME_ENGINE_SYNC = _os.environ.get("NO_SES", "") != "1"
SES_RAW_ONLY = _os.environ.get("NO_SES", "") == "2"


class Tracker:
    def __init__(self, nc, stack):
        self.nc = nc
        self.stack = stack
        self.engs = {"pe": nc.tensor, "act": nc.scalar, "dve": nc.vector,
                     "pool": nc.gpsimd, "sp": nc.sync}
        self.sems = {}
        self.count = {}
        for e in self.engs:
            self.sems[e] = stack.enter_context(nc.semaphore("s_" + e))
            self.count[e] = 0
        self.last_write = {}
        self.reads = {}
        self.seen = {e: {} for e in self.engs}
        self.n_waits = 0
        self.n_inst = 0
        self.log = {e: [] for e in self.engs}
        self._pending_waits = {e: [] for e in self.engs}

    def check_deadlock(self):
        cnt = {k: 0 for k in self.sems}
        pos = {e: 0 for e in self.engs}
        progress = True
        while progress:
            progress = False
            for e in self.engs:
                q = self.log[e]
                while pos[e] < len(q):
                    waits, inc = q[pos[e]]
                    if all(cnt[s_] >= v for s_, v in waits):
                        if inc is not None:
                            cnt[inc[0]] += inc[1]
                        pos[e] += 1
                        progress = True
                    else:
                        break
        stuck = {e: (pos[e], len(self.log[e]), self.log[e][pos[e]][0], {s_: cnt[s_] for s_, _ in self.log[e][pos[e]][0]})
                 for e in self.engs if pos[e] < len(self.log[e])}
        return stuck

    def _dma_src(self, key):
        sid = ("dma", key)
        if sid not in self.sems:
            nm = "d_" + "_".join(str(k) for k in (key if isinstance(key, tuple) else (key,)))
            self.sems[sid] = self.stack.enter_context(self.nc.semaphore(nm[:40]))
            self.count[sid] = 0
        return sid

    def _deps(self, e, reads, writes):
        need = {}

        def add(src, val):
            if src == e and (not SAME_ENGINE_SYNC or e in ("pe", "sp") or val > self.count[e]):
                return
            if need.get(src, 0) < val:
                need[src] = val

        for k in reads:
            if k in self.last_write:
                add(*self.last_write[k])
        for k in writes:
            if k in self.last_write:
                if not (SES_RAW_ONLY and self.last_write[k][0] == e):
                    add(*self.last_write[k])
            for src, val in self.reads.get(k, {}).items():
                if not (SES_RAW_ONLY and src == e):
                    add(src, val)
        out = []
        for src, val in need.items():
            if self.seen[e].get(src, 0) >= val:
                continue
            self.seen[e][src] = val
            out.append((src, val))
        return out

    def _emit_waits(self, e, deps):
        eng = self.engs[e]
        for src, val in deps:
            eng.wait_ge(self.sems[src], val)
            self.n_waits += 1
            self._pending_waits[e].append((src, val))

    def op(self, e, fn, reads=(), writes=(), inc=True):
        deps = self._deps(e, reads, writes)
        self._emit_waits(e, deps)
        ins = fn()
        self.n_inst += 1
        val = self.count[e] + 1
        self.log[e].append((self._pending_waits[e], (e, 1) if inc else None))
        self._pending_waits[e] = []
        if inc:
            ins.then_inc(self.sems[e], 1)
            self.count[e] = val
        for k in reads:
            d = self.reads.setdefault(k, {})
            d[e] = max(d.get(e, 0), val)
        for k in writes:
            self.last_write[k] = (e, val)
            self.reads[k] = {}
        return ins

    def dma(self, e, out, in_, reads=(), writes=(), **kw):
        deps = self._deps(e, reads, writes)
        self._emit_waits(e, deps)
        key = writes[0] if writes else ("rd",) + tuple(reads[:1])
        sid = self._dma_src(key)
        ins = self.engs[e].dma_start(out=out, in_=in_, **kw)
        self.log[e].append((self._pending_waits[e], (sid, 16)))
        self._pending_waits[e] = []
        self.count[sid] += 16
        val = self.count[sid]
        ins.then_inc(self.sems[sid], 16)
        self.n_inst += 1
        for k in reads:
            self.reads.setdefault(k, {})[sid] = val
        for k in writes:
            self.last_write[k] = (sid, val)
            self.reads[k] = {}
        return ins

    def barrier(self):
        for e in self.engs:
            for src, val in self.count.items():
                if src == e or val == 0:
                    continue
                if self.seen[e].get(src, 0) >= val:
                    continue
                self.seen[e][src] = val
                self.engs[e].wait_ge(self.sems[src], val)
                self.n_waits += 1
                self._pending_waits[e].append((src, val))
        self.last_write = {}
        self.reads = {}

    def finish(self, e="sp"):
        for src, val in self.count.items():
            if src == e or val == 0:
                continue
            if self.seen[e].get(src, 0) >= val:
                continue
            self.seen[e][src] = val
            self.engs[e].wait_ge(self.sems[src], val)


def bccol(ap, n):
    dims = [list(d) for d in ap.ap]
    dims[-1] = [0, n]
    return bass.AP(tensor=ap.tensor, offset=ap.offset, ap=dims)


def bcast(ap, pos, n):
    dims = [list(d) for d in ap.ap]
    dims.insert(1 + pos, [0, n])
    return bass.AP(tensor=ap.tensor, offset=ap.offset, ap=dims)


SWAP64 = np.concatenate([np.arange(32, 64), np.arange(0, 32)])


def _w_blocks():
    blks = []
    a = np.arange

    def add(name, cols):
        blks.append((name, "w_in", np.asarray(cols), 8))

    for j in range(4):
        add(f"sbv{j}", 1024 + j * 128 + a(128))
    for c in range(4):
        add(f"sbq{c}", 0 + c * 128 + a(128))
        add(f"sbk{c}", 512 + c * 128 + a(128))
        add(f"sbz{c}", 1536 + c * 128 + a(128))
    add("kc", 2560 + a(128))
    add("vc", 2688 + a(128))
    for c in range(4):
        hs = (c, c + 4)
        add(f"nq{c}", np.concatenate([2048 + h * 64 + a(64) for h in hs]))
        add(f"nqs{c}", np.concatenate([2048 + h * 64 + SWAP64 for h in hs]))
    add("ks", 2816 + a(128))
    add("kss", np.concatenate([2816 + g * 64 + SWAP64 for g in range(2)]))
    add("kw", 3072 + a(128))
    add("kws", np.concatenate([3072 + g * 64 + SWAP64 for g in range(2)]))
    add("vs", 2944 + a(128))
    add("vw", 3200 + a(128))
    add("ng", 3840 + a(24))
    for j in range(4):
        add(f"nz{j}", 3328 + j * 128 + a(128))
    for n in range(8):
        blks.append((f"wpa{n}", "w_proj_a", n * 128 + a(128), 4))
        blks.append((f"wpb{n}", "w_proj_b", n * 128 + a(128), 4))
        add(f"ma{n}", 3864 + n * 128 + a(128))
        add(f"mb{n}", 4888 + n * 128 + a(128))
    for n in range(8):
        blks.append((f"wo{n}", "w_out", n * 128 + a(128), 8))
    return blks


W_BLOCKS = _w_blocks()
W_OFF = {}
_off = 0
for _name, _src, _cols, _nk in W_BLOCKS:
    W_OFF[_name] = (_off, _nk, len(_cols))
    _off += _nk * len(_cols)
W_TOT = _off


def _consts():
    c = {}
    half = 32
    freq = (np.float32(10000.0) ** (-np.arange(half, dtype=np.float32) / np.float32(half))).astype(np.float32)
    pos = np.arange(S, dtype=np.float32)
    ang = (pos[:, None] * freq[None, :]).astype(np.float32)
    cos = np.cos(ang).astype(np.float32).T
    sin = np.sin(ang).astype(np.float32).T
    p = np.arange(128)
    sign = np.where((p % 64) < 32, -1.0, 1.0).astype(np.float32)[:, None]
    cos2 = cos[p % 32]
    sin2 = sin[p % 32] * sign
    c["cos"] = np.ascontiguousarray(cos2)
    c["sin"] = np.ascontiguousarray(sin2)
    posc = 16 * np.arange(N_CMP) + 31
    cc = np.zeros((128, 128), np.float32)
    sc = np.zeros((128, 128), np.float32)
    cc[:, :N_CMP] = cos2[:, posc]
    sc[:, :N_CMP] = sin2[:, posc]
    c["cosc"] = cc
    c["sinc"] = sc
    ident = np.eye(128, dtype=np.float32)
    ones = np.ones((128, 128), np.float32)
    jj, ss = np.meshgrid(np.arange(128), np.arange(128), indexing="ij")
    tri = (jj >= ss).astype(np.float32)
    bd = ((jj // 64) == (ss // 64)).astype(np.float32)
    w2m = np.where(jj > ss, 0.0, NEGB).astype(np.float32)
    c["cmat"] = np.concatenate([ident, ones, tri, bd, w2m, -8.0 * tri, -8.0 * ones], axis=1)
    u = np.arange(896)[None, :]
    s_ = np.arange(128)[:, None]
    c["big"] = np.where(s_ < u - 384, 0.0, NEGB).astype(np.float32)
    n_ = np.arange(128)[:, None]
    t_ = np.arange(S)[None, :]
    c["vbias"] = np.where((16 * n_ + 31 <= t_) & (n_ < N_CMP), 0.0, NEGB).astype(np.float32)
    ci = np.arange(128)[:, None]
    sj = np.arange(32)[None, :]
    ov = ((16 * ci < 64 * (sj + 1)) & (16 * ci + 32 > 64 * sj) & (ci < N_CMP)).astype(np.float32)
    c["ov1"] = np.concatenate([np.ones((128, 1), np.float32), ov], axis=1)
    t = np.arange(S)
    cur = (t // 64)[:, None]
    jb = np.arange(32)[None, :]
    m1 = np.ones((S, 32), np.float32)
    ad = np.zeros((S, 32), np.float32)
    fut = jb > cur
    m1[fut] = 0.0
    ad[fut] = -1e30
    frc = ((jb == 0) | (jb == cur - 1)) & ~fut
    m1[frc] = 0.0
    ad[frc] = 1e4
    cu = jb == cur
    m1[cu] = 0.0
    ad[cu] = 2e4
    c["impm"] = np.ascontiguousarray(m1.reshape(NT, 128, 32).transpose(1, 0, 2)).reshape(128, NT * 32)
    c["impa"] = np.ascontiguousarray(ad.reshape(NT, 128, 32).transpose(1, 0, 2)).reshape(128, NT * 32)
    return c


def prep_inputs(inp):
    f = lambda a: np.ascontiguousarray(np.asarray(a, dtype=np.float32))
    L = L_DEPTH
    shared = {}
    wst = np.empty((L, 128, W_TOT), np.float32)
    srcs = {k: f(inp[k]) for k in ("w_in", "w_proj_a", "w_proj_b", "w_out")}
    for name, src, cols, nk in W_BLOCKS:
        off, _, nc_ = W_OFF[name]
        w = srcs[src][:, :, cols]
        w = w.reshape(L, nk, 128, nc_).transpose(0, 2, 1, 3).reshape(L, 128, nk * nc_)
        wst[:, :, off:off + nk * nc_] = w
    shared["wst"] = wst
    ada_w = f(inp["ada_w"])
    shared["ada_w"] = np.ascontiguousarray(
        ada_w.reshape(L, 8, 128, 6, 512).transpose(0, 3, 2, 1, 4).reshape(L, 6, 128, 8 * 512))
    shared["ada_b"] = np.ascontiguousarray(f(inp["ada_b"]).reshape(L, 24, 128).transpose(0, 2, 1))
    shared["norm_g"] = np.ascontiguousarray(f(inp["norm_g"]).reshape(L, 8, 128).transpose(0, 2, 1))
    p = np.arange(128)
    gv = np.zeros((L, 128, 8), np.float32)
    for i, k in enumerate(("q_norm_g", "ks_norm_g", "kw_norm_g", "kc_norm_g")):
        g = f(inp[k])
        gv[:, :, 2 * i] = g[:, p % 64]
        gv[:, :, 2 * i + 1] = g[:, SWAP64[p % 64]]
    shared["gvec"] = gv
    for nm in ("k", "v"):
        w1 = f(inp[f"cmp_w1_{nm}"])
        w1 = w1.reshape(L, 32, 64, 128).transpose(0, 2, 1, 3).reshape(L, 64, 32 * 128)
        shared[f"w1{nm}"] = np.ascontiguousarray(np.concatenate([w1, w1], axis=1))
        pe = f(inp[f"cmp_pe_{nm}"]).transpose(0, 2, 1)
        shared[f"pe{nm}"] = np.ascontiguousarray(np.concatenate([pe, pe], axis=1))
    w2k = f(inp["cmp_w2_k"])
    shared["w2k"] = np.ascontiguousarray(np.concatenate([w2k, w2k[:, :, SWAP64]], axis=2))
    shared["w2v"] = f(inp["cmp_w2_v"])
    shared.update(_consts())
    x = f(inp["x"])
    c = f(inp["c"])
    maps = []
    for b in range(8):
        m = dict(shared)
        m["x"] = x[b]
        m["c"] = np.ascontiguousarray(c[b].reshape(8, 128).T)
        maps.append(m)
    return maps


IN_SHAPES = {
    "x": [S, D], "c": [128, 8], "wst": [L_DEPTH, 128, W_TOT], "ada_w": [L_DEPTH, 6, 128, 4096],
    "ada_b": [L_DEPTH, 128, 24], "norm_g": [L_DEPTH, 128, 8], "gvec": [L_DEPTH, 128, 8],
    "w1k": [L_DEPTH, 128, 4096], "w1v": [L_DEPTH, 128, 4096], "pek": [L_DEPTH, 128, 32],
    "pev": [L_DEPTH, 128, 32], "w2k": [L_DEPTH, 128, 128], "w2v": [L_DEPTH, 128, 64],
    "cos": [128, S], "sin": [128, S], "cosc": [128, 128], "sinc": [128, 128],
    "cmat": [128, 896], "big": [128, 896], "vbias": [128, S], "ov1": [128, 33],
    "impm": [128, NT * 32], "impa": [128, NT * 32],
}


def build_program(n_layers=L_DEPTH, debug=(), stop_after=None):
    nc = bass.Bass("TRN2", target_bir_lowering=False)
    I = {k: nc.dram_tensor(k, shp, F32, kind="ExternalInput").ap() for k, shp in IN_SHAPES.items()}
    out_d = nc.dram_tensor("out", [S, D], F32, kind="ExternalOutput").ap()
    dbg_out = {}

    with ExitStack() as st:
        T = Tracker(nc, st)

        uid = [0]

        def sb(stack, name, shape, dt):
            uid[0] += 1
            return stack.enter_context(nc.sbuf_tensor(f"sb{uid[0]}_{name}", shape, dt))

        PS = [st.enter_context(nc.psum_tensor(f"ps{i}", [128, 512], F32)) for i in range(8)]
        PSK = [("ps", i) for i in range(8)]

        def mm(out, lhsT, rhs, start, stop, reads, writes, inc=None):
            if inc is None:
                inc = stop
            return T.op("pe", lambda: nc.tensor.matmul(out, lhsT=lhsT, rhs=rhs, start=start, stop=stop,
                                                       skip_group_check=True),
                        reads, writes, inc=inc)

        def act(out, in_, func, reads, writes, **kw):
            return T.op("act", lambda: nc.scalar.activation(out, in_, func, **kw), reads, writes)

        def E(eng):
            return nc.vector if eng == "dve" else nc.gpsimd

        def tt(out, in0, in1, op, reads, writes, eng="dve"):
            return T.op(eng, lambda: E(eng).tensor_tensor(out, in0, in1, op), reads, writes)

        def ts(out, in0, s1, s2, op0, op1, reads, writes, eng="dve"):
            if s2 is None:
                return T.op(eng, lambda: E(eng).tensor_scalar(out, in0, s1, None, op0), reads, writes)
            return T.op(eng, lambda: E(eng).tensor_scalar(out, in0, s1, s2, op0, op1), reads, writes)

        def stt(out, in0, scalar, in1, op0, op1, reads, writes):
            return T.op("dve", lambda: nc.vector.scalar_tensor_tensor(out, in0, scalar, in1, op0, op1), reads, writes)

        def cp(out, in_, reads, writes, eng="dve"):
            if eng == "act":
                return T.op("act", lambda: nc.scalar.copy(out, in_), reads, writes)
            return T.op(eng, lambda: E(eng).tensor_copy(out, in_), reads, writes)

        def recip(out, in_, reads, writes):
            return T.op("dve", lambda: nc.vector.reciprocal(out, in_), reads, writes)

        def dump(name, ap, reads):
            if name not in debug:
                return
            shp = list(ap.shape)
            d = nc.dram_tensor("dbg_" + name, shp, ap.dtype, kind="ExternalOutput").ap()
            dbg_out[name] = d
            T.dma("sp", d, ap, reads=reads)

        xT = sb(st, "xT", [128, 8, S], F32)
        hT = sb(st, "hT", [128, 8, S], BF16)
        aT = sb(st, "aT", [128, 4, S], BF16)
        bT = sb(st, "bT", [128, 4, S], BF16)
        NSLOT = 8
        wpool = sb(st, "wpool", [128, NSLOT, 1024], BF16)
        cmat_f = sb(st, "cmat_f", [128, 128], F32)
        cmat = sb(st, "cmat", [128, 896], BF16)
        big = sb(st, "big", [128, 896], BF16)
        ov1 = sb(st, "ov1", [128, 33], BF16)
        cosc = sb(st, "cosc", [128, 128], F32)
        sinc = sb(st, "sinc", [128, 128], F32)
        c_sb = sb(st, "c_sb", [128, 8], F32)
        siluc = sb(st, "siluc", [128, 8, 2], F32)
        small = sb(st, "small", [128, 128], F32)
        siluc_bf = sb(st, "siluc_bf", [128, 8, 2], BF16)
        ident_bf = cmat[:, 0:128]
        ones_bf = cmat[:, 128:256]
        tri_bf = cmat[:, 256:384]
        bd_bf = cmat[:, 384:512]
        w2m_bf = cmat[:, 512:640]
        tri8_bf = cmat[:, 640:768]
        ones8_bf = cmat[:, 768:896]
        def smv(par):
            o = 64 * par
            return (small[:, o:o + 8], small[:, o + 8:o + 16], small[:, o + 16:o + 24], small[:, o + 24:o + 32],
                    small[:, o + 32:o + 40], small[:, o + 40:o + 64])

        def emit_mod(lm, stack):
            par = lm % 2
            gsv, shiftv, gatev, normg, gvec, modT = smv(par)
            adab = [sb(stack, f"adab{i}", [128, 8, 512], BF16) for i in range(2)]
            adab_b = sb(stack, "adab_b", [128, 24], F32)
            T.dma("sp", adab_b[:], I["ada_b"][lm], writes=["adab_b"])
            T.dma("sp", normg, I["norm_g"][lm], writes=[("small_ng", par)])
            T.dma("sp", gvec, I["gvec"][lm], writes=[("small_gv", par)])
            for nb in range(6):
                ab = adab[nb % 2]
                ak = ("adab", nb % 2)
                T.dma("pool", ab[:], I["ada_w"][lm, nb].rearrange("p (a b) -> p a b", a=8), writes=[ak], **CAST)
                for jj in range(4):
                    j = nb * 4 + jj
                    for kc in range(8):
                        mm(PS[7][:, 2 * j:2 * j + 2], ab[:, kc, jj * 128:(jj + 1) * 128], siluc_bf[:, kc, :],
                           kc == 0, kc == 7, [ak, "siluc_bf"], [PSK[7]])
            tt(modT, PS[7][:, 0:48:2], adab_b[:], ALU.add, [PSK[7], "adab_b"], [("modT", par)])
            stt(gsv, modT[:, 8:16], 1.0, normg, ALU.add, ALU.mult, [("modT", par), ("small_ng", par)], [("gsv", par)])
            cp(shiftv, modT[:, 0:8], [("modT", par)], [("shiftv", par)])
            cp(gatev, modT[:, 16:24], [("modT", par)], [("gatev", par)])
            dump(f"mod{lm}", modT, [("modT", par)])

        CAST = dict(max_dma_last_dim=4096)
        T.dma("sp", cmat_f[:], I["cmat"][:, 0:128], writes=["cmat_f"])
        T.dma("pool", cmat[:], I["cmat"], writes=["cmat"], **CAST)
        T.dma("pool", big[:], I["big"], writes=["big"], **CAST)
        T.dma("pool", ov1[:], I["ov1"], writes=["ov1"], **CAST)
        T.dma("sp", cosc[:], I["cosc"], writes=["cosc"])
        T.dma("sp", sinc[:], I["sinc"], writes=["sinc"])
        T.dma("sp", c_sb[:], I["c"], writes=["c_sb"])

        wstate = {"half": 0}

        def load_group(l, names):
            half = wstate["half"]
            wstate["half"] ^= 1
            res = {}
            for i, nm in enumerate(names):
                off, nk, ncol = W_OFF[nm]
                slot = half * 4 + i
                T.dma("pool", wpool[:, slot, 0:nk * ncol], I["wst"][l, :, off:off + nk * ncol],
                      writes=[("w", slot)], **CAST)
                res[nm] = (slot, nk, ncol)
            return res

        def wview(info, kc):
            slot, nk, ncol = info
            return wpool[:, slot, kc * ncol:(kc + 1) * ncol]

        def proj_F(psum_ap, pkey, info, tq, extra_reads=()):
            slot, nk, ncol = info
            for kc in range(8):
                mm(psum_ap, wview(info, kc), hT[:, kc, tq * 512:(tq + 1) * 512], kc == 0, kc == 7,
                   [("w", slot), "hT"] + list(extra_reads), [pkey])

        with ExitStack() as ps_:
            xin = [sb(ps_, f"xin{i}", [128, D], F32) for i in range(2)]
            for t in range(NT):
                xi = xin[t % 2]
                xk = ("xin", t % 2)
                T.dma("sp", xi[:], I["x"][t * 128:(t + 1) * 128, :], writes=[xk])
                for half in range(2):
                    pb = PS[(2 * t + half) % 4]
                    pk = PSK[(2 * t + half) % 4]
                    for j in range(4):
                        c = half * 4 + j
                        T.op("pe", lambda: nc.tensor.transpose(pb[:, j * 128:(j + 1) * 128],
                                                               xi[:, c * 128:(c + 1) * 128], cmat_f[:]),
                             [xk, "cmat_f"], [pk], inc=(j == 3))
                    cp(xT[:, half * 4:half * 4 + 4, t * 128:(t + 1) * 128],
                       pb[:].rearrange("p (a b) -> p a b", a=4), [pk], ["xT"],
                       eng="dve" if half == 0 else "act")
            act(siluc[:, :, 0], c_sb[:], AF.Exp, ["c_sb"], ["siluc"], scale=-1.0)
            ts(siluc[:, :, 0], siluc[:, :, 0], 1.0, None, ALU.add, None, ["siluc"], ["siluc"])
            recip(siluc[:, :, 0], siluc[:, :, 0], ["siluc"], ["siluc"])
            tt(siluc[:, :, 0], siluc[:, :, 0], c_sb[:], ALU.mult, ["siluc", "c_sb"], ["siluc"])
            cp(siluc[:, :, 1], siluc[:, :, 0], ["siluc"], ["siluc"])
            cp(siluc_bf[:], siluc[:], ["siluc"], ["siluc_bf"])
            if stop_after != ("pro", 0):
                emit_mod(0, ps_)
            T.barrier()

        for l in range(n_layers if stop_after != ("pro", 0) else 0):
            with ExitStack() as s1:
                par = l % 2
                gsv, shiftv, gatev, normg, gvec, modT = smv(par)
                sqt = [sb(s1, f"sqt{i}", [128, 512], BF16) for i in range(2)]
                lnv = sb(s1, "lnv", [128, 512], F32)
                rstd = sb(s1, "rstd", [128, 512], F32)
                tmpf = [sb(s1, f"tmpf{i}", [128, 512], F32) for i in range(2)]
                for tq in range(4 if stop_after != ("s1a", l) else 0):
                    tsl = slice(tq * 512, (tq + 1) * 512)
                    for c in range(8):
                        sq = sqt[c % 2]
                        act(sq[:], xT[:, c, tsl], AF.Square, ["xT"], [("sqt", c % 2)])
                        mm(PS[1][:], ones_bf, sq[:], c == 0, c == 7, [("sqt", c % 2), "cmat"], [PSK[1]], inc=True)
                    act(lnv[:], PS[1][:], AF.Ln, [PSK[1]], ["lnv"], scale=1.0 / D, bias=EPS)
                    act(rstd[:], lnv[:], AF.Exp, ["lnv"], ["rstd"], scale=-0.5)
                    for c in range(8):
                        tf = tmpf[c % 2]
                        stt(tf[:], xT[:, c, tsl], gsv[:, c:c + 1], rstd[:], ALU.mult, ALU.mult,
                            ["xT", ("gsv", par), "rstd"], [("tmpf", c % 2)])
                        act(hT[:, c, tsl], tf[:], AF.Identity, [("tmpf", c % 2), ("shiftv", par)], ["hT"],
                            bias=shiftv[:, c:c + 1], scale=1.0)
                dump(f"hT{l}", hT[:, :, 0:256], ["hT"])
                T.barrier()
            if stop_after in (("s1", l), ("s1a", l)):
                break

            with ExitStack() as s2:
                sbvm = [[sb(s2, f"sbvm{pp}{i}", [128, NT, 128], BF16) for i in range(2)] for pp in range(2)]
                qmb = [[sb(s2, f"qm{pp}{i}", [128, S], BF16) for i in range(2)] for pp in range(2)]
                kcb = [sb(s2, f"kc_{pp}", [128, S], BF16) for pp in range(2)]
                NB = 3
                e_t = [sb(s2, f"e_t{i}", [128, 512], F32) for i in range(2)]
                sp_t = [sb(s2, f"sp_t{i}", [128, 512], BF16) for i in range(NB)]
                w_t = [sb(s2, f"w_t{i}", [128, 512], BF16) for i in range(NB)]
                lacc = [sb(s2, f"lacc{i}", [128, 512], BF16) for i in range(2)]
                zr = [sb(s2, f"zr{i}", [128, 512], F32) for i in range(2)]
                for pp in range(2):
                    T.op("pool", lambda: nc.gpsimd.memset(qmb[pp][0][64:128, :], 0.0), [], [f"qm{pp}0"])
                    T.op("pool", lambda: nc.gpsimd.memset(qmb[pp][1][0:64, :], 0.0), [], [f"qm{pp}1"])
                    T.op("pool", lambda: nc.gpsimd.memset(sbvm[pp][0][:, :, 64:128], 0.0), [], [f"sbvm{pp}0"])
                    T.op("pool", lambda: nc.gpsimd.memset(sbvm[pp][1][:, :, 0:64], 0.0), [], [f"sbvm{pp}1"])
                nxt = load_group(l, ["sbq0", "sbk0", "sbz0", "sbv0"])
                qi = 0

                def v_unit(cn, gn, t):
                    pp = cn % 2
                    info = gn[f"sbv{cn}"]
                    pb, pk = PS[7], PSK[7]
                    for kc in range(8):
                        mm(pb[:, 0:128], hT[:, kc, t * 128:(t + 1) * 128], wview(info, kc), kc == 0, kc == 7,
                           ["hT", ("w", info[0])], [pk])
                    cp(sbvm[pp][0][:, t, 0:64], pb[:, 0:64], [pk], [f"sbvm{pp}0"], eng="dve")
                    cp(sbvm[pp][1][:, t, 64:128], pb[:, 64:128], [pk], [f"sbvm{pp}1"], eng="dve")

                def proj_unit(cn, gn, tq, which):
                    pp = cn % 2
                    tsl = slice(tq * 512, (tq + 1) * 512)
                    pb, pk = PS[7], PSK[7]
                    if which == "q":
                        proj_F(pb[:], pk, gn[f"sbq{cn}"], tq)
                        cp(qmb[pp][0][0:64, tsl], pb[0:64, :], [pk], [f"qm{pp}0"], eng="dve")
                        cp(qmb[pp][1][64:128, tsl], pb[64:128, :], [pk], [f"qm{pp}1"], eng="dve")
                    elif which == "k":
                        proj_F(pb[:], pk, gn[f"sbk{cn}"], tq)
                        cp(kcb[pp][:, tsl], pb[:], [pk], [f"kc_{pp}"], eng="dve")
                    else:
                        v_unit(cn, gn, tq)

                for tq in range(4):
                    proj_unit(0, nxt, tq, "q")
                    proj_unit(0, nxt, tq, "k")
                for t in range(NT):
                    v_unit(0, nxt, t)
                for c in range(4):
                    g = nxt
                    pending = []
                    if c < 3:
                        nxt = load_group(l, [f"sbq{c + 1}", f"sbk{c + 1}", f"sbz{c + 1}", f"sbv{c + 1}"])
                        pending = [(c + 1, nxt, tq, w_) for tq in range(4) for w_ in "qk"]
                        pending += [(c + 1, nxt, t, "v") for t in range(NT)]
                    qm = qmb[c % 2]
                    kc_ = kcb[c % 2]
                    qmk = [f"qm{c % 2}0", f"qm{c % 2}1"]
                    kck = f"kc_{c % 2}"
                    tiles = []
                    for Q in range(4):
                        nkb = 4 * Q + 4
                        for idx in range(nkb):
                            for hh in range(2):
                                kb = nkb - 1 - idx
                                tiles.append(dict(Q=Q, hh=hh, kb=kb, first=(idx == 0), last=(kb == 0),
                                                  c0=(128 * (kb - 4 * Q) if kb >= 4 * Q else 0),
                                                  diag=(kb >= 4 * Q)))
                    ntl = len(tiles)
                    accb = lambda Q: (PS[4 + (qi + Q) % 2], PSK[4 + (qi + Q) % 2])

                    def zgate(Q):
                        z = zr[(qi + Q) % 2]
                        zk = ("zr", (qi + Q) % 2)
                        proj_F(PS[6][:], PSK[6], g[f"sbz{c}"], Q)
                        act(z[:], PS[6][:], AF.Exp, [PSK[6]], [zk], scale=-1.0)
                        act(z[:], z[:], AF.Ln, [zk], [zk], bias=1.0)
                        act(z[:], z[:], AF.Exp, [zk], [zk], scale=-1.0)
                        tt(z[:], PS[6][:], z[:], ALU.mult, [PSK[6], zk], [zk])

                    def P1(j):
                        t_ = tiles[j]
                        Q, hh, kb, c0 = t_["Q"], t_["hh"], t_["kb"], t_["c0"]
                        zs, zsk = PS[j % 4], PSK[j % 4]
                        if t_["first"] and hh == 0:
                            zgate(Q)
                        cols = slice(Q * 512 + c0, (Q + 1) * 512)
                        mm(zs[:, c0:512], kc_[:, kb * 128:(kb + 1) * 128], qm[hh][:, cols], True, not t_["diag"],
                           [kck, qmk[hh]], [zsk], inc=True)
                        if t_["diag"]:
                            mm(zs[:, c0:512], ident_bf, big[:, 384:896 - c0], False, True, ["cmat", "big"], [zsk])

                    def A1a(j):
                        t_ = tiles[j]
                        c0 = t_["c0"]
                        zs, zsk = PS[j % 4], PSK[j % 4]
                        e, ek = e_t[j % 2], ("e_t", j % 2)
                        act(e[:, c0:512], zs[:, c0:512], AF.Exp, [zsk], [ek], scale=0.125)

                    def A1b(j):
                        t_ = tiles[j]
                        c0 = t_["c0"]
                        e, ek = e_t[j % 2], ("e_t", j % 2)
                        sp, spk = sp_t[j % NB], ("sp_t", j % NB)
                        act(sp[:, c0:512], e[:, c0:512], AF.Ln, [ek], [spk], bias=1.0)

                    def P2(j):
                        t_ = tiles[j]
                        c0, hh = t_["c0"], t_["hh"]
                        zs, zsk = PS[j % 4], PSK[j % 4]
                        sp, spk = sp_t[j % NB], ("sp_t", j % NB)
                        lk = ("lacc", hh)
                        if t_["first"]:
                            T.op("dve", lambda: nc.vector.memset(lacc[hh][:], 0.0), [], [lk])
                            mm(zs[:, c0:512], tri8_bf, sp[:, c0:512], False, True, ["cmat", spk], [zsk])
                        else:
                            mm(zs[:, c0:512], tri8_bf, sp[:, c0:512], False, False, ["cmat", spk], [zsk], inc=False)
                            mm(zs[:, c0:512], ones8_bf, lacc[hh][:, c0:512], False, True, ["cmat", lk], [zsk])
                        if not t_["last"]:
                            tt(lacc[hh][:, c0:512], lacc[hh][:, c0:512], sp[:, c0:512], ALU.add, [lk, spk], [lk])

                    def A2(j):
                        t_ = tiles[j]
                        c0 = t_["c0"]
                        zs, zsk = PS[j % 4], PSK[j % 4]
                        w, wk_ = w_t[j % NB], ("w_t", j % NB)
                        act(w[:, c0:512], zs[:, c0:512], AF.Exp, [zsk], [wk_], scale=0.125)

                    def P3(j):
                        t_ = tiles[j]
                        Q, hh, kb, c0 = t_["Q"], t_["hh"], t_["kb"], t_["c0"]
                        h = 2 * c + hh
                        w, wk_ = w_t[j % NB], ("w_t", j % NB)
                        ab, abk = accb(Q)
                        mm(ab[:, c0:512], sbvm[c % 2][hh][:, kb, :], w[:, c0:512],
                           t_["first"] and hh == 0, t_["last"] and hh == 1, [f"sbvm{c % 2}{hh}", wk_], [abk], inc=True)
                        if t_["last"] and hh == 1:
                            qs = slice(Q * 512, (Q + 1) * 512)
                            if c == 0 and Q == 0:
                                cp(e_t[0][:], ab[:], [abk], [("e_t", 0)])
                                dump(f"sba{l}", e_t[0][:], [("e_t", 0)])
                            tt(aT[:, c, qs], ab[:], zr[(qi + Q) % 2][:], ALU.mult, [abk, ("zr", (qi + Q) % 2)], ["aT"])

                    for j in range(ntl + 4):
                        if j < ntl:
                            P1(j)
                        if 0 <= j - 1 < ntl:
                            A1a(j - 1)
                        if 0 <= j - 3 < ntl:
                            A2(j - 3)
                        if 0 <= j - 1 < ntl:
                            A1b(j - 1)
                        if 0 <= j - 2 < ntl:
                            P2(j - 2)
                        if 0 <= j - 4 < ntl:
                            P3(j - 4)
                        if pending and j >= 4 and (j - 4) % 3 == 0:
                            proj_unit(*pending.pop(0))
                    while pending:
                        proj_unit(*pending.pop(0))
                    qi += 4
                dump(f"aT{l}", aT[:, :, 0:256], ["aT"])
                T.barrier()
            if stop_after == ("s2", l):
                break

            with ExitStack() as s3:
                vc1 = sb(s3, "vc1", [128, 2, 98], BF16)
                kcT = sb(s3, "kcT", [128, 128], BF16)
                sigg = sb(s3, "sigg", [128, NT, 32], F32)
                pb16 = [sb(s3, f"pb16_{i}", [128, 512], BF16) for i in range(3)]
                sml = sb(s3, "sml", [128, 256], F32)
                negsel = sb(s3, "negsel", [128, 2, 32], BF16)
                for g_ in range(2):
                    cp(vc1[:, g_, 64:97], ov1[:, :], ["ov1"], ["vc1"], eng="pool")
                with ExitStack() as s3a:
                    wk = [sb(s3a, f"wk{i}", [128, 512], F32) for i in range(4)]
                    kcr = sb(s3a, "kcr", [128, S], BF16)
                    vcr = sb(s3a, "vcr", [128, S], BF16)
                    w1k = sb(s3a, "w1k", [128, 32, 128], BF16)
                    w1v = sb(s3a, "w1v", [128, 32, 128], BF16)
                    pek = sb(s3a, "pek", [128, 32], BF16)
                    pev = sb(s3a, "pev", [128, 32], BF16)
                    w2k = sb(s3a, "w2k", [128, 128], BF16)
                    w2v = sb(s3a, "w2v", [128, 64], BF16)
                    hid = sb(s3a, "hid", [128, 128], BF16)
                    T.dma("pool", w1k[:], I["w1k"][l].rearrange("p (a b) -> p a b", a=32), writes=["w1k"], **CAST)
                    T.dma("pool", w1v[:], I["w1v"][l].rearrange("p (a b) -> p a b", a=32), writes=["w1v"], **CAST)
                    T.dma("pool", pek[:], I["pek"][l], writes=["pek"], **CAST)
                    T.dma("pool", pev[:], I["pev"][l], writes=["pev"], **CAST)
                    T.dma("pool", w2k[:], I["w2k"][l], writes=["w2k"], **CAST)
                    T.dma("pool", w2v[:], I["w2v"][l], writes=["w2v"], **CAST)
                    g = load_group(l, ["kc", "vc"])
                    for tq in range(4):
                        proj_F(PS[0][:], PSK[0], g["kc"], tq)
                        cp(kcr[:, tq * 512:(tq + 1) * 512], PS[0][:], [PSK[0]], ["kcr"], eng="dve")
                        proj_F(PS[1][:], PSK[1], g["vc"], tq)
                        cp(vcr[:, tq * 512:(tq + 1) * 512], PS[1][:], [PSK[1]], ["vcr"], eng="act")
                    for kv, raw, rawk, w1, w1key, pe_, pekey in (("k", kcr, "kcr", w1k, "w1k", pek, "pek"),
                                                                 ("v", vcr, "vcr", w1v, "w1v", pev, "pev")):
                        for g_ in range(2):
                            po = slice(64 * g_, 64 * g_ + 64)
                            pre = PS[2][:, 0:N_CMP]
                            for li in range(32):
                                mm(pre, w1[po, li, :], raw[po, li:li + 16 * (N_CMP - 1) + 1:16], li == 0, False,
                                   [w1key, rawk], [PSK[2]], inc=False)
                                mm(pre, w1[po, li, :], bccol(pe_[po, li:li + 1], N_CMP),
                                   False, li == 31, [w1key, pekey], [PSK[2]], inc=(li == 31))
                            e0 = wk[0][:, 0:N_CMP]
                            act(e0, pre, AF.Exp, [PSK[2]], [("wk", 0)], scale=-1.0)
                            ts(e0, e0, 1.0, None, ALU.add, None, [("wk", 0)], [("wk", 0)])
                            recip(e0, e0, [("wk", 0)], [("wk", 0)])
                            tt(hid[:, 0:N_CMP], pre, e0, ALU.mult, [PSK[2], ("wk", 0)], ["hid"])
                            if kv == "k":
                                pk_, pk2 = PS[3][po, 0:N_CMP], PS[3][po, 128:128 + N_CMP]
                                mm(pk_, w2k[:, 0:64], hid[:, 0:N_CMP], True, True, ["w2k", "hid"], [PSK[3]])
                                mm(pk2, w2k[:, 64:128], hid[:, 0:N_CMP], True, True, ["w2k", "hid"], [PSK[3]])
                                sq = pb16[0][po, 0:N_CMP]
                                act(sq, pk_, AF.Square, [PSK[3]], [("pb16", 0)])
                                ssq = PS[5][po, 0:N_CMP]
                                mm(ssq, ones_bf[po, 0:64], sq, True, True, ["cmat", ("pb16", 0)], [PSK[5]])
                                lv = wk[1][po, 0:N_CMP]
                                act(lv, ssq, AF.Ln, [PSK[5]], [("wk", 1)], scale=1.0 / 64, bias=EPS)
                                act(lv, lv, AF.Exp, [("wk", 1)], [("wk", 1)], scale=-0.5)
                                kn = wk[2][po, 0:N_CMP]
                                kns = wk[3][po, 0:N_CMP]
                                stt(kn, pk_, gvec[po, 6:7], lv, ALU.mult, ALU.mult, [PSK[3], ("small_gv", par), ("wk", 1)],
                                    [("wk", 2)])
                                stt(kns, pk2, gvec[po, 7:8], lv, ALU.mult, ALU.mult, [PSK[3], ("small_gv", par), ("wk", 1)],
                                    [("wk", 3)])
                                tt(kn, kn, cosc[po, 0:N_CMP], ALU.mult, [("wk", 2), "cosc"], [("wk", 2)])
                                tt(kns, kns, sinc[po, 0:N_CMP], ALU.mult, [("wk", 3), "sinc"], [("wk", 3)], eng="pool")
                                tt(kcT[po, 0:N_CMP], kn, kns, ALU.add, [("wk", 2), ("wk", 3)], ["kcT"])
                            else:
                                pv_ = PS[3][0:N_CMP, 256:320]
                                mm(pv_, hid[:, 0:N_CMP], w2v[:, :], True, True, ["hid", "w2v"], [PSK[3]])
                                cp(vc1[0:N_CMP, g_, 0:64], pv_, [PSK[3]], ["vc1"])
                    dump(f"kcT{l}", kcT[:], ["kcT"])
                    dump(f"vc1{l}", vc1[:], ["vc1"])
                    T.barrier()
                qT = sb(s3, "qT", [128, 4, S], BF16)
                ksT = sb(s3, "ksT", [128, S], BF16)
                kwT = sb(s3, "kwT", [128, S], BF16)
                v1 = sb(s3, "v1", [128, NT, 2, 2, 66], BF16)
                T.op("pool", lambda: nc.gpsimd.memset(v1[:, :, :, :, 64:66], 1.0), [], ["v1"])
                with ExitStack() as s3b:
                    wk = [sb(s3b, f"wk{i}", [128, 512], F32) for i in range(4)]
                    cos_t = sb(s3b, "cos_t", [128, 512], F32)
                    sin_t = sb(s3b, "sin_t", [128, 512], F32)
                    jobs = [(f"nq{c}", f"nqs{c}", qT[:, c, :], "qT", 0) for c in range(4)]
                    jobs += [("ks", "kss", ksT[:], "ksT", 2), ("kw", "kws", kwT[:], "kwT", 4)]
                    nxt = load_group(l, [jobs[0][0], jobs[0][1]])
                    for ji, (na, nb_, dest, dkey, gi) in enumerate(jobs):
                        g = nxt
                        if ji + 1 < len(jobs):
                            nxt = load_group(l, [jobs[ji + 1][0], jobs[ji + 1][1]])
                        else:
                            nxt = load_group(l, ["vs", "vw", "ng"])
                        for tq in range(4):
                            tsl = slice(tq * 512, (tq + 1) * 512)
                            T.dma("sp", cos_t[:], I["cos"][:, tsl], writes=["cos_t"])
                            T.dma("sp", sin_t[:], I["sin"][:, tsl], writes=["sin_t"])
                            pa, pak = PS[tq % 2], PSK[tq % 2]
                            pbb, pbk = PS[2 + tq % 2], PSK[2 + tq % 2]
                            pq, pqk = PS[4 + tq % 2], PSK[4 + tq % 2]
                            proj_F(pa[:], pak, g[na], tq)
                            proj_F(pbb[:], pbk, g[nb_], tq)
                            sq = pb16[tq % 2]
                            sqk = ("pb16", tq % 2)
                            act(sq[:], pa[:], AF.Square, [pak], [sqk])
                            mm(pq[:], bd_bf, sq[:], True, True, ["cmat", sqk], [pqk])
                            act(wk[0][:], pq[:], AF.Ln, [pqk], [("wk", 0)], scale=1.0 / 64, bias=EPS)
                            act(wk[0][:], wk[0][:], AF.Exp, [("wk", 0)], [("wk", 0)], scale=-0.5)
                            stt(wk[1][:], pa[:], gvec[:, gi:gi + 1], wk[0][:], ALU.mult, ALU.mult,
                                [pak, ("small_gv", par), ("wk", 0)], [("wk", 1)])
                            stt(wk[2][:], pbb[:], gvec[:, gi + 1:gi + 2], wk[0][:], ALU.mult, ALU.mult,
                                [pbk, ("small_gv", par), ("wk", 0)], [("wk", 2)])
                            tt(wk[1][:], wk[1][:], cos_t[:], ALU.mult, [("wk", 1), "cos_t"], [("wk", 1)],
                               eng="pool")
                            tt(wk[2][:], wk[2][:], sin_t[:], ALU.mult, [("wk", 2), "sin_t"], [("wk", 2)],
                               eng="pool")
                            tt(dest[:, tsl], wk[1][:], wk[2][:], ALU.add, [("wk", 1), ("wk", 2)], [dkey])
                    dump(f"qT{l}", qT[:, :, 0:256], ["qT"])
                    dump(f"ksT{l}", ksT[:, 0:512], ["ksT"])
                    g = nxt
                    nzg = load_group(l, [f"nz{j}" for j in range(4)])
                    slot0 = g["vs"][0]
                    for t in range(NT):
                        pb, pk = PS[t % 2], PSK[t % 2]
                        for kc in range(8):
                            mm(pb[:, 0:256], hT[:, kc, t * 128:(t + 1) * 128],
                               wpool[:, slot0:slot0 + 2, kc * 128:(kc + 1) * 128],
                               kc == 0, kc == 7, ["hT", ("w", slot0), ("w", slot0 + 1)], [pk])
                        cp(v1[:, t, :, :, 0:64], pb[:, 0:256].rearrange("p (a b c) -> p a b c", a=2, b=2), [pk], ["v1"],
                           eng="dve" if t % 2 == 0 else "act")
                    ngi = g["ng"]
                    for t in range(NT):
                        for kc in range(8):
                            mm(PS[2][:, t * 32:t * 32 + 24], hT[:, kc, t * 128:(t + 1) * 128], wview(ngi, kc),
                               kc == 0, kc == 7, ["hT", ("w", ngi[0])], [PSK[2]], inc=(kc == 7 and t == NT - 1))
                    sg = sigg[:].rearrange("p a b -> p (a b)")
                    act(sg, PS[2][:], AF.Exp, [PSK[2]], ["sigg"], scale=-1.0)
                    ts(sg, sg, 1.0, None, ALU.add, None, ["sigg"], ["sigg"])
                    recip(sg, sg, ["sigg"], ["sigg"])
                    dump(f"sigg{l}", sigg[:], ["sigg"])
                    dump(f"v1{l}", v1[:, 0:2], ["v1"])
                    T.barrier()
                nzslot = nzg["nz0"][0]
                with ExitStack() as s3c:
                    szs = [sb(s3c, "sz0", [128, 512], F32)] * 2
                    qmg = [sb(s3c, f"qmg{i}", [128, 4, 128], BF16) for i in range(2)]
                    T.op("pool", lambda: nc.gpsimd.memset(qmg[0][64:128], 0.0), [], [("qmg", 0)])
                    T.op("pool", lambda: nc.gpsimd.memset(qmg[1][0:64], 0.0), [], [("qmg", 1)])
                    negx = sb(s3c, "negx", [128, 32 * 64], BF16)
                    vbias = sb(s3c, "vbias", [128, S], BF16)
                    impm = sb(s3c, "impm", [128, NT, 32], BF16)
                    impa = sb(s3c, "impa", [128, NT, 32], BF16)
                    T.dma("pool", vbias[:], I["vbias"], writes=["vbias"], **CAST)
                    T.dma("pool", impm[:], I["impm"].rearrange("p (a b) -> p a b", a=NT), writes=["impm"], **CAST)
                    T.dma("pool", impa[:], I["impa"].rearrange("p (a b) -> p a b", a=NT), writes=["impa"], **CAST)
                    ocomb = sb(s3c, "ocomb", [128, 512], F32)
                    b16 = sb(s3c, "b16", [128, 512], BF16)
                    tiles = []
                    for tb in range(NT):
                        kbs = [kb for kb in (tb - 2, tb - 1, tb) if kb >= 0]
                        nW = len(kbs)
                        tiles.append(dict(tb=tb, g=0, br="c", kb=0, first=True, last=True))
                        tiles.append(dict(tb=tb, g=0, br="nz", kb=0))
                        for i, kb in enumerate(kbs):
                            tiles.append(dict(tb=tb, g=0, br="w", kb=kb, first=(i == 0), last=(i == nW - 1)))
                        tiles.append(dict(tb=tb, g=1, br="c", kb=0, first=True, last=True))
                        if tb > 0:
                            tiles.append(dict(tb=tb - 1, g=0, br="tr", kb=0))
                        for kb in range(tb + 1):
                            tiles.append(dict(tb=tb, g=0, br="s", kb=kb, first=(kb == 0), last=(kb == tb), tail=(kb == tb)))
                        for i, kb in enumerate(kbs):
                            tiles.append(dict(tb=tb, g=1, br="w", kb=kb, first=(i == 0), last=(i == nW - 1),
                                              expand=(nW > 1 and i == 1)))
                        for kb in range(tb + 1):
                            tiles.append(dict(tb=tb, g=1, br="s", kb=kb, first=(kb == 0), last=(kb == tb), tail=(kb == tb),
                                              expand=(nW == 1 and kb == 0)))
                    ntl = len(tiles)
                    NTR = 3
                    tmpI = sml[:, 128:256].rearrange("p (h j) -> p h j", h=4)
                    tmpO = sb(s3c, "tmpO", [128, 4, 64], F32)

                    def acc_of(tb, g_, br):
                        b_ = {"c": 3, "w": 4 + g_, "s": 6 + g_}[br]
                        return PS[b_][:].rearrange("p (a b) -> p a b", a=4), PSK[b_]

                    def nz_P1(tb, zs, zsk):
                        tbs = slice(tb * 128, (tb + 1) * 128)
                        for kc in range(8):
                            mm(zs[:], hT[:, kc, tbs], wpool[:, nzslot:nzslot + 4, kc * 128:(kc + 1) * 128],
                               kc == 0, kc == 7, ["hT"] + [("w", nzslot + i) for i in range(4)], [zsk])

                    def nz_A1(tb, zs, zsk):
                        sz, szk = szs[0], ("sz", 0)
                        act(sz[:], zs[:], AF.Exp, [zsk], [szk], scale=-1.0)
                        act(sz[:], sz[:], AF.Ln, [szk], [szk], bias=1.0)
                        act(sz[:], sz[:], AF.Exp, [szk], [szk], scale=-1.0)
                        tt(sz[:], zs[:], sz[:], ALU.mult, [zsk, szk], [szk])

                    def expand(tb, g_):
                        nbk = 2 * (tb + 1)
                        cp(negx[:, 0:nbk * 64].rearrange("p (a b) -> p a b", b=64),
                           bcast(negsel[:, g_, 0:nbk], 1, 64), [("negsel", g_)], ["negx"])

                    def P1(j):
                        t_ = tiles[j]
                        tb, g_, br, kb = t_["tb"], t_["g"], t_["br"], t_["kb"]
                        tbs = slice(tb * 128, (tb + 1) * 128)
                        po = slice(64 * g_, 64 * g_ + 64)
                        qg = qmg[g_][:]
                        qk_ = ("qmg", g_)
                        zs, zsk = PS[j % NTR], PSK[j % NTR]
                        if br == "nz":
                            nz_P1(tb, zs, zsk)
                            return
                        if br == "tr":
                            for c in range(4):
                                mm(zs[:, c * 128:(c + 1) * 128], b16[:, c * 128:(c + 1) * 128], ident_bf, True, True,
                                   ["b16", "cmat"], [zsk], inc=(c == 3))
                            return
                        if t_.get("expand"):
                            expand(tb, 1)
                        if br == "c":
                            cp(qmg[g_][po], qT[po, :, tbs], ["qT"], [qk_], eng="pool")
                            mm(zs[0:N_CMP, :], kcT[:, 0:N_CMP], qg, True, False, ["kcT", qk_], [zsk], inc=False)
                            mm(zs[0:N_CMP, :], ident_bf[0:N_CMP, 0:N_CMP], bcast(vbias[0:N_CMP, tbs], 0, 4), False, True,
                               ["cmat", "vbias"], [zsk])
                        elif br == "w":
                            nob = (kb == tb - 1)
                            mm(zs[:], kwT[:, kb * 128:(kb + 1) * 128], qg, True, nob, ["kwT", qk_], [zsk], inc=nob)
                            if kb == tb:
                                mm(zs[:], ident_bf, bcast(big[:, 385:513], 0, 4), False, True, ["cmat", "big"], [zsk])
                            elif kb == tb - 2:
                                mm(zs[:], ident_bf, bcast(w2m_bf, 0, 4), False, True, ["cmat"], [zsk])
                        else:
                            mm(zs[:], ksT[:, kb * 128:(kb + 1) * 128], qg, True, False, ["ksT", qk_], [zsk], inc=False)
                            mm(zs[:], negx[:, 2 * kb * 64:(2 * kb + 2) * 64], bcast(ident_bf, 0, 4), False, kb != tb,
                               ["negx", "cmat"], [zsk], inc=(kb != tb))
                            if kb == tb:
                                mm(zs[:], ident_bf, bcast(big[:, 385:513], 0, 4), False, True, ["cmat", "big"], [zsk])

                    def A1(j):
                        t_ = tiles[j]
                        nr = N_CMP if t_["br"] == "c" else 128
                        zs, zsk = PS[j % NTR], PSK[j % NTR]
                        if t_["br"] == "nz":
                            nz_A1(t_["tb"], zs, zsk)
                            return
                        if t_["br"] == "tr":
                            tbs = slice(t_["tb"] * 128, (t_["tb"] + 1) * 128)
                            cp(bT[:, :, tbs], zs[:].rearrange("p (a b) -> p a b", a=4), [zsk], ["bT"], eng="act")
                            return
                        p, pk = pb16[j % 3], ("pb16", j % 3)
                        act(p[0:nr, :], zs[0:nr, :], AF.Exp, [zsk], [pk], scale=0.125)

                    def P2(j):
                        t_ = tiles[j]
                        tb, g_, br, kb = t_["tb"], t_["g"], t_["br"], t_["kb"]
                        if br in ("nz", "tr"):
                            return
                        p, pk = pb16[j % 3], ("pb16", j % 3)
                        a3, ak = acc_of(tb, g_, br)
                        for h in range(4):
                            st_ = t_["first"] and h == 0
                            sp_ = t_["last"] and h == 3
                            if br == "c":
                                mm(a3[:, h, 0:97], p[0:N_CMP, h * 128:(h + 1) * 128], vc1[0:N_CMP, g_, 0:97],
                                   st_, sp_, [pk, "vc1"], [ak], inc=(h == 3))
                            else:
                                mm(a3[:, h, 0:65], p[:, h * 128:(h + 1) * 128],
                                   v1[:, kb, 0 if br == "s" else 1, g_, 0:65], st_, sp_, [pk, "v1"], [ak], inc=(h == 3))
                        if br == "c":
                            select(tb, g_)
                        if t_.get("tail"):
                            combine(tb, g_)

                    def select(tb, g_):
                        a3, ak = acc_of(tb, g_, "c")
                        rc = sml[:, 16 * g_:16 * g_ + 4]
                        rck = ("sml_rc", g_)
                        ts(rc, a3[:, :, 64], 1e-30, None, ALU.max, None, [ak], [rck])
                        recip(rc, rc, [rck], [rck])
                        imp = sml[:, 32 + 32 * g_:64 + 32 * g_]
                        ik = ("sml_imp", g_)
                        tt(tmpI[:], a3[:, :, 65:97], bcast(rc, 1, 32), ALU.mult, [ak, rck], ["tmpI"])
                        T.op("dve", lambda: nc.vector.tensor_reduce(out=imp, in_=tmpI[:].rearrange("p h j -> p j h"),
                                                                    axis=mybir.AxisListType.X, op=ALU.add),
                             ["tmpI"], [ik])
                        tt(imp, imp, impm[:, tb, :], ALU.mult, [ik, "impm"], [ik])
                        tt(imp, imp, impa[:, tb, :], ALU.add, [ik, "impa"], [ik])
                        top8 = sml[:, 96 + 8 * g_:104 + 8 * g_]
                        tk = ("sml_top", g_)
                        T.op("dve", lambda: nc.vector.max(out=top8, in_=imp), [ik], [tk])
                        ts(negsel[:, g_, :], imp, top8[:, 7:8], NEGB, ALU.is_lt, ALU.mult, [ik, tk], [("negsel", g_)])
                        if g_ == 0:
                            expand(tb, 0)
                        tt(rc, rc, sigg[:, tb, 0 + 4 * g_:4 + 4 * g_], ALU.mult, [rck, "sigg"], [rck])
                        ok = ("ocomb", g_)
                        tt(ocomb[:, g_ * 256:(g_ + 1) * 256].rearrange("p (h d) -> p h d", h=4), a3[:, :, 0:64],
                           bcast(rc, 1, 64), ALU.mult, [ak, rck], [ok])

                    def combine(tb, g_):
                        tbs = slice(tb * 128, (tb + 1) * 128)
                        ok = ("ocomb", g_)
                        ocv = ocomb[:, g_ * 256:(g_ + 1) * 256].rearrange("p (h d) -> p h d", h=4)
                        for br, gi, off in (("w", 16, 4), ("s", 8, 8)):
                            a3, ak = acc_of(tb, g_, br)
                            r_ = sml[:, 16 * g_ + off:16 * g_ + off + 4]
                            rk = ("sml_r" + br, g_)
                            recip(r_, a3[:, :, 64], [ak], [rk])
                            tt(r_, r_, sigg[:, tb, gi + 4 * g_:gi + 4 + 4 * g_], ALU.mult, [rk, "sigg"], [rk])
                            tt(tmpO[:], a3[:, :, 0:64], bcast(r_, 1, 64), ALU.mult, [ak, rk], ["tmpO"])
                            tt(ocv, ocv, tmpO[:], ALU.add, [ok, "tmpO"], [ok])
                        if g_ == 1:
                            if tb == 5:
                                dump(f"ocomb{l}", ocomb[:], [("ocomb", 0), ("ocomb", 1)])
                            tt(b16[:], ocomb[:], szs[0][:], ALU.mult, [("ocomb", 0), ("ocomb", 1), ("sz", 0)],
                               ["b16"])

                    for j in range(ntl + 2):
                        if j < ntl:
                            P1(j)
                        if 0 <= j - 1 < ntl:
                            A1(j - 1)
                        if 0 <= j - 2 < ntl:
                            P2(j - 2)
                    tiles.append(dict(tb=NT - 1, g=0, br="tr", kb=0))
                    P1(ntl)
                    A1(ntl)
                    dump(f"bT{l}", bT[:, :, 0:256], ["bT"])
                    T.barrier()
            if stop_after == ("s3", l):
                break

            with ExitStack() as s4:
                yT = sb(s4, "yT", [128, 8, S], BF16)
                ra = [sb(s4, f"ra{i}", [128, 512], F32) for i in range(2)]
                rb = [sb(s4, f"rb{i}", [128, 512], F32) for i in range(2)]
                nxt = load_group(l, ["wpa0", "wpb0", "ma0", "mb0"])
                it = 0
                for n in range(8):
                    g = nxt
                    if n < 7:
                        nxt = load_group(l, [f"wpa{n + 1}", f"wpb{n + 1}", f"ma{n + 1}", f"mb{n + 1}"])
                    else:
                        nxt = load_group(l, ["wo0", "wo1", "wo2", "wo3"])
                    for tq in range(4):
                        tsl = slice(tq * 512, (tq + 1) * 512)
                        r = it % 2
                        it += 1
                        p_a, p_ak = PS[0 + r], PSK[0 + r]
                        p_b, p_bk = PS[2 + r], PSK[2 + r]
                        p_ma, p_mak = PS[4 + r], PSK[4 + r]
                        p_mb, p_mbk = PS[6 + r], PSK[6 + r]
                        ia, ib = g[f"wpa{n}"], g[f"wpb{n}"]
                        for fc in range(4):
                            mm(p_a[:], wview(ia, fc), aT[:, fc, tsl], fc == 0, fc == 3, [("w", ia[0]), "aT"], [p_ak])
                        for fc in range(4):
                            mm(p_b[:], wview(ib, fc), bT[:, fc, tsl], fc == 0, fc == 3, [("w", ib[0]), "bT"], [p_bk])
                        proj_F(p_ma[:], p_mak, g[f"ma{n}"], tq)
                        proj_F(p_mb[:], p_mbk, g[f"mb{n}"], tq)
                        act(ra[r][:], p_ma[:], AF.Exp, [p_mak], [("ra", r)], scale=-1.0)
                        act(rb[r][:], p_mb[:], AF.Exp, [p_mbk], [("rb", r)], scale=-1.0)
                        act(ra[r][:], ra[r][:], AF.Ln, [("ra", r)], [("ra", r)], bias=1.0)
                        act(rb[r][:], rb[r][:], AF.Ln, [("rb", r)], [("rb", r)], bias=1.0)
                        act(ra[r][:], ra[r][:], AF.Exp, [("ra", r)], [("ra", r)], scale=-1.0)
                        act(rb[r][:], rb[r][:], AF.Exp, [("rb", r)], [("rb", r)], scale=-1.0)
                        tt(ra[r][:], p_a[:], ra[r][:], ALU.mult, [p_ak, ("ra", r)], [("ra", r)])
                        tt(rb[r][:], p_b[:], rb[r][:], ALU.mult, [p_bk, ("rb", r)], [("rb", r)])
                        tt(yT[:, n, tsl], ra[r][:], rb[r][:], ALU.add, [("ra", r), ("rb", r)], ["yT"])
                dump(f"yT{l}", yT[:, :, 0:256], ["yT"])
                if l + 1 < n_layers:
                    emit_mod(l + 1, s4)
                for half in range(2):
                    g = nxt
                    if half == 0:
                        nxt = load_group(l, ["wo4", "wo5", "wo6", "wo7"])
                    for nn in range(4):
                        n = half * 4 + nn
                        info = g[f"wo{n}"]
                        for tq in range(4):
                            tsl = slice(tq * 512, (tq + 1) * 512)
                            r = it % 2
                            it += 1
                            po_, pok = PS[r], PSK[r]
                            for kc in range(8):
                                mm(po_[:], wview(info, kc), yT[:, kc, tsl], kc == 0, kc == 7, [("w", info[0]), "yT"],
                                   [pok])
                            stt(xT[:, n, tsl], po_[:], gatev[:, n:n + 1], xT[:, n, tsl], ALU.mult, ALU.add,
                                [pok, ("gatev", par), "xT"], ["xT"])
                T.barrier()

        if stop_after is None or True:
            with ExitStack() as pe_:
                xo = [sb(pe_, f"xo{i}", [128, D], F32) for i in range(2)]
                for t in range(NT):
                    xi = xo[t % 2]
                    xk = ("xo", t % 2)
                    for half in range(2):
                        pb = PS[(2 * t + half) % 4]
                        pk = PSK[(2 * t + half) % 4]
                        for j in range(4):
                            c = half * 4 + j
                            T.op("pe", lambda: nc.tensor.transpose(pb[:, j * 128:(j + 1) * 128],
                                                                   xT[:, c, t * 128:(t + 1) * 128], cmat_f[:]),
                                 ["xT", "cmat_f"], [pk], inc=(j == 3))
                        cp(xi[:, half * 512:(half + 1) * 512], pb[:], [pk], [xk], eng="dve" if half == 0 else "act")
                    T.dma("sp", out_d[t * 128:(t + 1) * 128, :], xi[:], reads=[xk])
                T.finish("sp")
        build_program.stats = (T.n_inst, T.n_waits, len(T.sems))
        build_program.stuck = T.check_deadlock()
    return nc, dbg_out


_CACHE = {}


def kernel(**inputs):
    maps = prep_inputs(inputs)
    if "nc" not in _CACHE:
        _CACHE["nc"] = build_program()[0]
    nc = _CACHE["nc"]
    res = run_bass_kernel_spmd(nc, maps, core_ids=list(range(8)))
    out = np.stack([np.asarray(r["out"], dtype=np.float32) for r in res.results], axis=0)
    return out
```

```python
from contextlib import ExitStack
import numpy as np
import concourse.bass as bass
import concourse.mybir as mybir
from concourse.bass_utils import run_bass_kernel_spmd

F32 = mybir.dt.float32
BF16 = mybir.dt.bfloat16
AF = mybir.ActivationFunctionType
ALU = mybir.AluOpType

D = 1024
S = 2048
L_DEPTH = 4
NT = 16
NEGB = -32768.0
EPS = 1e-6
N_CMP = 127

import os as _os
SAME_ENGINE_SYNC = _os.environ.get("NO_SES", "") != "1"
SES_RAW_ONLY = _os.environ.get("NO_SES", "") == "2"


class Tracker:
    def __init__(self, nc, stack):
        self.nc = nc
        self.stack = stack
        self.engs = {"pe": nc.tensor, "act": nc.scalar, "dve": nc.vector,
                     "pool": nc.gpsimd, "sp": nc.sync}
        self.sems = {}
        self.count = {}
        for e in self.engs:
            self.sems[e] = stack.enter_context(nc.semaphore("s_" + e))
            self.count[e] = 0
        self.last_write = {}
        self.reads = {}
        self.seen = {e: {} for e in self.engs}
        self.n_waits = 0
        self.n_inst = 0
        self.log = {e: [] for e in self.engs}
        self._pending_waits = {e: [] for e in self.engs}

    def check_deadlock(self):
        cnt = {k: 0 for k in self.sems}
        pos = {e: 0 for e in self.engs}
        progress = True
        while progress:
            progress = False
            for e in self.engs:
                q = self.log[e]
                while pos[e] < len(q):
                    waits, inc = q[pos[e]]
                    if all(cnt[s_] >= v for s_, v in waits):
                        if inc is not None:
                            cnt[inc[0]] += inc[1]
                        pos[e] += 1
                        progress = True
                    else:
                        break
        stuck = {e: (pos[e], len(self.log[e]), self.log[e][pos[e]][0], {s_: cnt[s_] for s_, _ in self.log[e][pos[e]][0]})
                 for e in self.engs if pos[e] < len(self.log[e])}
        return stuck

    def _dma_src(self, key):
        sid = ("dma", key)
        if sid not in self.sems:
            nm = "d_" + "_".join(str(k) for k in (key if isinstance(key, tuple) else (key,)))
            self.sems[sid] = self.stack.enter_context(self.nc.semaphore(nm[:40]))
            self.count[sid] = 0
        return sid

    def _deps(self, e, reads, writes):
        need = {}

        def add(src, val):
            if src == e and (not SAME_ENGINE_SYNC or e in ("pe", "sp") or val > self.count[e]):
                return
            if need.get(src, 0) < val:
                need[src] = val

        for k in reads:
            if k in self.last_write:
                add(*self.last_write[k])
        for k in writes:
            if k in self.last_write:
                if not (SES_RAW_ONLY and self.last_write[k][0] == e):
                    add(*self.last_write[k])
            for src, val in self.reads.get(k, {}).items():
                if not (SES_RAW_ONLY and src == e):
                    add(src, val)
        out = []
        for src, val in need.items():
            if self.seen[e].get(src, 0) >= val:
                continue
            self.seen[e][src] = val
            out.append((src, val))
        return out

    def _emit_waits(self, e, deps):
        eng = self.engs[e]
        for src, val in deps:
            eng.wait_ge(self.sems[src], val)
            self.n_waits += 1
            self._pending_waits[e].append((src, val))

    def op(self, e, fn, reads=(), writes=(), inc=True):
        deps = self._deps(e, reads, writes)
        self._emit_waits(e, deps)
        ins = fn()
        self.n_inst += 1
        val = self.count[e] + 1
        self.log[e].append((self._pending_waits[e], (e, 1) if inc else None))
        self._pending_waits[e] = []
        if inc:
            ins.then_inc(self.sems[e], 1)
            self.count[e] = val
        for k in reads:
            d = self.reads.setdefault(k, {})
            d[e] = max(d.get(e, 0), val)
        for k in writes:
            self.last_write[k] = (e, val)
            self.reads[k] = {}
        return ins

    def dma(self, e, out, in_, reads=(), writes=(), **kw):
        deps = self._deps(e, reads, writes)
        self._emit_waits(e, deps)
        key = writes[0] if writes else ("rd",) + tuple(reads[:1])
        sid = self._dma_src(key)
        ins = self.engs[e].dma_start(out=out, in_=in_, **kw)
        self.log[e].append((self._pending_waits[e], (sid, 16)))
        self._pending_waits[e] = []
        self.count[sid] += 16
        val = self.count[sid]
        ins.then_inc(self.sems[sid], 16)
        self.n_inst += 1
        for k in reads:
            self.reads.setdefault(k, {})[sid] = val
        for k in writes:
            self.last_write[k] = (sid, val)
            self.reads[k] = {}
        return ins

    def barrier(self):
        for e in self.engs:
            for src, val in self.count.items():
                if src == e or val == 0:
                    continue
                if self.seen[e].get(src, 0) >= val:
                    continue
                self.seen[e][src] = val
                self.engs[e].wait_ge(self.sems[src], val)
                self.n_waits += 1
                self._pending_waits[e].append((src, val))
        self.last_write = {}
        self.reads = {}

    def finish(self, e="sp"):
        for src, val in self.count.items():
            if src == e or val == 0:
                continue
            if self.seen[e].get(src, 0) >= val:
                continue
            self.seen[e][src] = val
            self.engs[e].wait_ge(self.sems[src], val)


def bccol(ap, n):
    dims = [list(d) for d in ap.ap]
    dims[-1] = [0, n]
    return bass.AP(tensor=ap.tensor, offset=ap.offset, ap=dims)


def bcast(ap, pos, n):
    dims = [list(d) for d in ap.ap]
    dims.insert(1 + pos, [0, n])
    return bass.AP(tensor=ap.tensor, offset=ap.offset, ap=dims)


SWAP64 = np.concatenate([np.arange(32, 64), np.arange(0, 32)])


def _w_blocks():
    blks = []
    a = np.arange

    def add(name, cols):
        blks.append((name, "w_in", np.asarray(cols), 8))

    for j in range(4):
        add(f"sbv{j}", 1024 + j * 128 + a(128))
    for c in range(4):
        add(f"sbq{c}", 0 + c * 128 + a(128))
        add(f"sbk{c}", 512 + c * 128 + a(128))
        add(f"sbz{c}", 1536 + c * 128 + a(128))
    add("kc", 2560 + a(128))
    add("vc", 2688 + a(128))
    for c in range(4):
        hs = (c, c + 4)
        add(f"nq{c}", np.concatenate([2048 + h * 64 + a(64) for h in hs]))
        add(f"nqs{c}", np.concatenate([2048 + h * 64 + SWAP64 for h in hs]))
    add("ks", 2816 + a(128))
    add("kss", np.concatenate([2816 + g * 64 + SWAP64 for g in range(2)]))
    add("kw", 3072 + a(128))
    add("kws", np.concatenate([3072 + g * 64 + SWAP64 for g in range(2)]))
    add("vs", 2944 + a(128))
    add("vw", 3200 + a(128))
    add("ng", 3840 + a(24))
    for j in range(4):
        add(f"nz{j}", 3328 + j * 128 + a(128))
    for n in range(8):
        blks.append((f"wpa{n}", "w_proj_a", n * 128 + a(128), 4))
        blks.append((f"wpb{n}", "w_proj_b", n * 128 + a(128), 4))
        add(f"ma{n}", 3864 + n * 128 + a(128))
        add(f"mb{n}", 4888 + n * 128 + a(128))
    for n in range(8):
        blks.append((f"wo{n}", "w_out", n * 128 + a(128), 8))
    return blks


W_BLOCKS = _w_blocks()
W_OFF = {}
_off = 0
for _name, _src, _cols, _nk in W_BLOCKS:
    W_OFF[_name] = (_off, _nk, len(_cols))
    _off += _nk * len(_cols)
W_TOT = _off


def _consts():
    c = {}
    half = 32
    freq = (np.float32(10000.0) ** (-np.arange(half, dtype=np.float32) / np.float32(half))).astype(np.float32)
    pos = np.arange(S, dtype=np.float32)
    ang = (pos[:, None] * freq[None, :]).astype(np.float32)
    cos = np.cos(ang).astype(np.float32).T
    sin = np.sin(ang).astype(np.float32).T
    p = np.arange(128)
    sign = np.where((p % 64) < 32, -1.0, 1.0).astype(np.float32)[:, None]
    cos2 = cos[p % 32]
    sin2 = sin[p % 32] * sign
    c["cos"] = np.ascontiguousarray(cos2)
    c["sin"] = np.ascontiguousarray(sin2)
    posc = 16 * np.arange(N_CMP) + 31
    cc = np.zeros((128, 128), np.float32)
    sc = np.zeros((128, 128), np.float32)
    cc[:, :N_CMP] = cos2[:, posc]
    sc[:, :N_CMP] = sin2[:, posc]
    c["cosc"] = cc
    c["sinc"] = sc
    ident = np.eye(128, dtype=np.float32)
    ones = np.ones((128, 128), np.float32)
    jj, ss = np.meshgrid(np.arange(128), np.arange(128), indexing="ij")
    tri = (jj >= ss).astype(np.float32)
    bd = ((jj // 64) == (ss // 64)).astype(np.float32)
    w2m = np.where(jj > ss, 0.0, NEGB).astype(np.float32)
    c["cmat"] = np.concatenate([ident, ones, tri, bd, w2m, -8.0 * tri, -8.0 * ones], axis=1)
    u = np.arange(896)[None, :]
    s_ = np.arange(128)[:, None]
    c["big"] = np.where(s_ < u - 384, 0.0, NEGB).astype(np.float32)
    n_ = np.arange(128)[:, None]
    t_ = np.arange(S)[None, :]
    c["vbias"] = np.where((16 * n_ + 31 <= t_) & (n_ < N_CMP), 0.0, NEGB).astype(np.float32)
    ci = np.arange(128)[:, None]
    sj = np.arange(32)[None, :]
    ov = ((16 * ci < 64 * (sj + 1)) & (16 * ci + 32 > 64 * sj) & (ci < N_CMP)).astype(np.float32)
    c["ov1"] = np.concatenate([np.ones((128, 1), np.float32), ov], axis=1)
    t = np.arange(S)
    cur = (t // 64)[:, None]
    jb = np.arange(32)[None, :]
    m1 = np.ones((S, 32), np.float32)
    ad = np.zeros((S, 32), np.float32)
    fut = jb > cur
    m1[fut] = 0.0
    ad[fut] = -1e30
    frc = ((jb == 0) | (jb == cur - 1)) & ~fut
    m1[frc] = 0.0
    ad[frc] = 1e4
    cu = jb == cur
    m1[cu] = 0.0
    ad[cu] = 2e4
    c["impm"] = np.ascontiguousarray(m1.reshape(NT, 128, 32).transpose(1, 0, 2)).reshape(128, NT * 32)
    c["impa"] = np.ascontiguousarray(ad.reshape(NT, 128, 32).transpose(1, 0, 2)).reshape(128, NT * 32)
    return c


def prep_inputs(inp):
    f = lambda a: np.ascontiguousarray(np.asarray(a, dtype=np.float32))
    L = L_DEPTH
    shared = {}
    wst = np.empty((L, 128, W_TOT), np.float32)
    srcs = {k: f(inp[k]) for k in ("w_in", "w_proj_a", "w_proj_b", "w_out")}
    for name, src, cols, nk in W_BLOCKS:
        off, _, nc_ = W_OFF[name]
        w = srcs[src][:, :, cols]
        w = w.reshape(L, nk, 128, nc_).transpose(0, 2, 1, 3).reshape(L, 128, nk * nc_)
        wst[:, :, off:off + nk * nc_] = w
    shared["wst"] = wst
    ada_w = f(inp["ada_w"])
    shared["ada_w"] = np.ascontiguousarray(
        ada_w.reshape(L, 8, 128, 6, 512).transpose(0, 3, 2, 1, 4).reshape(L, 6, 128, 8 * 512))
    shared["ada_b"] = np.ascontiguousarray(f(inp["ada_b"]).reshape(L, 24, 128).transpose(0, 2, 1))
    shared["norm_g"] = np.ascontiguousarray(f(inp["norm_g"]).reshape(L, 8, 128).transpose(0, 2, 1))
    p = np.arange(128)
    gv = np.zeros((L, 128, 8), np.float32)
    for i, k in enumerate(("q_norm_g", "ks_norm_g", "kw_norm_g", "kc_norm_g")):
        g = f(inp[k])
        gv[:, :, 2 * i] = g[:, p % 64]
        gv[:, :, 2 * i + 1] = g[:, SWAP64[p % 64]]
    shared["gvec"] = gv
    for nm in ("k", "v"):
        w1 = f(inp[f"cmp_w1_{nm}"])
        w1 = w1.reshape(L, 32, 64, 128).transpose(0, 2, 1, 3).reshape(L, 64, 32 * 128)
        shared[f"w1{nm}"] = np.ascontiguousarray(np.concatenate([w1, w1], axis=1))
        pe = f(inp[f"cmp_pe_{nm}"]).transpose(0, 2, 1)
        shared[f"pe{nm}"] = np.ascontiguousarray(np.concatenate([pe, pe], axis=1))
    w2k = f(inp["cmp_w2_k"])
    shared["w2k"] = np.ascontiguousarray(np.concatenate([w2k, w2k[:, :, SWAP64]], axis=2))
    shared["w2v"] = f(inp["cmp_w2_v"])
    shared.update(_consts())
    x = f(inp["x"])
    c = f(inp["c"])
    maps = []
    for b in range(8):
        m = dict(shared)
        m["x"] = x[b]
        m["c"] = np.ascontiguousarray(c[b].reshape(8, 128).T)
        maps.append(m)
    return maps


IN_SHAPES = {
    "x": [S, D], "c": [128, 8], "wst": [L_DEPTH, 128, W_TOT], "ada_w": [L_DEPTH, 6, 128, 4096],
    "ada_b": [L_DEPTH, 128, 24], "norm_g": [L_DEPTH, 128, 8], "gvec": [L_DEPTH, 128, 8],
    "w1k": [L_DEPTH, 128, 4096], "w1v": [L_DEPTH, 128, 4096], "pek": [L_DEPTH, 128, 32],
    "pev": [L_DEPTH, 128, 32], "w2k": [L_DEPTH, 128, 128], "w2v": [L_DEPTH, 128, 64],
    "cos": [128, S], "sin": [128, S], "cosc": [128, 128], "sinc": [128, 128],
    "cmat": [128, 896], "big": [128, 896], "vbias": [128, S], "ov1": [128, 33],
    "impm": [128, NT * 32], "impa": [128, NT * 32],
}


def build_program(n_layers=L_DEPTH, debug=(), stop_after=None):
    nc = bass.Bass("TRN2", target_bir_lowering=False)
    I = {k: nc.dram_tensor(k, shp, F32, kind="ExternalInput").ap() for k, shp in IN_SHAPES.items()}
    out_d = nc.dram_tensor("out", [S, D], F32, kind="ExternalOutput").ap()
    dbg_out = {}

    with ExitStack() as st:
        T = Tracker(nc, st)

        uid = [0]

        def sb(stack, name, shape, dt):
            uid[0] += 1
            return stack.enter_context(nc.sbuf_tensor(f"sb{uid[0]}_{name}", shape, dt))

        PS = [st.enter_context(nc.psum_tensor(f"ps{i}", [128, 512], F32)) for i in range(8)]
        PSK = [("ps", i) for i in range(8)]

        def mm(out, lhsT, rhs, start, stop, reads, writes, inc=None):
            if inc is None:
                inc = stop
            return T.op("pe", lambda: nc.tensor.matmul(out, lhsT=lhsT, rhs=rhs, start=start, stop=stop,
                                                       skip_group_check=True),
                        reads, writes, inc=inc)

        def act(out, in_, func, reads, writes, **kw):
            return T.op("act", lambda: nc.scalar.activation(out, in_, func, **kw), reads, writes)

        def E(eng):
            return nc.vector if eng == "dve" else nc.gpsimd

        def tt(out, in0, in1, op, reads, writes, eng="dve"):
            return T.op(eng, lambda: E(eng).tensor_tensor(out, in0, in1, op), reads, writes)

        def ts(out, in0, s1, s2, op0, op1, reads, writes, eng="dve"):
            if s2 is None:
                return T.op(eng, lambda: E(eng).tensor_scalar(out, in0, s1, None, op0), reads, writes)
            return T.op(eng, lambda: E(eng).tensor_scalar(out, in0, s1, s2, op0, op1), reads, writes)

        def stt(out, in0, scalar, in1, op0, op1, reads, writes):
            return T.op("dve", lambda: nc.vector.scalar_tensor_tensor(out, in0, scalar, in1, op0, op1), reads, writes)

        def cp(out, in_, reads, writes, eng="dve"):
            if eng == "act":
                return T.op("act", lambda: nc.scalar.copy(out, in_), reads, writes)
            return T.op(eng, lambda: E(eng).tensor_copy(out, in_), reads, writes)

        def recip(out, in_, reads, writes):
            return T.op("dve", lambda: nc.vector.reciprocal(out, in_), reads, writes)

        def dump(name, ap, reads):
            if name not in debug:
                return
            shp = list(ap.shape)
            d = nc.dram_tensor("dbg_" + name, shp, ap.dtype, kind="ExternalOutput").ap()
            dbg_out[name] = d
            T.dma("sp", d, ap, reads=reads)

        xT = sb(st, "xT", [128, 8, S], F32)
        hT = sb(st, "hT", [128, 8, S], BF16)
        aT = sb(st, "aT", [128, 4, S], BF16)
        bT = sb(st, "bT", [128, 4, S], BF16)
        NSLOT = 8
        wpool = sb(st, "wpool", [128, NSLOT, 1024], BF16)
        cmat_f = sb(st, "cmat_f", [128, 128], F32)
        cmat = sb(st, "cmat", [128, 896], BF16)
        big = sb(st, "big", [128, 896], BF16)
        ov1 = sb(st, "ov1", [128, 33], BF16)
        cosc = sb(st, "cosc", [128, 128], F32)
        sinc = sb(st, "sinc", [128, 128], F32)
        c_sb = sb(st, "c_sb", [128, 8], F32)
        siluc = sb(st, "siluc", [128, 8, 2], F32)
        small = sb(st, "small", [128, 128], F32)
        siluc_bf = sb(st, "siluc_bf", [128, 8, 2], BF16)
        ident_bf = cmat[:, 0:128]
        ones_bf = cmat[:, 128:256]
        tri_bf = cmat[:, 256:384]
        bd_bf = cmat[:, 384:512]
        w2m_bf = cmat[:, 512:640]
        tri8_bf = cmat[:, 640:768]
        ones8_bf = cmat[:, 768:896]
        def smv(par):
            o = 64 * par
            return (small[:, o:o + 8], small[:, o + 8:o + 16], small[:, o + 16:o + 24], small[:, o + 24:o + 32],
                    small[:, o + 32:o + 40], small[:, o + 40:o + 64])

        def emit_mod(lm, stack):
            par = lm % 2
            gsv, shiftv, gatev, normg, gvec, modT = smv(par)
            adab = [sb(stack, f"adab{i}", [128, 8, 512], BF16) for i in range(2)]
            adab_b = sb(stack, "adab_b", [128, 24], F32)
            T.dma("sp", adab_b[:], I["ada_b"][lm], writes=["adab_b"])
            T.dma("sp", normg, I["norm_g"][lm], writes=[("small_ng", par)])
            T.dma("sp", gvec, I["gvec"][lm], writes=[("small_gv", par)])
            for nb in range(6):
                ab = adab[nb % 2]
                ak = ("adab", nb % 2)
                T.dma("pool", ab[:], I["ada_w"][lm, nb].rearrange("p (a b) -> p a b", a=8), writes=[ak], **CAST)
                for jj in range(4):
                    j = nb * 4 + jj
                    for kc in range(8):
                        mm(PS[7][:, 2 * j:2 * j + 2], ab[:, kc, jj * 128:(jj + 1) * 128], siluc_bf[:, kc, :],
                           kc == 0, kc == 7, [ak, "siluc_bf"], [PSK[7]])
            tt(modT, PS[7][:, 0:48:2], adab_b[:], ALU.add, [PSK[7], "adab_b"], [("modT", par)])
            stt(gsv, modT[:, 8:16], 1.0, normg, ALU.add, ALU.mult, [("modT", par), ("small_ng", par)], [("gsv", par)])
            cp(shiftv, modT[:, 0:8], [("modT", par)], [("shiftv", par)])
            cp(gatev, modT[:, 16:24], [("modT", par)], [("gatev", par)])
            dump(f"mod{lm}", modT, [("modT", par)])

        CAST = dict(max_dma_last_dim=4096)
        T.dma("sp", cmat_f[:], I["cmat"][:, 0:128], writes=["cmat_f"])
        T.dma("pool", cmat[:], I["cmat"], writes=["cmat"], **CAST)
        T.dma("pool", big[:], I["big"], writes=["big"], **CAST)
        T.dma("pool", ov1[:], I["ov1"], writes=["ov1"], **CAST)
        T.dma("sp", cosc[:], I["cosc"], writes=["cosc"])
        T.dma("sp", sinc[:], I["sinc"], writes=["sinc"])
        T.dma("sp", c_sb[:], I["c"], writes=["c_sb"])

        wstate = {"half": 0}

        def load_group(l, names):
            half = wstate["half"]
            wstate["half"] ^= 1
            res = {}
            for i, nm in enumerate(names):
                off, nk, ncol = W_OFF[nm]
                slot = half * 4 + i
                T.dma("pool", wpool[:, slot, 0:nk * ncol], I["wst"][l, :, off:off + nk * ncol],
                      writes=[("w", slot)], **CAST)
                res[nm] = (slot, nk, ncol)
            return res

        pref = {}

        def wview(info, kc):
            slot, nk, ncol = info
            return wpool[:, slot, kc * ncol:(kc + 1) * ncol]

        def proj_F(psum_ap, pkey, info, tq, extra_reads=()):
            slot, nk, ncol = info
            for kc in range(8):
                mm(psum_ap, wview(info, kc), hT[:, kc, tq * 512:(tq + 1) * 512], kc == 0, kc == 7,
                   [("w", slot), "hT"] + list(extra_reads), [pkey])

        with ExitStack() as ps_:
            xin = [sb(ps_, f"xin{i}", [128, D], F32) for i in range(2)]
            for t in range(NT):
                xi = xin[t % 2]
                xk = ("xin", t % 2)
                T.dma("sp", xi[:], I["x"][t * 128:(t + 1) * 128, :], writes=[xk])
                for half in range(2):
                    pb = PS[(2 * t + half) % 4]
                    pk = PSK[(2 * t + half) % 4]
                    for j in range(4):
                        c = half * 4 + j
                        T.op("pe", lambda: nc.tensor.transpose(pb[:, j * 128:(j + 1) * 128],
                                                               xi[:, c * 128:(c + 1) * 128], cmat_f[:]),
                             [xk, "cmat_f"], [pk], inc=(j == 3))
                    cp(xT[:, half * 4:half * 4 + 4, t * 128:(t + 1) * 128],
                       pb[:].rearrange("p (a b) -> p a b", a=4), [pk], ["xT"],
                       eng="dve" if half == 0 else "act")
            act(siluc[:, :, 0], c_sb[:], AF.Exp, ["c_sb"], ["siluc"], scale=-1.0)
            ts(siluc[:, :, 0], siluc[:, :, 0], 1.0, None, ALU.add, None, ["siluc"], ["siluc"])
            recip(siluc[:, :, 0], siluc[:, :, 0], ["siluc"], ["siluc"])
            tt(siluc[:, :, 0], siluc[:, :, 0], c_sb[:], ALU.mult, ["siluc", "c_sb"], ["siluc"])
            cp(siluc[:, :, 1], siluc[:, :, 0], ["siluc"], ["siluc"])
            cp(siluc_bf[:], siluc[:], ["siluc"], ["siluc_bf"])
            if stop_after != ("pro", 0):
                emit_mod(0, ps_)
            T.barrier()

        for l in range(n_layers if stop_after != ("pro", 0) else 0):
            with ExitStack() as s1:
                par = l % 2
                pref["s2"] = load_group(l, ["sbq0", "sbk0", "sbz0", "sbv0"])
                gsv, shiftv, gatev, normg, gvec, modT = smv(par)
                sqt = [sb(s1, f"sqt{i}", [128, 512], BF16) for i in range(2)]
                lnv = sb(s1, "lnv", [128, 512], F32)
                rstd = sb(s1, "rstd", [128, 512], F32)
                tmpf = [sb(s1, f"tmpf{i}", [128, 512], F32) for i in range(2)]
                for tq in range(4 if stop_after != ("s1a", l) else 0):
                    tsl = slice(tq * 512, (tq + 1) * 512)
                    for c in range(8):
                        sq = sqt[c % 2]
                        act(sq[:], xT[:, c, tsl], AF.Square, ["xT"], [("sqt", c % 2)])
                        mm(PS[1][:], ones_bf, sq[:], c == 0, c == 7, [("sqt", c % 2), "cmat"], [PSK[1]], inc=True)
                    act(lnv[:], PS[1][:], AF.Ln, [PSK[1]], ["lnv"], scale=1.0 / D, bias=EPS)
                    act(rstd[:], lnv[:], AF.Exp, ["lnv"], ["rstd"], scale=-0.5)
                    for c in range(8):
                        tf = tmpf[c % 2]
                        stt(tf[:], xT[:, c, tsl], gsv[:, c:c + 1], rstd[:], ALU.mult, ALU.mult,
                            ["xT", ("gsv", par), "rstd"], [("tmpf", c % 2)])
                        act(hT[:, c, tsl], tf[:], AF.Identity, [("tmpf", c % 2), ("shiftv", par)], ["hT"],
                            bias=shiftv[:, c:c + 1], scale=1.0)
                dump(f"hT{l}", hT[:, :, 0:256], ["hT"])
                T.barrier()
            if stop_after in (("s1", l), ("s1a", l)):
                break

            with ExitStack() as s2:
                sbvm = [[sb(s2, f"sbvm{pp}{i}", [128, NT, 128], BF16) for i in range(2)] for pp in range(2)]
                qmb = [[sb(s2, f"qm{pp}{i}", [128, S], BF16) for i in range(2)] for pp in range(2)]
                kcb = [sb(s2, f"kc_{pp}", [128, S], BF16) for pp in range(2)]
                NB = 3
                e_t = [sb(s2, f"e_t{i}", [128, 512], F32) for i in range(2)]
                sp_t = [sb(s2, f"sp_t{i}", [128, 512], BF16) for i in range(NB)]
                w_t = [sb(s2, f"w_t{i}", [128, 512], BF16) for i in range(NB)]
                lacc = [sb(s2, f"lacc{i}", [128, 512], BF16) for i in range(2)]
                zr = [sb(s2, f"zr{i}", [128, 512], F32) for i in range(2)]
                for pp in range(2):
                    T.op("pool", lambda: nc.gpsimd.memset(qmb[pp][0][64:128, :], 0.0), [], [f"qm{pp}0"])
                    T.op("pool", lambda: nc.gpsimd.memset(qmb[pp][1][0:64, :], 0.0), [], [f"qm{pp}1"])
                    T.op("pool", lambda: nc.gpsimd.memset(sbvm[pp][0][:, :, 64:128], 0.0), [], [f"sbvm{pp}0"])
                    T.op("pool", lambda: nc.gpsimd.memset(sbvm[pp][1][:, :, 0:64], 0.0), [], [f"sbvm{pp}1"])
                nxt = pref.pop("s2")
                qi = 0

                def v_unit(cn, gn, t):
                    pp = cn % 2
                    info = gn[f"sbv{cn}"]
                    pb, pk = PS[7], PSK[7]
                    for kc in range(8):
                        mm(pb[:, 0:128], hT[:, kc, t * 128:(t + 1) * 128], wview(info, kc), kc == 0, kc == 7,
                           ["hT", ("w", info[0])], [pk])
                    cp(sbvm[pp][0][:, t, 0:64], pb[:, 0:64], [pk], [f"sbvm{pp}0"], eng="dve")
                    cp(sbvm[pp][1][:, t, 64:128], pb[:, 64:128], [pk], [f"sbvm{pp}1"], eng="dve")

                def proj_unit(cn, gn, tq, which):
                    pp = cn % 2
                    tsl = slice(tq * 512, (tq + 1) * 512)
                    pb, pk = PS[7], PSK[7]
                    if which == "q":
                        proj_F(pb[:], pk, gn[f"sbq{cn}"], tq)
                        cp(qmb[pp][0][0:64, tsl], pb[0:64, :], [pk], [f"qm{pp}0"], eng="dve")
                        cp(qmb[pp][1][64:128, tsl], pb[64:128, :], [pk], [f"qm{pp}1"], eng="dve")
                    elif which == "k":
                        proj_F(pb[:], pk, gn[f"sbk{cn}"], tq)
                        cp(kcb[pp][:, tsl], pb[:], [pk], [f"kc_{pp}"], eng="dve")
                    else:
                        v_unit(cn, gn, tq)

                for tq in range(4):
                    proj_unit(0, nxt, tq, "q")
                    proj_unit(0, nxt, tq, "k")
                for t in range(NT):
                    v_unit(0, nxt, t)
                for c in range(4):
                    g = nxt
                    pending = []
                    if c < 3:
                        nxt = load_group(l, [f"sbq{c + 1}", f"sbk{c + 1}", f"sbz{c + 1}", f"sbv{c + 1}"])
                        pending = [(c + 1, nxt, tq, w_) for tq in range(4) for w_ in "qk"]
                        pending += [(c + 1, nxt, t, "v") for t in range(NT)]
                    else:
                        pref["3a"] = load_group(l, ["kc", "vc"])
                    qm = qmb[c % 2]
                    kc_ = kcb[c % 2]
                    qmk = [f"qm{c % 2}0", f"qm{c % 2}1"]
                    kck = f"kc_{c % 2}"
                    tiles = []
                    for Q in range(4):
                        nkb = 4 * Q + 4
                        for idx in range(nkb):
                            for hh in range(2):
                                kb = nkb - 1 - idx
                                tiles.append(dict(Q=Q, hh=hh, kb=kb, first=(idx == 0), last=(kb == 0),
                                                  c0=(128 * (kb - 4 * Q) if kb >= 4 * Q else 0),
                                                  diag=(kb >= 4 * Q)))
                    ntl = len(tiles)
                    accb = lambda Q: (PS[4 + (qi + Q) % 2], PSK[4 + (qi + Q) % 2])

                    def zgate(Q):
                        z = zr[(qi + Q) % 2]
                        zk = ("zr", (qi + Q) % 2)
                        proj_F(PS[6][:], PSK[6], g[f"sbz{c}"], Q)
                        act(z[:], PS[6][:], AF.Exp, [PSK[6]], [zk], scale=-1.0)
                        act(z[:], z[:], AF.Ln, [zk], [zk], bias=1.0)
                        act(z[:], z[:], AF.Exp, [zk], [zk], scale=-1.0)
                        tt(z[:], PS[6][:], z[:], ALU.mult, [PSK[6], zk], [zk])

                    def P1(j):
                        t_ = tiles[j]
                        Q, hh, kb, c0 = t_["Q"], t_["hh"], t_["kb"], t_["c0"]
                        zs, zsk = PS[j % 4], PSK[j % 4]
                        if t_["first"] and hh == 0:
                            zgate(Q)
                        cols = slice(Q * 512 + c0, (Q + 1) * 512)
                        mm(zs[:, c0:512], kc_[:, kb * 128:(kb + 1) * 128], qm[hh][:, cols], True, not t_["diag"],
                           [kck, qmk[hh]], [zsk], inc=True)
                        if t_["diag"]:
                            mm(zs[:, c0:512], ident_bf, big[:, 384:896 - c0], False, True, ["cmat", "big"], [zsk])

                    def A1a(j):
                        t_ = tiles[j]
                        c0 = t_["c0"]
                        zs, zsk = PS[j % 4], PSK[j % 4]
                        e, ek = e_t[j % 2], ("e_t", j % 2)
                        act(e[:, c0:512], zs[:, c0:512], AF.Exp, [zsk], [ek], scale=0.125)

                    def A1b(j):
                        t_ = tiles[j]
                        c0 = t_["c0"]
                        e, ek = e_t[j % 2], ("e_t", j % 2)
                        sp, spk = sp_t[j % NB], ("sp_t", j % NB)
                        act(sp[:, c0:512], e[:, c0:512], AF.Ln, [ek], [spk], bias=1.0)

                    def P2(j):
                        t_ = tiles[j]
                        c0, hh = t_["c0"], t_["hh"]
                        zs, zsk = PS[j % 4], PSK[j % 4]
                        sp, spk = sp_t[j % NB], ("sp_t", j % NB)
                        lk = ("lacc", hh)
                        if t_["first"]:
                            T.op("dve", lambda: nc.vector.memset(lacc[hh][:], 0.0), [], [lk])
                            mm(zs[:, c0:512], tri8_bf, sp[:, c0:512], False, True, ["cmat", spk], [zsk])
                        else:
                            mm(zs[:, c0:512], tri8_bf, sp[:, c0:512], False, False, ["cmat", spk], [zsk], inc=False)
                            mm(zs[:, c0:512], ones8_bf, lacc[hh][:, c0:512], False, True, ["cmat", lk], [zsk])
                        if not t_["last"]:
                            tt(lacc[hh][:, c0:512], lacc[hh][:, c0:512], sp[:, c0:512], ALU.add, [lk, spk], [lk])

                    def A2(j):
                        t_ = tiles[j]
                        c0 = t_["c0"]
                        zs, zsk = PS[j % 4], PSK[j % 4]
                        w, wk_ = w_t[j % NB], ("w_t", j % NB)
                        act(w[:, c0:512], zs[:, c0:512], AF.Exp, [zsk], [wk_], scale=0.125)

                    def P3(j):
                        t_ = tiles[j]
                        Q, hh, kb, c0 = t_["Q"], t_["hh"], t_["kb"], t_["c0"]
                        h = 2 * c + hh
                        w, wk_ = w_t[j % NB], ("w_t", j % NB)
                        ab, abk = accb(Q)
                        mm(ab[:, c0:512], sbvm[c % 2][hh][:, kb, :], w[:, c0:512],
                           t_["first"] and hh == 0, t_["last"] and hh == 1, [f"sbvm{c % 2}{hh}", wk_], [abk], inc=True)
                        if t_["last"] and hh == 1:
                            qs = slice(Q * 512, (Q + 1) * 512)
                            if c == 0 and Q == 0:
                                cp(e_t[0][:], ab[:], [abk], [("e_t", 0)])
                                dump(f"sba{l}", e_t[0][:], [("e_t", 0)])
                            tt(aT[:, c, qs], ab[:], zr[(qi + Q) % 2][:], ALU.mult, [abk, ("zr", (qi + Q) % 2)], ["aT"])

                    for j in range(ntl + 4):
                        if j < ntl:
                            P1(j)
                        if 0 <= j - 1 < ntl:
                            A1a(j - 1)
                        if 0 <= j - 3 < ntl:
                            A2(j - 3)
                        if 0 <= j - 1 < ntl:
                            A1b(j - 1)
                        if 0 <= j - 2 < ntl:
                            P2(j - 2)
                        if 0 <= j - 4 < ntl:
                            P3(j - 4)
                        if pending and j >= 4 and (j - 4) % 3 == 0:
                            proj_unit(*pending.pop(0))
                    while pending:
                        proj_unit(*pending.pop(0))
                    qi += 4
                dump(f"aT{l}", aT[:, :, 0:256], ["aT"])
                T.barrier()
            if stop_after == ("s2", l):
                break

            with ExitStack() as s3:
                vc1 = sb(s3, "vc1", [128, 2, 98], BF16)
                kcT = sb(s3, "kcT", [128, 128], BF16)
                sigg = sb(s3, "sigg", [128, NT, 32], F32)
                pb16 = [sb(s3, f"pb16_{i}", [128, 512], BF16) for i in range(3)]
                sml = sb(s3, "sml", [128, 256], F32)
                negsel = sb(s3, "negsel", [128, 2, 32], BF16)
                for g_ in range(2):
                    cp(vc1[:, g_, 64:97], ov1[:, :], ["ov1"], ["vc1"], eng="pool")
                with ExitStack() as s3a:
                    wk = [sb(s3a, f"wk{i}", [128, 512], F32) for i in range(4)]
                    kcr = sb(s3a, "kcr", [128, S], BF16)
                    vcr = sb(s3a, "vcr", [128, S], BF16)
                    w1k = sb(s3a, "w1k", [128, 32, 128], BF16)
                    w1v = sb(s3a, "w1v", [128, 32, 128], BF16)
                    pek = sb(s3a, "pek", [128, 32], BF16)
                    pev = sb(s3a, "pev", [128, 32], BF16)
                    w2k = sb(s3a, "w2k", [128, 128], BF16)
                    w2v = sb(s3a, "w2v", [128, 64], BF16)
                    hid = sb(s3a, "hid", [128, 128], BF16)
                    T.dma("pool", w1k[:], I["w1k"][l].rearrange("p (a b) -> p a b", a=32), writes=["w1k"], **CAST)
                    T.dma("pool", w1v[:], I["w1v"][l].rearrange("p (a b) -> p a b", a=32), writes=["w1v"], **CAST)
                    T.dma("pool", pek[:], I["pek"][l], writes=["pek"], **CAST)
                    T.dma("pool", pev[:], I["pev"][l], writes=["pev"], **CAST)
                    T.dma("pool", w2k[:], I["w2k"][l], writes=["w2k"], **CAST)
                    T.dma("pool", w2v[:], I["w2v"][l], writes=["w2v"], **CAST)
                    g = pref.pop("3a")
                    pref["3b"] = load_group(l, ["nq0", "nqs0"])
                    for tq in range(4):
                        proj_F(PS[0][:], PSK[0], g["kc"], tq)
                        cp(kcr[:, tq * 512:(tq + 1) * 512], PS[0][:], [PSK[0]], ["kcr"], eng="dve")
                        proj_F(PS[1][:], PSK[1], g["vc"], tq)
                        cp(vcr[:, tq * 512:(tq + 1) * 512], PS[1][:], [PSK[1]], ["vcr"], eng="act")
                    for kv, raw, rawk, w1, w1key, pe_, pekey in (("k", kcr, "kcr", w1k, "w1k", pek, "pek"),
                                                                 ("v", vcr, "vcr", w1v, "w1v", pev, "pev")):
                        for g_ in range(2):
                            po = slice(64 * g_, 64 * g_ + 64)
                            pre = PS[2][:, 0:N_CMP]
                            for li in range(32):
                                mm(pre, w1[po, li, :], raw[po, li:li + 16 * (N_CMP - 1) + 1:16], li == 0, False,
                                   [w1key, rawk], [PSK[2]], inc=False)
                                mm(pre, w1[po, li, :], bccol(pe_[po, li:li + 1], N_CMP),
                                   False, li == 31, [w1key, pekey], [PSK[2]], inc=(li == 31))
                            e0 = wk[0][:, 0:N_CMP]
                            act(e0, pre, AF.Exp, [PSK[2]], [("wk", 0)], scale=-1.0)
                            ts(e0, e0, 1.0, None, ALU.add, None, [("wk", 0)], [("wk", 0)])
                            recip(e0, e0, [("wk", 0)], [("wk", 0)])
                            tt(hid[:, 0:N_CMP], pre, e0, ALU.mult, [PSK[2], ("wk", 0)], ["hid"])
                            if kv == "k":
                                pk_, pk2 = PS[3][po, 0:N_CMP], PS[3][po, 128:128 + N_CMP]
                                mm(pk_, w2k[:, 0:64], hid[:, 0:N_CMP], True, True, ["w2k", "hid"], [PSK[3]])
                                mm(pk2, w2k[:, 64:128], hid[:, 0:N_CMP], True, True, ["w2k", "hid"], [PSK[3]])
                                sq = pb16[0][po, 0:N_CMP]
                                act(sq, pk_, AF.Square, [PSK[3]], [("pb16", 0)])
                                ssq = PS[5][po, 0:N_CMP]
                                mm(ssq, ones_bf[po, 0:64], sq, True, True, ["cmat", ("pb16", 0)], [PSK[5]])
                                lv = wk[1][po, 0:N_CMP]
                                act(lv, ssq, AF.Ln, [PSK[5]], [("wk", 1)], scale=1.0 / 64, bias=EPS)
                                act(lv, lv, AF.Exp, [("wk", 1)], [("wk", 1)], scale=-0.5)
                                kn = wk[2][po, 0:N_CMP]
                                kns = wk[3][po, 0:N_CMP]
                                stt(kn, pk_, gvec[po, 6:7], lv, ALU.mult, ALU.mult, [PSK[3], ("small_gv", par), ("wk", 1)],
                                    [("wk", 2)])
                                stt(kns, pk2, gvec[po, 7:8], lv, ALU.mult, ALU.mult, [PSK[3], ("small_gv", par), ("wk", 1)],
                                    [("wk", 3)])
                                tt(kn, kn, cosc[po, 0:N_CMP], ALU.mult, [("wk", 2), "cosc"], [("wk", 2)])
                                tt(kns, kns, sinc[po, 0:N_CMP], ALU.mult, [("wk", 3), "sinc"], [("wk", 3)], eng="pool")
                                tt(kcT[po, 0:N_CMP], kn, kns, ALU.add, [("wk", 2), ("wk", 3)], ["kcT"])
                            else:
                                pv_ = PS[3][0:N_CMP, 256:320]
                                mm(pv_, hid[:, 0:N_CMP], w2v[:, :], True, True, ["hid", "w2v"], [PSK[3]])
                                cp(vc1[0:N_CMP, g_, 0:64], pv_, [PSK[3]], ["vc1"])
                    dump(f"kcT{l}", kcT[:], ["kcT"])
                    dump(f"vc1{l}", vc1[:], ["vc1"])
                    T.barrier()
                qT = sb(s3, "qT", [128, 4, S], BF16)
                ksT = sb(s3, "ksT", [128, S], BF16)
                kwT = sb(s3, "kwT", [128, S], BF16)
                v1 = sb(s3, "v1", [128, NT, 2, 2, 66], BF16)
                T.op("pool", lambda: nc.gpsimd.memset(v1[:, :, :, :, 64:66], 1.0), [], ["v1"])
                with ExitStack() as s3b:
                    wk = [sb(s3b, f"wk{i}", [128, 512], F32) for i in range(4)]
                    cos_t = sb(s3b, "cos_t", [128, 512], F32)
                    sin_t = sb(s3b, "sin_t", [128, 512], F32)
                    jobs = [(f"nq{c}", f"nqs{c}", qT[:, c, :], "qT", 0) for c in range(4)]
                    jobs += [("ks", "kss", ksT[:], "ksT", 2), ("kw", "kws", kwT[:], "kwT", 4)]
                    nxt = pref.pop("3b")
                    for ji, (na, nb_, dest, dkey, gi) in enumerate(jobs):
                        g = nxt
                        if ji + 1 < len(jobs):
                            nxt = load_group(l, [jobs[ji + 1][0], jobs[ji + 1][1]])
                        else:
                            nxt = load_group(l, ["vs", "vw", "ng"])
                        for tq in range(4):
                            tsl = slice(tq * 512, (tq + 1) * 512)
                            T.dma("sp", cos_t[:], I["cos"][:, tsl], writes=["cos_t"])
                            T.dma("sp", sin_t[:], I["sin"][:, tsl], writes=["sin_t"])
                            pa, pak = PS[tq % 2], PSK[tq % 2]
                            pbb, pbk = PS[2 + tq % 2], PSK[2 + tq % 2]
                            pq, pqk = PS[4 + tq % 2], PSK[4 + tq % 2]
                            proj_F(pa[:], pak, g[na], tq)
                            proj_F(pbb[:], pbk, g[nb_], tq)
                            sq = pb16[tq % 2]
                            sqk = ("pb16", tq % 2)
                            act(sq[:], pa[:], AF.Square, [pak], [sqk])
                            mm(pq[:], bd_bf, sq[:], True, True, ["cmat", sqk], [pqk])
                            act(wk[0][:], pq[:], AF.Ln, [pqk], [("wk", 0)], scale=1.0 / 64, bias=EPS)
                            act(wk[0][:], wk[0][:], AF.Exp, [("wk", 0)], [("wk", 0)], scale=-0.5)
                            stt(wk[1][:], pa[:], gvec[:, gi:gi + 1], wk[0][:], ALU.mult, ALU.mult,
                                [pak, ("small_gv", par), ("wk", 0)], [("wk", 1)])
                            stt(wk[2][:], pbb[:], gvec[:, gi + 1:gi + 2], wk[0][:], ALU.mult, ALU.mult,
                                [pbk, ("small_gv", par), ("wk", 0)], [("wk", 2)])
                            tt(wk[1][:], wk[1][:], cos_t[:], ALU.mult, [("wk", 1), "cos_t"], [("wk", 1)],
                               eng="pool")
                            tt(wk[2][:], wk[2][:], sin_t[:], ALU.mult, [("wk", 2), "sin_t"], [("wk", 2)],
                               eng="pool")
                            tt(dest[:, tsl], wk[1][:], wk[2][:], ALU.add, [("wk", 1), ("wk", 2)], [dkey])
                    dump(f"qT{l}", qT[:, :, 0:256], ["qT"])
                    dump(f"ksT{l}", ksT[:, 0:512], ["ksT"])
                    g = nxt
                    nzg = load_group(l, [f"nz{j}" for j in range(4)])
                    slot0 = g["vs"][0]
                    for t in range(NT):
                        pb, pk = PS[t % 2], PSK[t % 2]
                        for kc in range(8):
                            mm(pb[:, 0:256], hT[:, kc, t * 128:(t + 1) * 128],
                               wpool[:, slot0:slot0 + 2, kc * 128:(kc + 1) * 128],
                               kc == 0, kc == 7, ["hT", ("w", slot0), ("w", slot0 + 1)], [pk])
                        cp(v1[:, t, :, :, 0:64], pb[:, 0:256].rearrange("p (a b c) -> p a b c", a=2, b=2), [pk], ["v1"],
                           eng="dve" if t % 2 == 0 else "act")
                    ngi = g["ng"]
                    for t in range(NT):
                        for kc in range(8):
                            mm(PS[2][:, t * 32:t * 32 + 24], hT[:, kc, t * 128:(t + 1) * 128], wview(ngi, kc),
                               kc == 0, kc == 7, ["hT", ("w", ngi[0])], [PSK[2]], inc=(kc == 7 and t == NT - 1))
                    sg = sigg[:].rearrange("p a b -> p (a b)")
                    act(sg, PS[2][:], AF.Exp, [PSK[2]], ["sigg"], scale=-1.0)
                    ts(sg, sg, 1.0, None, ALU.add, None, ["sigg"], ["sigg"])
                    recip(sg, sg, ["sigg"], ["sigg"])
                    dump(f"sigg{l}", sigg[:], ["sigg"])
                    dump(f"v1{l}", v1[:, 0:2], ["v1"])
                    T.barrier()
                nzslot = nzg["nz0"][0]
                pref["s4"] = load_group(l, ["wpa0", "wpb0", "ma0", "mb0"])
                with ExitStack() as s3c:
                    szs = [sb(s3c, "sz0", [128, 512], F32)] * 2
                    qmg = [sb(s3c, f"qmg{i}", [128, 4, 128], BF16) for i in range(2)]
                    T.op("pool", lambda: nc.gpsimd.memset(qmg[0][64:128], 0.0), [], [("qmg", 0)])
                    T.op("pool", lambda: nc.gpsimd.memset(qmg[1][0:64], 0.0), [], [("qmg", 1)])
                    negx = sb(s3c, "negx", [128, 32 * 64], BF16)
                    vbias = sb(s3c, "vbias", [128, S], BF16)
                    impm = sb(s3c, "impm", [128, NT, 32], BF16)
                    impa = sb(s3c, "impa", [128, NT, 32], BF16)
                    T.dma("pool", vbias[:], I["vbias"], writes=["vbias"], **CAST)
                    T.dma("pool", impm[:], I["impm"].rearrange("p (a b) -> p a b", a=NT), writes=["impm"], **CAST)
                    T.dma("pool", impa[:], I["impa"].rearrange("p (a b) -> p a b", a=NT), writes=["impa"], **CAST)
                    ocomb = sb(s3c, "ocomb", [128, 512], F32)
                    b16 = sb(s3c, "b16", [128, 512], BF16)
                    tiles = []
                    for tb in range(NT):
                        kbs = [kb for kb in (tb - 2, tb - 1, tb) if kb >= 0]
                        nW = len(kbs)
                        tiles.append(dict(tb=tb, g=0, br="c", kb=0, first=True, last=True))
                        tiles.append(dict(tb=tb, g=0, br="nz", kb=0))
                        for i, kb in enumerate(kbs):
                            tiles.append(dict(tb=tb, g=0, br="w", kb=kb, first=(i == 0), last=(i == nW - 1)))
                        tiles.append(dict(tb=tb, g=1, br="c", kb=0, first=True, last=True))
                        if tb > 0:
                            tiles.append(dict(tb=tb - 1, g=0, br="tr", kb=0))
                        for kb in range(tb + 1):
                            tiles.append(dict(tb=tb, g=0, br="s", kb=kb, first=(kb == 0), last=(kb == tb), tail=(kb == tb)))
                        for i, kb in enumerate(kbs):
                            tiles.append(dict(tb=tb, g=1, br="w", kb=kb, first=(i == 0), last=(i == nW - 1),
                                              expand=(nW > 1 and i == 1)))
                        for kb in range(tb + 1):
                            tiles.append(dict(tb=tb, g=1, br="s", kb=kb, first=(kb == 0), last=(kb == tb), tail=(kb == tb),
                                              expand=(nW == 1 and kb == 0)))
                    ntl = len(tiles)
                    NTR = 3
                    tmpI = sml[:, 128:256].rearrange("p (h j) -> p h j", h=4)
                    tmpO = sb(s3c, "tmpO", [128, 4, 64], F32)

                    def acc_of(tb, g_, br):
                        b_ = {"c": 3, "w": 4 + g_, "s": 6 + g_}[br]
                        return PS[b_][:].rearrange("p (a b) -> p a b", a=4), PSK[b_]

                    def nz_P1(tb, zs, zsk):
                        tbs = slice(tb * 128, (tb + 1) * 128)
                        for kc in range(8):
                            mm(zs[:], hT[:, kc, tbs], wpool[:, nzslot:nzslot + 4, kc * 128:(kc + 1) * 128],
                               kc == 0, kc == 7, ["hT"] + [("w", nzslot + i) for i in range(4)], [zsk])

                    def nz_A1(tb, zs, zsk):
                        sz, szk = szs[0], ("sz", 0)
                        act(sz[:], zs[:], AF.Exp, [zsk], [szk], scale=-1.0)
                        act(sz[:], sz[:], AF.Ln, [szk], [szk], bias=1.0)
                        act(sz[:], sz[:], AF.Exp, [szk], [szk], scale=-1.0)
                        tt(sz[:], zs[:], sz[:], ALU.mult, [zsk, szk], [szk])

                    def expand(tb, g_):
                        nbk = 2 * (tb + 1)
                        cp(negx[:, 0:nbk * 64].rearrange("p (a b) -> p a b", b=64),
                           bcast(negsel[:, g_, 0:nbk], 1, 64), [("negsel", g_)], ["negx"])

                    def P1(j):
                        t_ = tiles[j]
                        tb, g_, br, kb = t_["tb"], t_["g"], t_["br"], t_["kb"]
                        tbs = slice(tb * 128, (tb + 1) * 128)
                        po = slice(64 * g_, 64 * g_ + 64)
                        qg = qmg[g_][:]
                        qk_ = ("qmg", g_)
                        zs, zsk = PS[j % NTR], PSK[j % NTR]
                        if br == "nz":
                            nz_P1(tb, zs, zsk)
                            return
                        if br == "tr":
                            for c in range(4):
                                mm(zs[:, c * 128:(c + 1) * 128], b16[:, c * 128:(c + 1) * 128], ident_bf, True, True,
                                   ["b16", "cmat"], [zsk], inc=(c == 3))
                            return
                        if t_.get("expand"):
                            expand(tb, 1)
                        if br == "c":
                            cp(qmg[g_][po], qT[po, :, tbs], ["qT"], [qk_], eng="pool")
                            mm(zs[0:N_CMP, :], kcT[:, 0:N_CMP], qg, True, False, ["kcT", qk_], [zsk], inc=False)
                            mm(zs[0:N_CMP, :], ident_bf[0:N_CMP, 0:N_CMP], bcast(vbias[0:N_CMP, tbs], 0, 4), False, True,
                               ["cmat", "vbias"], [zsk])
                        elif br == "w":
                            nob = (kb == tb - 1)
                            mm(zs[:], kwT[:, kb * 128:(kb + 1) * 128], qg, True, nob, ["kwT", qk_], [zsk], inc=nob)
                            if kb == tb:
                                mm(zs[:], ident_bf, bcast(big[:, 385:513], 0, 4), False, True, ["cmat", "big"], [zsk])
                            elif kb == tb - 2:
                                mm(zs[:], ident_bf, bcast(w2m_bf, 0, 4), False, True, ["cmat"], [zsk])
                        else:
                            mm(zs[:], ksT[:, kb * 128:(kb + 1) * 128], qg, True, False, ["ksT", qk_], [zsk], inc=False)
                            mm(zs[:], negx[:, 2 * kb * 64:(2 * kb + 2) * 64], bcast(ident_bf, 0, 4), False, kb != tb,
                               ["negx", "cmat"], [zsk], inc=(kb != tb))
                            if kb == tb:
                                mm(zs[:], ident_bf, bcast(big[:, 385:513], 0, 4), False, True, ["cmat", "big"], [zsk])

                    def A1(j):
                        t_ = tiles[j]
                        nr = N_CMP if t_["br"] == "c" else 128
                        zs, zsk = PS[j % NTR], PSK[j % NTR]
                        if t_["br"] == "nz":
                            nz_A1(t_["tb"], zs, zsk)
                            return
                        if t_["br"] == "tr":
                            tbs = slice(t_["tb"] * 128, (t_["tb"] + 1) * 128)
                            cp(bT[:, :, tbs], zs[:].rearrange("p (a b) -> p a b", a=4), [zsk], ["bT"], eng="act")
                            return
                        p, pk = pb16[j % 3], ("pb16", j % 3)
                        act(p[0:nr, :], zs[0:nr, :], AF.Exp, [zsk], [pk], scale=0.125)

                    def P2(j):
                        t_ = tiles[j]
                        tb, g_, br, kb = t_["tb"], t_["g"], t_["br"], t_["kb"]
                        if br in ("nz", "tr"):
                            return
                        p, pk = pb16[j % 3], ("pb16", j % 3)
                        a3, ak = acc_of(tb, g_, br)
                        for h in range(4):
                            st_ = t_["first"] and h == 0
                            sp_ = t_["last"] and h == 3
                            if br == "c":
                                mm(a3[:, h, 0:97], p[0:N_CMP, h * 128:(h + 1) * 128], vc1[0:N_CMP, g_, 0:97],
                                   st_, sp_, [pk, "vc1"], [ak], inc=(h == 3))
                            else:
                                mm(a3[:, h, 0:65], p[:, h * 128:(h + 1) * 128],
                                   v1[:, kb, 0 if br == "s" else 1, g_, 0:65], st_, sp_, [pk, "v1"], [ak], inc=(h == 3))
                        if br == "c":
                            select(tb, g_)
                        if t_.get("tail"):
                            combine(tb, g_)

                    def select(tb, g_):
                        a3, ak = acc_of(tb, g_, "c")
                        rc = sml[:, 16 * g_:16 * g_ + 4]
                        rck = ("sml_rc", g_)
                        ts(rc, a3[:, :, 64], 1e-30, None, ALU.max, None, [ak], [rck])
                        recip(rc, rc, [rck], [rck])
                        imp = sml[:, 32 + 32 * g_:64 + 32 * g_]
                        ik = ("sml_imp", g_)
                        tt(tmpI[:], a3[:, :, 65:97], bcast(rc, 1, 32), ALU.mult, [ak, rck], ["tmpI"])
                        T.op("dve", lambda: nc.vector.tensor_reduce(out=imp, in_=tmpI[:].rearrange("p h j -> p j h"),
                                                                    axis=mybir.AxisListType.X, op=ALU.add),
                             ["tmpI"], [ik])
                        tt(imp, imp, impm[:, tb, :], ALU.mult, [ik, "impm"], [ik])
                        tt(imp, imp, impa[:, tb, :], ALU.add, [ik, "impa"], [ik])
                        top8 = sml[:, 96 + 8 * g_:104 + 8 * g_]
                        tk = ("sml_top", g_)
                        T.op("dve", lambda: nc.vector.max(out=top8, in_=imp), [ik], [tk])
                        ts(negsel[:, g_, :], imp, top8[:, 7:8], NEGB, ALU.is_lt, ALU.mult, [ik, tk], [("negsel", g_)])
                        if g_ == 0:
                            expand(tb, 0)
                        tt(rc, rc, sigg[:, tb, 0 + 4 * g_:4 + 4 * g_], ALU.mult, [rck, "sigg"], [rck])
                        ok = ("ocomb", g_)
                        tt(ocomb[:, g_ * 256:(g_ + 1) * 256].rearrange("p (h d) -> p h d", h=4), a3[:, :, 0:64],
                           bcast(rc, 1, 64), ALU.mult, [ak, rck], [ok])

                    def combine(tb, g_):
                        tbs = slice(tb * 128, (tb + 1) * 128)
                        ok = ("ocomb", g_)
                        ocv = ocomb[:, g_ * 256:(g_ + 1) * 256].rearrange("p (h d) -> p h d", h=4)
                        for br, gi, off in (("w", 16, 4), ("s", 8, 8)):
                            a3, ak = acc_of(tb, g_, br)
                            r_ = sml[:, 16 * g_ + off:16 * g_ + off + 4]
                            rk = ("sml_r" + br, g_)
                            recip(r_, a3[:, :, 64], [ak], [rk])
                            tt(r_, r_, sigg[:, tb, gi + 4 * g_:gi + 4 + 4 * g_], ALU.mult, [rk, "sigg"], [rk])
                            tt(tmpO[:], a3[:, :, 0:64], bcast(r_, 1, 64), ALU.mult, [ak, rk], ["tmpO"])
                            tt(ocv, ocv, tmpO[:], ALU.add, [ok, "tmpO"], [ok])
                        if g_ == 1:
                            if tb == 5:
                                dump(f"ocomb{l}", ocomb[:], [("ocomb", 0), ("ocomb", 1)])
                            tt(b16[:], ocomb[:], szs[0][:], ALU.mult, [("ocomb", 0), ("ocomb", 1), ("sz", 0)],
                               ["b16"])

                    for j in range(ntl + 2):
                        if j < ntl:
                            P1(j)
                        if 0 <= j - 1 < ntl:
                            A1(j - 1)
                        if 0 <= j - 2 < ntl:
                            P2(j - 2)
                    tiles.append(dict(tb=NT - 1, g=0, br="tr", kb=0))
                    P1(ntl)
                    A1(ntl)
                    dump(f"bT{l}", bT[:, :, 0:256], ["bT"])
                    T.barrier()
            if stop_after == ("s3", l):
                break

            with ExitStack() as s4:
                yT = sb(s4, "yT", [128, 8, S], BF16)
                ra = [sb(s4, f"ra{i}", [128, 512], F32) for i in range(2)]
                rb = [sb(s4, f"rb{i}", [128, 512], F32) for i in range(2)]
                nxt = pref.pop("s4")
                it = 0
                for n in range(8):
                    g = nxt
                    if n < 7:
                        nxt = load_group(l, [f"wpa{n + 1}", f"wpb{n + 1}", f"ma{n + 1}", f"mb{n + 1}"])
                    else:
                        nxt = load_group(l, ["wo0", "wo1", "wo2", "wo3"])
                    for tq in range(4):
                        tsl = slice(tq * 512, (tq + 1) * 512)
                        r = it % 2
                        it += 1
                        p_a, p_ak = PS[0 + r], PSK[0 + r]
                        p_b, p_bk = PS[2 + r], PSK[2 + r]
                        p_ma, p_mak = PS[4 + r], PSK[4 + r]
                        p_mb, p_mbk = PS[6 + r], PSK[6 + r]
                        ia, ib = g[f"wpa{n}"], g[f"wpb{n}"]
                        for fc in range(4):
                            mm(p_a[:], wview(ia, fc), aT[:, fc, tsl], fc == 0, fc == 3, [("w", ia[0]), "aT"], [p_ak])
                        for fc in range(4):
                            mm(p_b[:], wview(ib, fc), bT[:, fc, tsl], fc == 0, fc == 3, [("w", ib[0]), "bT"], [p_bk])
                        proj_F(p_ma[:], p_mak, g[f"ma{n}"], tq)
                        proj_F(p_mb[:], p_mbk, g[f"mb{n}"], tq)
                        act(ra[r][:], p_ma[:], AF.Exp, [p_mak], [("ra", r)], scale=-1.0)
                        act(rb[r][:], p_mb[:], AF.Exp, [p_mbk], [("rb", r)], scale=-1.0)
                        act(ra[r][:], ra[r][:], AF.Ln, [("ra", r)], [("ra", r)], bias=1.0)
                        act(rb[r][:], rb[r][:], AF.Ln, [("rb", r)], [("rb", r)], bias=1.0)
                        act(ra[r][:], ra[r][:], AF.Exp, [("ra", r)], [("ra", r)], scale=-1.0)
                        act(rb[r][:], rb[r][:], AF.Exp, [("rb", r)], [("rb", r)], scale=-1.0)
                        tt(ra[r][:], p_a[:], ra[r][:], ALU.mult, [p_ak, ("ra", r)], [("ra", r)])
                        tt(rb[r][:], p_b[:], rb[r][:], ALU.mult, [p_bk, ("rb", r)], [("rb", r)])
                        tt(yT[:, n, tsl], ra[r][:], rb[r][:], ALU.add, [("ra", r), ("rb", r)], ["yT"])
                dump(f"yT{l}", yT[:, :, 0:256], ["yT"])
                if l + 1 < n_layers:
                    emit_mod(l + 1, s4)
                for half in range(2):
                    g = nxt
                    if half == 0:
                        nxt = load_group(l, ["wo4", "wo5", "wo6", "wo7"])
                    for nn in range(4):
                        n = half * 4 + nn
                        info = g[f"wo{n}"]
                        for tq in range(4):
                            tsl = slice(tq * 512, (tq + 1) * 512)
                            r = it % 2
                            it += 1
                            po_, pok = PS[r], PSK[r]
                            for kc in range(8):
                                mm(po_[:], wview(info, kc), yT[:, kc, tsl], kc == 0, kc == 7, [("w", info[0]), "yT"],
                                   [pok])
                            stt(xT[:, n, tsl], po_[:], gatev[:, n:n + 1], xT[:, n, tsl], ALU.mult, ALU.add,
                                [pok, ("gatev", par), "xT"], ["xT"])
                T.barrier()

        if stop_after is None or True:
            with ExitStack() as pe_:
                xo = [sb(pe_, f"xo{i}", [128, D], F32) for i in range(2)]
                for t in range(NT):
                    xi = xo[t % 2]
                    xk = ("xo", t % 2)
                    for half in range(2):
                        pb = PS[(2 * t + half) % 4]
                        pk = PSK[(2 * t + half) % 4]
                        for j in range(4):
                            c = half * 4 + j
                            T.op("pe", lambda: nc.tensor.transpose(pb[:, j * 128:(j + 1) * 128],
                                                                   xT[:, c, t * 128:(t + 1) * 128], cmat_f[:]),
                                 ["xT", "cmat_f"], [pk], inc=(j == 3))
                        cp(xi[:, half * 512:(half + 1) * 512], pb[:], [pk], [xk], eng="dve" if half == 0 else "act")
                    T.dma("sp", out_d[t * 128:(t + 1) * 128, :], xi[:], reads=[xk])
                T.finish("sp")
        build_program.stats = (T.n_inst, T.n_waits, len(T.sems))
        build_program.stuck = T.check_deadlock()
    return nc, dbg_out


_CACHE = {}


def kernel(**inputs):
    maps = prep_inputs(inputs)
    if "nc" not in _CACHE:
        _CACHE["nc"] = build_program()[0]
    nc = _CACHE["nc"]
    res = run_bass_kernel_spmd(nc, maps, core_ids=list(range(8)))
    out = np.stack([np.asarray(r["out"], dtype=np.float32) for r in res.results], axis=0)
    return out
```

```python
from contextlib import ExitStack
import numpy as np
import concourse.bass as bass
import concourse.mybir as mybir
from concourse.bass_utils import run_bass_kernel_spmd

F32 = mybir.dt.float32
BF16 = mybir.dt.bfloat16
AF = mybir.ActivationFunctionType
ALU = mybir.AluOpType

D = 1024
S = 2048
L_DEPTH = 4
NT = 16
NEGB = -32768.0
EPS = 1e-6
N_CMP = 127

import os as _os
SAME_ENGINE_SYNC = _os.environ.get("NO_SES", "") != "1"
SES_RAW_ONLY = _os.environ.get("NO_SES", "") == "2"


class Tracker:
    def __init__(self, nc, stack):
        self.nc = nc
        self.stack = stack
        self.engs = {"pe": nc.tensor, "act": nc.scalar, "dve": nc.vector,
                     "pool": nc.gpsimd, "sp": nc.sync}
        self.sems = {}
        self.count = {}
        for e in self.engs:
            self.sems[e] = stack.enter_context(nc.semaphore("s_" + e))
            self.count[e] = 0
        self.last_write = {}
        self.reads = {}
        self.seen = {e: {} for e in self.engs}
        self.n_waits = 0
        self.n_inst = 0
        self.log = {e: [] for e in self.engs}
        self._pending_waits = {e: [] for e in self.engs}

    def check_deadlock(self):
        cnt = {k: 0 for k in self.sems}
        pos = {e: 0 for e in self.engs}
        progress = True
        while progress:
            progress = False
            for e in self.engs:
                q = self.log[e]
                while pos[e] < len(q):
                    waits, inc = q[pos[e]]
                    if all(cnt[s_] >= v for s_, v in waits):
                        if inc is not None:
                            cnt[inc[0]] += inc[1]
                        pos[e] += 1
                        progress = True
                    else:
                        break
        stuck = {e: (pos[e], len(self.log[e]), self.log[e][pos[e]][0], {s_: cnt[s_] for s_, _ in self.log[e][pos[e]][0]})
                 for e in self.engs if pos[e] < len(self.log[e])}
        return stuck

    def _dma_src(self, key):
        sid = ("dma", key)
        if sid not in self.sems:
            nm = "d_" + "_".join(str(k) for k in (key if isinstance(key, tuple) else (key,)))
            self.sems[sid] = self.stack.enter_context(self.nc.semaphore(nm[:40]))
            self.count[sid] = 0
        return sid

    def _deps(self, e, reads, writes):
        need = {}

        def add(src, val):
            if src == e and (not SAME_ENGINE_SYNC or e in ("pe", "sp") or val > self.count[e]):
                return
            if need.get(src, 0) < val:
                need[src] = val

        for k in reads:
            if k in self.last_write:
                add(*self.last_write[k])
        for k in writes:
            if k in self.last_write:
                if not (SES_RAW_ONLY and self.last_write[k][0] == e):
                    add(*self.last_write[k])
            for src, val in self.reads.get(k, {}).items():
                if not (SES_RAW_ONLY and src == e):
                    add(src, val)
        out = []
        for src, val in need.items():
            if self.seen[e].get(src, 0) >= val:
                continue
            self.seen[e][src] = val
            out.append((src, val))
        return out

    def _emit_waits(self, e, deps):
        eng = self.engs[e]
        for src, val in deps:
            eng.wait_ge(self.sems[src], val)
            self.n_waits += 1
            self._pending_waits[e].append((src, val))

    def op(self, e, fn, reads=(), writes=(), inc=True):
        deps = self._deps(e, reads, writes)
        self._emit_waits(e, deps)
        ins = fn()
        self.n_inst += 1
        val = self.count[e] + 1
        self.log[e].append((self._pending_waits[e], (e, 1) if inc else None))
        self._pending_waits[e] = []
        if inc:
            ins.then_inc(self.sems[e], 1)
            self.count[e] = val
        for k in reads:
            d = self.reads.setdefault(k, {})
            d[e] = max(d.get(e, 0), val)
        for k in writes:
            self.last_write[k] = (e, val)
            self.reads[k] = {}
        return ins

    def dma(self, e, out, in_, reads=(), writes=(), **kw):
        deps = self._deps(e, reads, writes)
        self._emit_waits(e, deps)
        key = writes[0] if writes else ("rd",) + tuple(reads[:1])
        sid = self._dma_src(key)
        ins = self.engs[e].dma_start(out=out, in_=in_, **kw)
        self.log[e].append((self._pending_waits[e], (sid, 16)))
        self._pending_waits[e] = []
        self.count[sid] += 16
        val = self.count[sid]
        ins.then_inc(self.sems[sid], 16)
        self.n_inst += 1
        for k in reads:
            self.reads.setdefault(k, {})[sid] = val
        for k in writes:
            self.last_write[k] = (sid, val)
            self.reads[k] = {}
        return ins

    def barrier(self):
        for e in self.engs:
            for src, val in self.count.items():
                if src == e or val == 0:
                    continue
                if self.seen[e].get(src, 0) >= val:
                    continue
                self.seen[e][src] = val
                self.engs[e].wait_ge(self.sems[src], val)
                self.n_waits += 1
                self._pending_waits[e].append((src, val))
        self.last_write = {}
        self.reads = {}

    def finish(self, e="sp"):
        for src, val in self.count.items():
            if src == e or val == 0:
                continue
            if self.seen[e].get(src, 0) >= val:
                continue
            self.seen[e][src] = val
            self.engs[e].wait_ge(self.sems[src], val)


def bccol(ap, n):
    dims = [list(d) for d in ap.ap]
    dims[-1] = [0, n]
    return bass.AP(tensor=ap.tensor, offset=ap.offset, ap=dims)


def bcast(ap, pos, n):
    dims = [list(d) for d in ap.ap]
    dims.insert(1 + pos, [0, n])
    return bass.AP(tensor=ap.tensor, offset=ap.offset, ap=dims)


SWAP64 = np.concatenate([np.arange(32, 64), np.arange(0, 32)])


def _w_blocks():
    blks = []
    a = np.arange

    def add(name, cols):
        blks.append((name, "w_in", np.asarray(cols), 8))

    for j in range(4):
        add(f"sbv{j}", 1024 + j * 128 + a(128))
    for c in range(4):
        add(f"sbq{c}", 0 + c * 128 + a(128))
        add(f"sbk{c}", 512 + c * 128 + a(128))
        add(f"sbz{c}", 1536 + c * 128 + a(128))
    add("kc", 2560 + a(128))
    add("vc", 2688 + a(128))
    for c in range(4):
        hs = (c, c + 4)
        add(f"nq{c}", np.concatenate([2048 + h * 64 + a(64) for h in hs]))
        add(f"nqs{c}", np.concatenate([2048 + h * 64 + SWAP64 for h in hs]))
    add("ks", 2816 + a(128))
    add("kss", np.concatenate([2816 + g * 64 + SWAP64 for g in range(2)]))
    add("kw", 3072 + a(128))
    add("kws", np.concatenate([3072 + g * 64 + SWAP64 for g in range(2)]))
    add("vs", 2944 + a(128))
    add("vw", 3200 + a(128))
    add("ng", 3840 + a(24))
    for j in range(4):
        add(f"nz{j}", 3328 + j * 128 + a(128))
    for n in range(8):
        blks.append((f"wpa{n}", "w_proj_a", n * 128 + a(128), 4))
        blks.append((f"wpb{n}", "w_proj_b", n * 128 + a(128), 4))
        add(f"ma{n}", 3864 + n * 128 + a(128))
        add(f"mb{n}", 4888 + n * 128 + a(128))
    for n in range(8):
        blks.append((f"wo{n}", "w_out", n * 128 + a(128), 8))
    return blks


W_BLOCKS = _w_blocks()
W_OFF = {}
_off = 0
for _name, _src, _cols, _nk in W_BLOCKS:
    W_OFF[_name] = (_off, _nk, len(_cols))
    _off += _nk * len(_cols)
W_TOT = _off


def _consts():
    c = {}
    half = 32
    freq = (np.float32(10000.0) ** (-np.arange(half, dtype=np.float32) / np.float32(half))).astype(np.float32)
    pos = np.arange(S, dtype=np.float32)
    ang = (pos[:, None] * freq[None, :]).astype(np.float32)
    cos = np.cos(ang).astype(np.float32).T
    sin = np.sin(ang).astype(np.float32).T
    p = np.arange(128)
    sign = np.where((p % 64) < 32, -1.0, 1.0).astype(np.float32)[:, None]
    cos2 = cos[p % 32]
    sin2 = sin[p % 32] * sign
    c["cos"] = np.ascontiguousarray(cos2)
    c["sin"] = np.ascontiguousarray(sin2)
    posc = 16 * np.arange(N_CMP) + 31
    cc = np.zeros((128, 128), np.float32)
    sc = np.zeros((128, 128), np.float32)
    cc[:, :N_CMP] = cos2[:, posc]
    sc[:, :N_CMP] = sin2[:, posc]
    c["cosc"] = cc
    c["sinc"] = sc
    ident = np.eye(128, dtype=np.float32)
    ones = np.ones((128, 128), np.float32)
    jj, ss = np.meshgrid(np.arange(128), np.arange(128), indexing="ij")
    tri = (jj >= ss).astype(np.float32)
    bd = ((jj // 64) == (ss // 64)).astype(np.float32)
    w2m = np.where(jj > ss, 0.0, NEGB).astype(np.float32)
    c["cmat"] = np.concatenate([ident, ones, tri, bd, w2m, -8.0 * tri, -8.0 * ones], axis=1)
    u = np.arange(896)[None, :]
    s_ = np.arange(128)[:, None]
    c["big"] = np.where(s_ < u - 384, 0.0, NEGB).astype(np.float32)
    n_ = np.arange(128)[:, None]
    t_ = np.arange(S)[None, :]
    c["vbias"] = np.where((16 * n_ + 31 <= t_) & (n_ < N_CMP), 0.0, NEGB).astype(np.float32)
    ci = np.arange(128)[:, None]
    sj = np.arange(32)[None, :]
    ov = ((16 * ci < 64 * (sj + 1)) & (16 * ci + 32 > 64 * sj) & (ci < N_CMP)).astype(np.float32)
    c["ov1"] = np.concatenate([np.ones((128, 1), np.float32), ov], axis=1)
    t = np.arange(S)
    cur = (t // 64)[:, None]
    jb = np.arange(32)[None, :]
    m1 = np.ones((S, 32), np.float32)
    ad = np.zeros((S, 32), np.float32)
    fut = jb > cur
    m1[fut] = 0.0
    ad[fut] = -1e30
    frc = ((jb == 0) | (jb == cur - 1)) & ~fut
    m1[frc] = 0.0
    ad[frc] = 1e4
    cu = jb == cur
    m1[cu] = 0.0
    ad[cu] = 2e4
    c["impm"] = np.ascontiguousarray(m1.reshape(NT, 128, 32).transpose(1, 0, 2)).reshape(128, NT * 32)
    c["impa"] = np.ascontiguousarray(ad.reshape(NT, 128, 32).transpose(1, 0, 2)).reshape(128, NT * 32)
    return c


def prep_inputs(inp):
    f = lambda a: np.ascontiguousarray(np.asarray(a, dtype=np.float32))
    L = L_DEPTH
    shared = {}
    wst = np.empty((L, 128, W_TOT), np.float32)
    srcs = {k: f(inp[k]) for k in ("w_in", "w_proj_a", "w_proj_b", "w_out")}
    for name, src, cols, nk in W_BLOCKS:
        off, _, nc_ = W_OFF[name]
        w = srcs[src][:, :, cols]
        w = w.reshape(L, nk, 128, nc_).transpose(0, 2, 1, 3).reshape(L, 128, nk * nc_)
        wst[:, :, off:off + nk * nc_] = w
    shared["wst"] = wst
    ada_w = f(inp["ada_w"])
    shared["ada_w"] = np.ascontiguousarray(
        ada_w.reshape(L, 8, 128, 6, 512).transpose(0, 3, 2, 1, 4).reshape(L, 6, 128, 8 * 512))
    shared["ada_b"] = np.ascontiguousarray(f(inp["ada_b"]).reshape(L, 24, 128).transpose(0, 2, 1))
    shared["norm_g"] = np.ascontiguousarray(f(inp["norm_g"]).reshape(L, 8, 128).transpose(0, 2, 1))
    p = np.arange(128)
    gv = np.zeros((L, 128, 8), np.float32)
    for i, k in enumerate(("q_norm_g", "ks_norm_g", "kw_norm_g", "kc_norm_g")):
        g = f(inp[k])
        gv[:, :, 2 * i] = g[:, p % 64]
        gv[:, :, 2 * i + 1] = g[:, SWAP64[p % 64]]
    shared["gvec"] = gv
    for nm in ("k", "v"):
        w1 = f(inp[f"cmp_w1_{nm}"])
        w1 = w1.reshape(L, 32, 64, 128).transpose(0, 2, 1, 3).reshape(L, 64, 32 * 128)
        shared[f"w1{nm}"] = np.ascontiguousarray(np.concatenate([w1, w1], axis=1))
        pe = f(inp[f"cmp_pe_{nm}"]).transpose(0, 2, 1)
        shared[f"pe{nm}"] = np.ascontiguousarray(np.concatenate([pe, pe], axis=1))
    w2k = f(inp["cmp_w2_k"])
    shared["w2k"] = np.ascontiguousarray(np.concatenate([w2k, w2k[:, :, SWAP64]], axis=2))
    shared["w2v"] = f(inp["cmp_w2_v"])
    shared.update(_consts())
    x = f(inp["x"])
    c = f(inp["c"])
    maps = []
    for b in range(8):
        m = dict(shared)
        m["x"] = x[b]
        m["c"] = np.ascontiguousarray(c[b].reshape(8, 128).T)
        maps.append(m)
    return maps


IN_SHAPES = {
    "x": [S, D], "c": [128, 8], "wst": [L_DEPTH, 128, W_TOT], "ada_w": [L_DEPTH, 6, 128, 4096],
    "ada_b": [L_DEPTH, 128, 24], "norm_g": [L_DEPTH, 128, 8], "gvec": [L_DEPTH, 128, 8],
    "w1k": [L_DEPTH, 128, 4096], "w1v": [L_DEPTH, 128, 4096], "pek": [L_DEPTH, 128, 32],
    "pev": [L_DEPTH, 128, 32], "w2k": [L_DEPTH, 128, 128], "w2v": [L_DEPTH, 128, 64],
    "cos": [128, S], "sin": [128, S], "cosc": [128, 128], "sinc": [128, 128],
    "cmat": [128, 896], "big": [128, 896], "vbias": [128, S], "ov1": [128, 33],
    "impm": [128, NT * 32], "impa": [128, NT * 32],
}


def build_program(n_layers=L_DEPTH, debug=(), stop_after=None):
    nc = bass.Bass("TRN2", target_bir_lowering=False)
    I = {k: nc.dram_tensor(k, shp, F32, kind="ExternalInput").ap() for k, shp in IN_SHAPES.items()}
    out_d = nc.dram_tensor("out", [S, D], F32, kind="ExternalOutput").ap()
    dbg_out = {}

    with ExitStack() as st:
        T = Tracker(nc, st)

        uid = [0]

        def sb(stack, name, shape, dt):
            uid[0] += 1
            return stack.enter_context(nc.sbuf_tensor(f"sb{uid[0]}_{name}", shape, dt))

        PS = [st.enter_context(nc.psum_tensor(f"ps{i}", [128, 512], F32)) for i in range(8)]
        PSK = [("ps", i) for i in range(8)]

        def mm(out, lhsT, rhs, start, stop, reads, writes, inc=None):
            if inc is None:
                inc = stop
            return T.op("pe", lambda: nc.tensor.matmul(out, lhsT=lhsT, rhs=rhs, start=start, stop=stop,
                                                       skip_group_check=True),
                        reads, writes, inc=inc)

        def act(out, in_, func, reads, writes, **kw):
            return T.op("act", lambda: nc.scalar.activation(out, in_, func, **kw), reads, writes)

        def E(eng):
            return nc.vector if eng == "dve" else nc.gpsimd

        def tt(out, in0, in1, op, reads, writes, eng="dve"):
            return T.op(eng, lambda: E(eng).tensor_tensor(out, in0, in1, op), reads, writes)

        def ts(out, in0, s1, s2, op0, op1, reads, writes, eng="dve"):
            if s2 is None:
                return T.op(eng, lambda: E(eng).tensor_scalar(out, in0, s1, None, op0), reads, writes)
            return T.op(eng, lambda: E(eng).tensor_scalar(out, in0, s1, s2, op0, op1), reads, writes)

        def stt(out, in0, scalar, in1, op0, op1, reads, writes):
            return T.op("dve", lambda: nc.vector.scalar_tensor_tensor(out, in0, scalar, in1, op0, op1), reads, writes)

        def cp(out, in_, reads, writes, eng="dve"):
            if eng == "act":
                return T.op("act", lambda: nc.scalar.copy(out, in_), reads, writes)
            return T.op(eng, lambda: E(eng).tensor_copy(out, in_), reads, writes)

        def recip(out, in_, reads, writes):
            return T.op("dve", lambda: nc.vector.reciprocal(out, in_), reads, writes)

        def dump(name, ap, reads):
            if name not in debug:
                return
            shp = list(ap.shape)
            d = nc.dram_tensor("dbg_" + name, shp, ap.dtype, kind="ExternalOutput").ap()
            dbg_out[name] = d
            T.dma("sp", d, ap, reads=reads)

        xT = sb(st, "xT", [128, 8, S], F32)
        hT = sb(st, "hT", [128, 8, S], BF16)
        aT = sb(st, "aT", [128, 4, S], BF16)
        bT = sb(st, "bT", [128, 4, S], BF16)
        NSLOT = 8
        wpool = sb(st, "wpool", [128, NSLOT, 1024], BF16)
        cmat_f = sb(st, "cmat_f", [128, 128], F32)
        cmat = sb(st, "cmat", [128, 896], BF16)
        big = sb(st, "big", [128, 896], BF16)
        ov1 = sb(st, "ov1", [128, 33], BF16)
        cosc = sb(st, "cosc", [128, 128], F32)
        sinc = sb(st, "sinc", [128, 128], F32)
        c_sb = sb(st, "c_sb", [128, 8], F32)
        siluc = sb(st, "siluc", [128, 8, 2], F32)
        small = sb(st, "small", [128, 128], F32)
        siluc_bf = sb(st, "siluc_bf", [128, 8, 2], BF16)
        ident_bf = cmat[:, 0:128]
        ones_bf = cmat[:, 128:256]
        tri_bf = cmat[:, 256:384]
        bd_bf = cmat[:, 384:512]
        w2m_bf = cmat[:, 512:640]
        tri8_bf = cmat[:, 640:768]
        ones8_bf = cmat[:, 768:896]
        def smv(par):
            o = 64 * par
            return (small[:, o:o + 8], small[:, o + 8:o + 16], small[:, o + 16:o + 24], small[:, o + 24:o + 32],
                    small[:, o + 32:o + 40], small[:, o + 40:o + 64])

        def emit_mod(lm, stack):
            par = lm % 2
            gsv, shiftv, gatev, normg, gvec, modT = smv(par)
            adab = [sb(stack, f"adab{i}", [128, 8, 512], BF16) for i in range(2)]
            adab_b = sb(stack, "adab_b", [128, 24], F32)
            T.dma("sp", adab_b[:], I["ada_b"][lm], writes=["adab_b"])
            T.dma("sp", normg, I["norm_g"][lm], writes=[("small_ng", par)])
            T.dma("sp", gvec, I["gvec"][lm], writes=[("small_gv", par)])
            for nb in range(6):
                ab = adab[nb % 2]
                ak = ("adab", nb % 2)
                T.dma("pool", ab[:], I["ada_w"][lm, nb].rearrange("p (a b) -> p a b", a=8), writes=[ak], **CAST)
                for jj in range(4):
                    j = nb * 4 + jj
                    for kc in range(8):
                        mm(PS[7][:, 2 * j:2 * j + 2], ab[:, kc, jj * 128:(jj + 1) * 128], siluc_bf[:, kc, :],
                           kc == 0, kc == 7, [ak, "siluc_bf"], [PSK[7]])
            tt(modT, PS[7][:, 0:48:2], adab_b[:], ALU.add, [PSK[7], "adab_b"], [("modT", par)])
            stt(gsv, modT[:, 8:16], 1.0, normg, ALU.add, ALU.mult, [("modT", par), ("small_ng", par)], [("gsv", par)])
            cp(shiftv, modT[:, 0:8], [("modT", par)], [("shiftv", par)])
            cp(gatev, modT[:, 16:24], [("modT", par)], [("gatev", par)])
            dump(f"mod{lm}", modT, [("modT", par)])

        CAST = dict(max_dma_last_dim=4096)
        T.dma("sp", cmat_f[:], I["cmat"][:, 0:128], writes=["cmat_f"])
        T.dma("pool", cmat[:], I["cmat"], writes=["cmat"], **CAST)
        T.dma("pool", big[:], I["big"], writes=["big"], **CAST)
        T.dma("pool", ov1[:], I["ov1"], writes=["ov1"], **CAST)
        T.dma("sp", cosc[:], I["cosc"], writes=["cosc"])
        T.dma("sp", sinc[:], I["sinc"], writes=["sinc"])
        T.dma("sp", c_sb[:], I["c"], writes=["c_sb"])

        wstate = {"half": 0}

        def load_group(l, names):
            half = wstate["half"]
            wstate["half"] ^= 1
            res = {}
            for i, nm in enumerate(names):
                off, nk, ncol = W_OFF[nm]
                slot = half * 4 + i
                T.dma("pool", wpool[:, slot, 0:nk * ncol], I["wst"][l, :, off:off + nk * ncol],
                      writes=[("w", slot)], **CAST)
                res[nm] = (slot, nk, ncol)
            return res

        pref = {}

        def wview(info, kc):
            slot, nk, ncol = info
            return wpool[:, slot, kc * ncol:(kc + 1) * ncol]

        def proj_F(psum_ap, pkey, info, tq, extra_reads=()):
            slot, nk, ncol = info
            for kc in range(8):
                mm(psum_ap, wview(info, kc), hT[:, kc, tq * 512:(tq + 1) * 512], kc == 0, kc == 7,
                   [("w", slot), "hT"] + list(extra_reads), [pkey])

        with ExitStack() as ps_:
            xin = [sb(ps_, f"xin{i}", [128, D], F32) for i in range(2)]
            for t in range(NT):
                xi = xin[t % 2]
                xk = ("xin", t % 2)
                T.dma("sp", xi[:], I["x"][t * 128:(t + 1) * 128, :], writes=[xk])
                for half in range(2):
                    pb = PS[(2 * t + half) % 4]
                    pk = PSK[(2 * t + half) % 4]
                    for j in range(4):
                        c = half * 4 + j
                        T.op("pe", lambda: nc.tensor.transpose(pb[:, j * 128:(j + 1) * 128],
                                                               xi[:, c * 128:(c + 1) * 128], cmat_f[:]),
                             [xk, "cmat_f"], [pk], inc=(j == 3))
                    cp(xT[:, half * 4:half * 4 + 4, t * 128:(t + 1) * 128],
                       pb[:].rearrange("p (a b) -> p a b", a=4), [pk], ["xT"],
                       eng="dve" if half == 0 else "act")
            act(siluc[:, :, 0], c_sb[:], AF.Exp, ["c_sb"], ["siluc"], scale=-1.0)
            ts(siluc[:, :, 0], siluc[:, :, 0], 1.0, None, ALU.add, None, ["siluc"], ["siluc"])
            recip(siluc[:, :, 0], siluc[:, :, 0], ["siluc"], ["siluc"])
            tt(siluc[:, :, 0], siluc[:, :, 0], c_sb[:], ALU.mult, ["siluc", "c_sb"], ["siluc"])
            cp(siluc[:, :, 1], siluc[:, :, 0], ["siluc"], ["siluc"])
            cp(siluc_bf[:], siluc[:], ["siluc"], ["siluc_bf"])
            if stop_after != ("pro", 0):
                emit_mod(0, ps_)
            T.barrier()

        for l in range(n_layers if stop_after != ("pro", 0) else 0):
            with ExitStack() as s1:
                par = l % 2
                pref["s2"] = load_group(l, ["sbq0", "sbk0", "sbz0", "sbv0"])
                gsv, shiftv, gatev, normg, gvec, modT = smv(par)
                sqt = [sb(s1, f"sqt{i}", [128, 512], BF16) for i in range(2)]
                lnv = sb(s1, "lnv", [128, 512], F32)
                rstd = sb(s1, "rstd", [128, 512], F32)
                tmpf = [sb(s1, f"tmpf{i}", [128, 512], F32) for i in range(2)]
                for tq in range(4 if stop_after != ("s1a", l) else 0):
                    tsl = slice(tq * 512, (tq + 1) * 512)
                    for c in range(8):
                        sq = sqt[c % 2]
                        act(sq[:], xT[:, c, tsl], AF.Square, ["xT"], [("sqt", c % 2)])
                        mm(PS[1][:], ones_bf, sq[:], c == 0, c == 7, [("sqt", c % 2), "cmat"], [PSK[1]], inc=True)
                    act(lnv[:], PS[1][:], AF.Ln, [PSK[1]], ["lnv"], scale=1.0 / D, bias=EPS)
                    act(rstd[:], lnv[:], AF.Exp, ["lnv"], ["rstd"], scale=-0.5)
                    for c in range(8):
                        tf = tmpf[c % 2]
                        stt(tf[:], xT[:, c, tsl], gsv[:, c:c + 1], rstd[:], ALU.mult, ALU.mult,
                            ["xT", ("gsv", par), "rstd"], [("tmpf", c % 2)])
                        act(hT[:, c, tsl], tf[:], AF.Identity, [("tmpf", c % 2), ("shiftv", par)], ["hT"],
                            bias=shiftv[:, c:c + 1], scale=1.0)
                dump(f"hT{l}", hT[:, :, 0:256], ["hT"])
                T.barrier()
            if stop_after in (("s1", l), ("s1a", l)):
                break

            with ExitStack() as s2:
                sbvm = [[sb(s2, f"sbvm{pp}{i}", [128, NT, 128], BF16) for i in range(2)] for pp in range(2)]
                qmb = [[sb(s2, f"qm{pp}{i}", [128, S], BF16) for i in range(2)] for pp in range(2)]
                kcb = [sb(s2, f"kc_{pp}", [128, S], BF16) for pp in range(2)]
                NB = 3
                e_t = [sb(s2, f"e_t{i}", [128, 512], F32) for i in range(2)]
                sp_t = [sb(s2, f"sp_t{i}", [128, 512], BF16) for i in range(NB)]
                w_t = [sb(s2, f"w_t{i}", [128, 512], BF16) for i in range(NB)]
                lacc = [sb(s2, f"lacc{i}", [128, 512], BF16) for i in range(2)]
                zr = [sb(s2, f"zr{i}", [128, 512], F32) for i in range(2)]
                for pp in range(2):
                    T.op("pool", lambda: nc.gpsimd.memset(qmb[pp][0][64:128, :], 0.0), [], [f"qm{pp}0"])
                    T.op("pool", lambda: nc.gpsimd.memset(qmb[pp][1][0:64, :], 0.0), [], [f"qm{pp}1"])
                    T.op("pool", lambda: nc.gpsimd.memset(sbvm[pp][0][:, :, 64:128], 0.0), [], [f"sbvm{pp}0"])
                    T.op("pool", lambda: nc.gpsimd.memset(sbvm[pp][1][:, :, 0:64], 0.0), [], [f"sbvm{pp}1"])
                nxt = pref.pop("s2")
                qi = 0

                ucnt = [0]

                def v_unit(cn, gn, t):
                    pp = cn % 2
                    info = gn[f"sbv{cn}"]
                    ucnt[0] += 1
                    pb, pk = PS[6 + ucnt[0] % 2], PSK[6 + ucnt[0] % 2]
                    for kc in range(8):
                        mm(pb[:, 0:128], hT[:, kc, t * 128:(t + 1) * 128], wview(info, kc), kc == 0, kc == 7,
                           ["hT", ("w", info[0])], [pk])
                    cp(sbvm[pp][0][:, t, 0:64], pb[:, 0:64], [pk], [f"sbvm{pp}0"], eng="dve")
                    cp(sbvm[pp][1][:, t, 64:128], pb[:, 64:128], [pk], [f"sbvm{pp}1"], eng="dve")

                def proj_unit(cn, gn, tq, which):
                    pp = cn % 2
                    tsl = slice(tq * 512, (tq + 1) * 512)
                    ucnt[0] += 1
                    pb, pk = PS[6 + ucnt[0] % 2], PSK[6 + ucnt[0] % 2]
                    if which == "q":
                        proj_F(pb[:], pk, gn[f"sbq{cn}"], tq)
                        cp(qmb[pp][0][0:64, tsl], pb[0:64, :], [pk], [f"qm{pp}0"], eng="dve")
                        cp(qmb[pp][1][64:128, tsl], pb[64:128, :], [pk], [f"qm{pp}1"], eng="dve")
                    elif which == "k":
                        proj_F(pb[:], pk, gn[f"sbk{cn}"], tq)
                        cp(kcb[pp][:, tsl], pb[:], [pk], [f"kc_{pp}"], eng="dve")
                    else:
                        v_unit(cn, gn, tq)

                for tq in range(4):
                    proj_unit(0, nxt, tq, "q")
                    proj_unit(0, nxt, tq, "k")
                for t in range(NT):
                    v_unit(0, nxt, t)
                for c in range(4):
                    g = nxt
                    pending = []
                    if c < 3:
                        nxt = load_group(l, [f"sbq{c + 1}", f"sbk{c + 1}", f"sbz{c + 1}", f"sbv{c + 1}"])
                        pending = [(c + 1, nxt, tq, w_) for tq in range(4) for w_ in "qk"]
                        pending += [(c + 1, nxt, t, "v") for t in range(NT)]
                    else:
                        pref["3a"] = load_group(l, ["kc", "vc"])
                    qm = qmb[c % 2]
                    kc_ = kcb[c % 2]
                    qmk = [f"qm{c % 2}0", f"qm{c % 2}1"]
                    kck = f"kc_{c % 2}"
                    tiles = []
                    for Q in range(4):
                        nkb = 4 * Q + 4
                        for idx in range(nkb):
                            for hh in range(2):
                                kb = nkb - 1 - idx
                                tiles.append(dict(Q=Q, hh=hh, kb=kb, first=(idx == 0), last=(kb == 0),
                                                  c0=(128 * (kb - 4 * Q) if kb >= 4 * Q else 0),
                                                  diag=(kb >= 4 * Q)))
                    ntl = len(tiles)
                    accb = lambda Q: (PS[4 + (qi + Q) % 2], PSK[4 + (qi + Q) % 2])

                    def zgate(Q):
                        z = zr[(qi + Q) % 2]
                        zk = ("zr", (qi + Q) % 2)
                        proj_F(PS[6][:], PSK[6], g[f"sbz{c}"], Q)
                        act(z[:], PS[6][:], AF.Exp, [PSK[6]], [zk], scale=-1.0)
                        act(z[:], z[:], AF.Ln, [zk], [zk], bias=1.0)
                        act(z[:], z[:], AF.Exp, [zk], [zk], scale=-1.0)
                        tt(z[:], PS[6][:], z[:], ALU.mult, [PSK[6], zk], [zk])

                    def P1(j):
                        t_ = tiles[j]
                        Q, hh, kb, c0 = t_["Q"], t_["hh"], t_["kb"], t_["c0"]
                        zs, zsk = PS[j % 4], PSK[j % 4]
                        if t_["first"] and hh == 0:
                            zgate(Q)
                        cols = slice(Q * 512 + c0, (Q + 1) * 512)
                        mm(zs[:, c0:512], kc_[:, kb * 128:(kb + 1) * 128], qm[hh][:, cols], True, not t_["diag"],
                           [kck, qmk[hh]], [zsk], inc=True)
                        if t_["diag"]:
                            mm(zs[:, c0:512], ident_bf, big[:, 384:896 - c0], False, True, ["cmat", "big"], [zsk])

                    def A1a(j):
                        t_ = tiles[j]
                        c0 = t_["c0"]
                        zs, zsk = PS[j % 4], PSK[j % 4]
                        e, ek = e_t[j % 2], ("e_t", j % 2)
                        act(e[:, c0:512], zs[:, c0:512], AF.Exp, [zsk], [ek], scale=0.125)

                    def A1b(j):
                        t_ = tiles[j]
                        c0 = t_["c0"]
                        e, ek = e_t[j % 2], ("e_t", j % 2)
                        sp, spk = sp_t[j % NB], ("sp_t", j % NB)
                        act(sp[:, c0:512], e[:, c0:512], AF.Ln, [ek], [spk], bias=1.0)

                    def P2(j):
                        t_ = tiles[j]
                        c0, hh = t_["c0"], t_["hh"]
                        zs, zsk = PS[j % 4], PSK[j % 4]
                        sp, spk = sp_t[j % NB], ("sp_t", j % NB)
                        lk = ("lacc", hh)
                        if t_["first"]:
                            T.op("dve", lambda: nc.vector.memset(lacc[hh][:], 0.0), [], [lk])
                            mm(zs[:, c0:512], tri8_bf, sp[:, c0:512], False, True, ["cmat", spk], [zsk])
                        else:
                            mm(zs[:, c0:512], tri8_bf, sp[:, c0:512], False, False, ["cmat", spk], [zsk], inc=False)
                            mm(zs[:, c0:512], ones8_bf, lacc[hh][:, c0:512], False, True, ["cmat", lk], [zsk])
                        if not t_["last"]:
                            tt(lacc[hh][:, c0:512], lacc[hh][:, c0:512], sp[:, c0:512], ALU.add, [lk, spk], [lk])

                    def A2(j):
                        t_ = tiles[j]
                        c0 = t_["c0"]
                        zs, zsk = PS[j % 4], PSK[j % 4]
                        w, wk_ = w_t[j % NB], ("w_t", j % NB)
                        act(w[:, c0:512], zs[:, c0:512], AF.Exp, [zsk], [wk_], scale=0.125)

                    def P3(j):
                        t_ = tiles[j]
                        Q, hh, kb, c0 = t_["Q"], t_["hh"], t_["kb"], t_["c0"]
                        h = 2 * c + hh
                        w, wk_ = w_t[j % NB], ("w_t", j % NB)
                        ab, abk = accb(Q)
                        mm(ab[:, c0:512], sbvm[c % 2][hh][:, kb, :], w[:, c0:512],
                           t_["first"] and hh == 0, t_["last"] and hh == 1, [f"sbvm{c % 2}{hh}", wk_], [abk], inc=True)
                        if t_["last"] and hh == 1:
                            qs = slice(Q * 512, (Q + 1) * 512)
                            if c == 0 and Q == 0:
                                cp(e_t[0][:], ab[:], [abk], [("e_t", 0)])
                                dump(f"sba{l}", e_t[0][:], [("e_t", 0)])
                            tt(aT[:, c, qs], ab[:], zr[(qi + Q) % 2][:], ALU.mult, [abk, ("zr", (qi + Q) % 2)], ["aT"])

                    for j in range(ntl + 4):
                        if j < ntl:
                            P1(j)
                        if 0 <= j - 1 < ntl:
                            A1a(j - 1)
                        if 0 <= j - 3 < ntl:
                            A2(j - 3)
                        if 0 <= j - 1 < ntl:
                            A1b(j - 1)
                        if 0 <= j - 2 < ntl:
                            P2(j - 2)
                        if 0 <= j - 4 < ntl:
                            P3(j - 4)
                        if pending and j >= 4 and (j - 4) % 3 == 0:
                            proj_unit(*pending.pop(0))
                    while pending:
                        proj_unit(*pending.pop(0))
                    qi += 4
                dump(f"aT{l}", aT[:, :, 0:256], ["aT"])
                T.barrier()
            if stop_after == ("s2", l):
                break

            with ExitStack() as s3:
                vc1 = sb(s3, "vc1", [128, 2, 98], BF16)
                kcT = sb(s3, "kcT", [128, 128], BF16)
                sigg = sb(s3, "sigg", [128, NT, 32], F32)
                pb16 = [sb(s3, f"pb16_{i}", [128, 512], BF16) for i in range(3)]
                sml = sb(s3, "sml", [128, 256], F32)
                negsel = sb(s3, "negsel", [128, 2, 32], BF16)
                for g_ in range(2):
                    cp(vc1[:, g_, 64:97], ov1[:, :], ["ov1"], ["vc1"], eng="pool")
                with ExitStack() as s3a:
                    wk = [sb(s3a, f"wk{i}", [128, 512], F32) for i in range(4)]
                    kcr = sb(s3a, "kcr", [128, S], BF16)
                    vcr = sb(s3a, "vcr", [128, S], BF16)
                    w1k = sb(s3a, "w1k", [128, 32, 128], BF16)
                    w1v = sb(s3a, "w1v", [128, 32, 128], BF16)
                    pek = sb(s3a, "pek", [128, 32], BF16)
                    pev = sb(s3a, "pev", [128, 32], BF16)
                    w2k = sb(s3a, "w2k", [128, 128], BF16)
                    w2v = sb(s3a, "w2v", [128, 64], BF16)
                    hid = sb(s3a, "hid", [128, 128], BF16)
                    T.dma("pool", w1k[:], I["w1k"][l].rearrange("p (a b) -> p a b", a=32), writes=["w1k"], **CAST)
                    T.dma("pool", w1v[:], I["w1v"][l].rearrange("p (a b) -> p a b", a=32), writes=["w1v"], **CAST)
                    T.dma("pool", pek[:], I["pek"][l], writes=["pek"], **CAST)
                    T.dma("pool", pev[:], I["pev"][l], writes=["pev"], **CAST)
                    T.dma("pool", w2k[:], I["w2k"][l], writes=["w2k"], **CAST)
                    T.dma("pool", w2v[:], I["w2v"][l], writes=["w2v"], **CAST)
                    g = pref.pop("3a")
                    pref["3b"] = load_group(l, ["nq0", "nqs0"])
                    for tq in range(4):
                        proj_F(PS[0][:], PSK[0], g["kc"], tq)
                        cp(kcr[:, tq * 512:(tq + 1) * 512], PS[0][:], [PSK[0]], ["kcr"], eng="dve")
                        proj_F(PS[1][:], PSK[1], g["vc"], tq)
                        cp(vcr[:, tq * 512:(tq + 1) * 512], PS[1][:], [PSK[1]], ["vcr"], eng="act")
                    for kv, raw, rawk, w1, w1key, pe_, pekey in (("k", kcr, "kcr", w1k, "w1k", pek, "pek"),
                                                                 ("v", vcr, "vcr", w1v, "w1v", pev, "pev")):
                        for g_ in range(2):
                            po = slice(64 * g_, 64 * g_ + 64)
                            pre = PS[2][:, 0:N_CMP]
                            for li in range(32):
                                mm(pre, w1[po, li, :], raw[po, li:li + 16 * (N_CMP - 1) + 1:16], li == 0, False,
                                   [w1key, rawk], [PSK[2]], inc=False)
                                mm(pre, w1[po, li, :], bccol(pe_[po, li:li + 1], N_CMP),
                                   False, li == 31, [w1key, pekey], [PSK[2]], inc=(li == 31))
                            e0 = wk[0][:, 0:N_CMP]
                            act(e0, pre, AF.Exp, [PSK[2]], [("wk", 0)], scale=-1.0)
                            ts(e0, e0, 1.0, None, ALU.add, None, [("wk", 0)], [("wk", 0)])
                            recip(e0, e0, [("wk", 0)], [("wk", 0)])
                            tt(hid[:, 0:N_CMP], pre, e0, ALU.mult, [PSK[2], ("wk", 0)], ["hid"])
                            if kv == "k":
                                pk_, pk2 = PS[3][po, 0:N_CMP], PS[3][po, 128:128 + N_CMP]
                                mm(pk_, w2k[:, 0:64], hid[:, 0:N_CMP], True, True, ["w2k", "hid"], [PSK[3]])
                                mm(pk2, w2k[:, 64:128], hid[:, 0:N_CMP], True, True, ["w2k", "hid"], [PSK[3]])
                                sq = pb16[0][po, 0:N_CMP]
                                act(sq, pk_, AF.Square, [PSK[3]], [("pb16", 0)])
                                ssq = PS[5][po, 0:N_CMP]
                                mm(ssq, ones_bf[po, 0:64], sq, True, True, ["cmat", ("pb16", 0)], [PSK[5]])
                                lv = wk[1][po, 0:N_CMP]
                                act(lv, ssq, AF.Ln, [PSK[5]], [("wk", 1)], scale=1.0 / 64, bias=EPS)
                                act(lv, lv, AF.Exp, [("wk", 1)], [("wk", 1)], scale=-0.5)
                                kn = wk[2][po, 0:N_CMP]
                                kns = wk[3][po, 0:N_CMP]
                                stt(kn, pk_, gvec[po, 6:7], lv, ALU.mult, ALU.mult, [PSK[3], ("small_gv", par), ("wk", 1)],
                                    [("wk", 2)])
                                stt(kns, pk2, gvec[po, 7:8], lv, ALU.mult, ALU.mult, [PSK[3], ("small_gv", par), ("wk", 1)],
                                    [("wk", 3)])
                                tt(kn, kn, cosc[po, 0:N_CMP], ALU.mult, [("wk", 2), "cosc"], [("wk", 2)])
                                tt(kns, kns, sinc[po, 0:N_CMP], ALU.mult, [("wk", 3), "sinc"], [("wk", 3)], eng="pool")
                                tt(kcT[po, 0:N_CMP], kn, kns, ALU.add, [("wk", 2), ("wk", 3)], ["kcT"])
                            else:
                                pv_ = PS[3][0:N_CMP, 256:320]
                                mm(pv_, hid[:, 0:N_CMP], w2v[:, :], True, True, ["hid", "w2v"], [PSK[3]])
                                cp(vc1[0:N_CMP, g_, 0:64], pv_, [PSK[3]], ["vc1"])
                    dump(f"kcT{l}", kcT[:], ["kcT"])
                    dump(f"vc1{l}", vc1[:], ["vc1"])
                    T.barrier()
                qT = sb(s3, "qT", [128, 4, S], BF16)
                ksT = sb(s3, "ksT", [128, S], BF16)
                kwT = sb(s3, "kwT", [128, S], BF16)
                v1 = sb(s3, "v1", [128, NT, 2, 2, 66], BF16)
                T.op("pool", lambda: nc.gpsimd.memset(v1[:, :, :, :, 64:66], 1.0), [], ["v1"])
                with ExitStack() as s3b:
                    wk = [sb(s3b, f"wk{i}", [128, 512], F32) for i in range(4)]
                    cos_t = sb(s3b, "cos_t", [128, 512], F32)
                    sin_t = sb(s3b, "sin_t", [128, 512], F32)
                    jobs = [(f"nq{c}", f"nqs{c}", qT[:, c, :], "qT", 0) for c in range(4)]
                    jobs += [("ks", "kss", ksT[:], "ksT", 2), ("kw", "kws", kwT[:], "kwT", 4)]
                    nxt = pref.pop("3b")
                    for ji, (na, nb_, dest, dkey, gi) in enumerate(jobs):
                        g = nxt
                        if ji + 1 < len(jobs):
                            nxt = load_group(l, [jobs[ji + 1][0], jobs[ji + 1][1]])
                        else:
                            nxt = load_group(l, ["vs", "vw", "ng"])
                        for tq in range(4):
                            tsl = slice(tq * 512, (tq + 1) * 512)
                            T.dma("sp", cos_t[:], I["cos"][:, tsl], writes=["cos_t"])
                            T.dma("sp", sin_t[:], I["sin"][:, tsl], writes=["sin_t"])
                            pa, pak = PS[tq % 2], PSK[tq % 2]
                            pbb, pbk = PS[2 + tq % 2], PSK[2 + tq % 2]
                            pq, pqk = PS[4 + tq % 2], PSK[4 + tq % 2]
                            proj_F(pa[:], pak, g[na], tq)
                            proj_F(pbb[:], pbk, g[nb_], tq)
                            sq = pb16[tq % 2]
                            sqk = ("pb16", tq % 2)
                            act(sq[:], pa[:], AF.Square, [pak], [sqk])
                            mm(pq[:], bd_bf, sq[:], True, True, ["cmat", sqk], [pqk])
                            act(wk[0][:], pq[:], AF.Ln, [pqk], [("wk", 0)], scale=1.0 / 64, bias=EPS)
                            act(wk[0][:], wk[0][:], AF.Exp, [("wk", 0)], [("wk", 0)], scale=-0.5)
                            stt(wk[1][:], pa[:], gvec[:, gi:gi + 1], wk[0][:], ALU.mult, ALU.mult,
                                [pak, ("small_gv", par), ("wk", 0)], [("wk", 1)])
                            stt(wk[2][:], pbb[:], gvec[:, gi + 1:gi + 2], wk[0][:], ALU.mult, ALU.mult,
                                [pbk, ("small_gv", par), ("wk", 0)], [("wk", 2)])
                            tt(wk[1][:], wk[1][:], cos_t[:], ALU.mult, [("wk", 1), "cos_t"], [("wk", 1)],
                               eng="pool")
                            tt(wk[2][:], wk[2][:], sin_t[:], ALU.mult, [("wk", 2), "sin_t"], [("wk", 2)],
                               eng="pool")
                            tt(dest[:, tsl], wk[1][:], wk[2][:], ALU.add, [("wk", 1), ("wk", 2)], [dkey])
                    dump(f"qT{l}", qT[:, :, 0:256], ["qT"])
                    dump(f"ksT{l}", ksT[:, 0:512], ["ksT"])
                    g = nxt
                    nzg = load_group(l, [f"nz{j}" for j in range(4)])
                    slot0 = g["vs"][0]
                    for t in range(NT):
                        pb, pk = PS[t % 2], PSK[t % 2]
                        for kc in range(8):
                            mm(pb[:, 0:256], hT[:, kc, t * 128:(t + 1) * 128],
                               wpool[:, slot0:slot0 + 2, kc * 128:(kc + 1) * 128],
                               kc == 0, kc == 7, ["hT", ("w", slot0), ("w", slot0 + 1)], [pk])
                        cp(v1[:, t, :, :, 0:64], pb[:, 0:256].rearrange("p (a b c) -> p a b c", a=2, b=2), [pk], ["v1"],
                           eng="dve" if t % 2 == 0 else "act")
                    ngi = g["ng"]
                    for t in range(NT):
                        for kc in range(8):
                            mm(PS[2][:, t * 32:t * 32 + 24], hT[:, kc, t * 128:(t + 1) * 128], wview(ngi, kc),
                               kc == 0, kc == 7, ["hT", ("w", ngi[0])], [PSK[2]], inc=(kc == 7 and t == NT - 1))
                    sg = sigg[:].rearrange("p a b -> p (a b)")
                    act(sg, PS[2][:], AF.Exp, [PSK[2]], ["sigg"], scale=-1.0)
                    ts(sg, sg, 1.0, None, ALU.add, None, ["sigg"], ["sigg"])
                    recip(sg, sg, ["sigg"], ["sigg"])
                    dump(f"sigg{l}", sigg[:], ["sigg"])
                    dump(f"v1{l}", v1[:, 0:2], ["v1"])
                    T.barrier()
                nzslot = nzg["nz0"][0]
                pref["s4"] = load_group(l, ["wpa0", "wpb0", "ma0", "mb0"])
                with ExitStack() as s3c:
                    szs = [sb(s3c, "sz0", [128, 512], F32)] * 2
                    qmg = [sb(s3c, f"qmg{i}", [128, 4, 128], BF16) for i in range(2)]
                    T.op("pool", lambda: nc.gpsimd.memset(qmg[0][64:128], 0.0), [], [("qmg", 0)])
                    T.op("pool", lambda: nc.gpsimd.memset(qmg[1][0:64], 0.0), [], [("qmg", 1)])
                    negx = sb(s3c, "negx", [128, 32 * 64], BF16)
                    vbias = sb(s3c, "vbias", [128, S], BF16)
                    impm = sb(s3c, "impm", [128, NT, 32], BF16)
                    impa = sb(s3c, "impa", [128, NT, 32], BF16)
                    T.dma("pool", vbias[:], I["vbias"], writes=["vbias"], **CAST)
                    T.dma("pool", impm[:], I["impm"].rearrange("p (a b) -> p a b", a=NT), writes=["impm"], **CAST)
                    T.dma("pool", impa[:], I["impa"].rearrange("p (a b) -> p a b", a=NT), writes=["impa"], **CAST)
                    ocomb = sb(s3c, "ocomb", [128, 512], F32)
                    b16 = sb(s3c, "b16", [128, 512], BF16)
                    tiles = []
                    for tb in range(NT):
                        kbs = [kb for kb in (tb - 2, tb - 1, tb) if kb >= 0]
                        nW = len(kbs)
                        tiles.append(dict(tb=tb, g=0, br="c", kb=0, first=True, last=True))
                        tiles.append(dict(tb=tb, g=0, br="nz", kb=0))
                        for i, kb in enumerate(kbs):
                            tiles.append(dict(tb=tb, g=0, br="w", kb=kb, first=(i == 0), last=(i == nW - 1)))
                        tiles.append(dict(tb=tb, g=1, br="c", kb=0, first=True, last=True))
                        if tb > 0:
                            tiles.append(dict(tb=tb - 1, g=0, br="tr", kb=0))
                        for kb in range(tb + 1):
                            tiles.append(dict(tb=tb, g=0, br="s", kb=kb, first=(kb == 0), last=(kb == tb), tail=(kb == tb)))
                        for i, kb in enumerate(kbs):
                            tiles.append(dict(tb=tb, g=1, br="w", kb=kb, first=(i == 0), last=(i == nW - 1),
                                              expand=(nW > 1 and i == 1)))
                        for kb in range(tb + 1):
                            tiles.append(dict(tb=tb, g=1, br="s", kb=kb, first=(kb == 0), last=(kb == tb), tail=(kb == tb),
                                              expand=(nW == 1 and kb == 0)))
                    ntl = len(tiles)
                    NTR = 3
                    tmpI = sml[:, 128:256].rearrange("p (h j) -> p h j", h=4)
                    tmpO = sb(s3c, "tmpO", [128, 4, 64], F32)

                    def acc_of(tb, g_, br):
                        b_ = {"c": 3, "w": 4 + g_, "s": 6 + g_}[br]
                        return PS[b_][:].rearrange("p (a b) -> p a b", a=4), PSK[b_]

                    def nz_P1(tb, zs, zsk):
                        tbs = slice(tb * 128, (tb + 1) * 128)
                        for kc in range(8):
                            mm(zs[:], hT[:, kc, tbs], wpool[:, nzslot:nzslot + 4, kc * 128:(kc + 1) * 128],
                               kc == 0, kc == 7, ["hT"] + [("w", nzslot + i) for i in range(4)], [zsk])

                    def nz_A1(tb, zs, zsk):
                        sz, szk = szs[0], ("sz", 0)
                        act(sz[:], zs[:], AF.Exp, [zsk], [szk], scale=-1.0)
                        act(sz[:], sz[:], AF.Ln, [szk], [szk], bias=1.0)
                        act(sz[:], sz[:], AF.Exp, [szk], [szk], scale=-1.0)
                        tt(sz[:], zs[:], sz[:], ALU.mult, [zsk, szk], [szk])

                    def expand(tb, g_):
                        nbk = 2 * (tb + 1)
                        cp(negx[:, 0:nbk * 64].rearrange("p (a b) -> p a b", b=64),
                           bcast(negsel[:, g_, 0:nbk], 1, 64), [("negsel", g_)], ["negx"])

                    def P1(j):
                        t_ = tiles[j]
                        tb, g_, br, kb = t_["tb"], t_["g"], t_["br"], t_["kb"]
                        tbs = slice(tb * 128, (tb + 1) * 128)
                        po = slice(64 * g_, 64 * g_ + 64)
                        qg = qmg[g_][:]
                        qk_ = ("qmg", g_)
                        zs, zsk = PS[j % NTR], PSK[j % NTR]
                        if br == "nz":
                            nz_P1(tb, zs, zsk)
                            return
                        if br == "tr":
                            for c in range(4):
                                mm(zs[:, c * 128:(c + 1) * 128], b16[:, c * 128:(c + 1) * 128], ident_bf, True, True,
                                   ["b16", "cmat"], [zsk], inc=(c == 3))
                            return
                        if t_.get("expand"):
                            expand(tb, 1)
                        if br == "c":
                            cp(qmg[g_][po], qT[po, :, tbs], ["qT"], [qk_], eng="pool")
                            mm(zs[0:N_CMP, :], kcT[:, 0:N_CMP], qg, True, False, ["kcT", qk_], [zsk], inc=False)
                            mm(zs[0:N_CMP, :], ident_bf[0:N_CMP, 0:N_CMP], bcast(vbias[0:N_CMP, tbs], 0, 4), False, True,
                               ["cmat", "vbias"], [zsk])
                        elif br == "w":
                            nob = (kb == tb - 1)
                            mm(zs[:], kwT[:, kb * 128:(kb + 1) * 128], qg, True, nob, ["kwT", qk_], [zsk], inc=nob)
                            if kb == tb:
                                mm(zs[:], ident_bf, bcast(big[:, 385:513], 0, 4), False, True, ["cmat", "big"], [zsk])
                            elif kb == tb - 2:
                                mm(zs[:], ident_bf, bcast(w2m_bf, 0, 4), False, True, ["cmat"], [zsk])
                        else:
                            mm(zs[:], ksT[:, kb * 128:(kb + 1) * 128], qg, True, False, ["ksT", qk_], [zsk], inc=False)
                            mm(zs[:], negx[:, 2 * kb * 64:(2 * kb + 2) * 64], bcast(ident_bf, 0, 4), False, kb != tb,
                               ["negx", "cmat"], [zsk], inc=(kb != tb))
                            if kb == tb:
                                mm(zs[:], ident_bf, bcast(big[:, 385:513], 0, 4), False, True, ["cmat", "big"], [zsk])

                    def A1(j):
                        t_ = tiles[j]
                        nr = N_CMP if t_["br"] == "c" else 128
                        zs, zsk = PS[j % NTR], PSK[j % NTR]
                        if t_["br"] == "nz":
                            nz_A1(t_["tb"], zs, zsk)
                            return
                        if t_["br"] == "tr":
                            tbs = slice(t_["tb"] * 128, (t_["tb"] + 1) * 128)
                            cp(bT[:, :, tbs], zs[:].rearrange("p (a b) -> p a b", a=4), [zsk], ["bT"], eng="act")
                            return
                        p, pk = pb16[j % 3], ("pb16", j % 3)
                        act(p[0:nr, :], zs[0:nr, :], AF.Exp, [zsk], [pk], scale=0.125)

                    def P2(j):
                        t_ = tiles[j]
                        tb, g_, br, kb = t_["tb"], t_["g"], t_["br"], t_["kb"]
                        if br in ("nz", "tr"):
                            return
                        p, pk = pb16[j % 3], ("pb16", j % 3)
                        a3, ak = acc_of(tb, g_, br)
                        for h in range(4):
                            st_ = t_["first"] and h == 0
                            sp_ = t_["last"] and h == 3
                            if br == "c":
                                mm(a3[:, h, 0:97], p[0:N_CMP, h * 128:(h + 1) * 128], vc1[0:N_CMP, g_, 0:97],
                                   st_, sp_, [pk, "vc1"], [ak], inc=(h == 3))
                            else:
                                mm(a3[:, h, 0:65], p[:, h * 128:(h + 1) * 128],
                                   v1[:, kb, 0 if br == "s" else 1, g_, 0:65], st_, sp_, [pk, "v1"], [ak], inc=(h == 3))
                        if br == "c":
                            select(tb, g_)
                        if t_.get("tail"):
                            combine(tb, g_)

                    def select(tb, g_):
                        a3, ak = acc_of(tb, g_, "c")
                        rc = sml[:, 16 * g_:16 * g_ + 4]
                        rck = ("sml_rc", g_)
                        ts(rc, a3[:, :, 64], 1e-30, None, ALU.max, None, [ak], [rck])
                        recip(rc, rc, [rck], [rck])
                        imp = sml[:, 32 + 32 * g_:64 + 32 * g_]
                        ik = ("sml_imp", g_)
                        tt(tmpI[:], a3[:, :, 65:97], bcast(rc, 1, 32), ALU.mult, [ak, rck], ["tmpI"])
                        T.op("dve", lambda: nc.vector.tensor_reduce(out=imp, in_=tmpI[:].rearrange("p h j -> p j h"),
                                                                    axis=mybir.AxisListType.X, op=ALU.add),
                             ["tmpI"], [ik])
                        tt(imp, imp, impm[:, tb, :], ALU.mult, [ik, "impm"], [ik])
                        tt(imp, imp, impa[:, tb, :], ALU.add, [ik, "impa"], [ik])
                        top8 = sml[:, 96 + 8 * g_:104 + 8 * g_]
                        tk = ("sml_top", g_)
                        T.op("dve", lambda: nc.vector.max(out=top8, in_=imp), [ik], [tk])
                        ts(negsel[:, g_, :], imp, top8[:, 7:8], NEGB, ALU.is_lt, ALU.mult, [ik, tk], [("negsel", g_)])
                        if g_ == 0:
                            expand(tb, 0)
                        tt(rc, rc, sigg[:, tb, 0 + 4 * g_:4 + 4 * g_], ALU.mult, [rck, "sigg"], [rck])
                        ok = ("ocomb", g_)
                        tt(ocomb[:, g_ * 256:(g_ + 1) * 256].rearrange("p (h d) -> p h d", h=4), a3[:, :, 0:64],
                           bcast(rc, 1, 64), ALU.mult, [ak, rck], [ok])

                    def combine(tb, g_):
                        tbs = slice(tb * 128, (tb + 1) * 128)
                        ok = ("ocomb", g_)
                        ocv = ocomb[:, g_ * 256:(g_ + 1) * 256].rearrange("p (h d) -> p h d", h=4)
                        for br, gi, off in (("w", 16, 4), ("s", 8, 8)):
                            a3, ak = acc_of(tb, g_, br)
                            r_ = sml[:, 16 * g_ + off:16 * g_ + off + 4]
                            rk = ("sml_r" + br, g_)
                            recip(r_, a3[:, :, 64], [ak], [rk])
                            tt(r_, r_, sigg[:, tb, gi + 4 * g_:gi + 4 + 4 * g_], ALU.mult, [rk, "sigg"], [rk])
                            tt(tmpO[:], a3[:, :, 0:64], bcast(r_, 1, 64), ALU.mult, [ak, rk], ["tmpO"])
                            tt(ocv, ocv, tmpO[:], ALU.add, [ok, "tmpO"], [ok])
                        if g_ == 1:
                            if tb == 5:
                                dump(f"ocomb{l}", ocomb[:], [("ocomb", 0), ("ocomb", 1)])
                            tt(b16[:], ocomb[:], szs[0][:], ALU.mult, [("ocomb", 0), ("ocomb", 1), ("sz", 0)],
                               ["b16"])

                    for j in range(ntl + 2):
                        if j < ntl:
                            P1(j)
                        if 0 <= j - 1 < ntl:
                            A1(j - 1)
                        if 0 <= j - 2 < ntl:
                            P2(j - 2)
                    tiles.append(dict(tb=NT - 1, g=0, br="tr", kb=0))
                    P1(ntl)
                    A1(ntl)
                    dump(f"bT{l}", bT[:, :, 0:256], ["bT"])
                    T.barrier()
            if stop_after == ("s3", l):
                break

            with ExitStack() as s4:
                yT = sb(s4, "yT", [128, 8, S], BF16)
                ra = [sb(s4, f"ra{i}", [128, 512], F32) for i in range(2)]
                rb = [sb(s4, f"rb{i}", [128, 512], F32) for i in range(2)]
                nxt = pref.pop("s4")
                it = 0
                for n in range(8):
                    g = nxt
                    if n < 7:
                        nxt = load_group(l, [f"wpa{n + 1}", f"wpb{n + 1}", f"ma{n + 1}", f"mb{n + 1}"])
                    else:
                        nxt = load_group(l, ["wo0", "wo1", "wo2", "wo3"])
                    for tq in range(4):
                        tsl = slice(tq * 512, (tq + 1) * 512)
                        r = it % 2
                        it += 1
                        p_a, p_ak = PS[0 + r], PSK[0 + r]
                        p_b, p_bk = PS[2 + r], PSK[2 + r]
                        p_ma, p_mak = PS[4 + r], PSK[4 + r]
                        p_mb, p_mbk = PS[6 + r], PSK[6 + r]
                        ia, ib = g[f"wpa{n}"], g[f"wpb{n}"]
                        for fc in range(4):
                            mm(p_a[:], wview(ia, fc), aT[:, fc, tsl], fc == 0, fc == 3, [("w", ia[0]), "aT"], [p_ak])
                        for fc in range(4):
                            mm(p_b[:], wview(ib, fc), bT[:, fc, tsl], fc == 0, fc == 3, [("w", ib[0]), "bT"], [p_bk])
                        proj_F(p_ma[:], p_mak, g[f"ma{n}"], tq)
                        proj_F(p_mb[:], p_mbk, g[f"mb{n}"], tq)
                        act(ra[r][:], p_ma[:], AF.Exp, [p_mak], [("ra", r)], scale=-1.0)
                        act(rb[r][:], p_mb[:], AF.Exp, [p_mbk], [("rb", r)], scale=-1.0)
                        act(ra[r][:], ra[r][:], AF.Ln, [("ra", r)], [("ra", r)], bias=1.0)
                        act(rb[r][:], rb[r][:], AF.Ln, [("rb", r)], [("rb", r)], bias=1.0)
                        act(ra[r][:], ra[r][:], AF.Exp, [("ra", r)], [("ra", r)], scale=-1.0)
                        act(rb[r][:], rb[r][:], AF.Exp, [("rb", r)], [("rb", r)], scale=-1.0)
                        tt(ra[r][:], p_a[:], ra[r][:], ALU.mult, [p_ak, ("ra", r)], [("ra", r)])
                        tt(rb[r][:], p_b[:], rb[r][:], ALU.mult, [p_bk, ("rb", r)], [("rb", r)])
                        tt(yT[:, n, tsl], ra[r][:], rb[r][:], ALU.add, [("ra", r), ("rb", r)], ["yT"])
                dump(f"yT{l}", yT[:, :, 0:256], ["yT"])
                if l + 1 < n_layers:
                    emit_mod(l + 1, s4)
                for half in range(2):
                    g = nxt
                    if half == 0:
                        nxt = load_group(l, ["wo4", "wo5", "wo6", "wo7"])
                    for nn in range(4):
                        n = half * 4 + nn
                        info = g[f"wo{n}"]
                        for tq in range(4):
                            tsl = slice(tq * 512, (tq + 1) * 512)
                            r = it % 2
                            it += 1
                            po_, pok = PS[r], PSK[r]
                            for kc in range(8):
                                mm(po_[:], wview(info, kc), yT[:, kc, tsl], kc == 0, kc == 7, [("w", info[0]), "yT"],
                                   [pok])
                            stt(xT[:, n, tsl], po_[:], gatev[:, n:n + 1], xT[:, n, tsl], ALU.mult, ALU.add,
                                [pok, ("gatev", par), "xT"], ["xT"])
                T.barrier()

        if stop_after is None or True:
            with ExitStack() as pe_:
                xo = [sb(pe_, f"xo{i}", [128, D], F32) for i in range(2)]
                for t in range(NT):
                    xi = xo[t % 2]
                    xk = ("xo", t % 2)
                    for half in range(2):
                        pb = PS[(2 * t + half) % 4]
                        pk = PSK[(2 * t + half) % 4]
                        for j in range(4):
                            c = half * 4 + j
                            T.op("pe", lambda: nc.tensor.transpose(pb[:, j * 128:(j + 1) * 128],
                                                                   xT[:, c, t * 128:(t + 1) * 128], cmat_f[:]),
                                 ["xT", "cmat_f"], [pk], inc=(j == 3))
                        cp(xi[:, half * 512:(half + 1) * 512], pb[:], [pk], [xk], eng="dve" if half == 0 else "act")
                    T.dma("sp", out_d[t * 128:(t + 1) * 128, :], xi[:], reads=[xk])
                T.finish("sp")
        build_program.stats = (T.n_inst, T.n_waits, len(T.sems))
        build_program.stuck = T.check_deadlock()
    return nc, dbg_out


_CACHE = {}


def kernel(**inputs):
    maps = prep_inputs(inputs)
    if "nc" not in _CACHE:
        _CACHE["nc"] = build_program()[0]
    nc = _CACHE["nc"]
    res = run_bass_kernel_spmd(nc, maps, core_ids=list(range(8)))
    out = np.stack([np.asarray(r["out"], dtype=np.float32) for r in res.results], axis=0)
    return out
```

```python
from contextlib import ExitStack
import numpy as np
import concourse.bass as bass
import concourse.mybir as mybir
from concourse.bass_utils import run_bass_kernel_spmd

F32 = mybir.dt.float32
BF16 = mybir.dt.bfloat16
AF = mybir.ActivationFunctionType
ALU = mybir.AluOpType

D = 1024
S = 2048
L_DEPTH = 4
NT = 16
NEGB = -32768.0
EPS = 1e-6
N_CMP = 127

import os as _os
SAME_ENGINE_SYNC = _os.environ.get("NO_SES", "") != "1"
SES_RAW_ONLY = _os.environ.get("NO_SES", "") == "2"


class Tracker:
    def __init__(self, nc, stack):
        self.nc = nc
        self.stack = stack
        self.engs = {"pe": nc.tensor, "act": nc.scalar, "dve": nc.vector,
                     "pool": nc.gpsimd, "sp": nc.sync}
        self.sems = {}
        self.count = {}
        for e in self.engs:
            self.sems[e] = stack.enter_context(nc.semaphore("s_" + e))
            self.count[e] = 0
        self.last_write = {}
        self.reads = {}
        self.seen = {e: {} for e in self.engs}
        self.n_waits = 0
        self.n_inst = 0
        self.log = {e: [] for e in self.engs}
        self._pending_waits = {e: [] for e in self.engs}

    def check_deadlock(self):
        cnt = {k: 0 for k in self.sems}
        pos = {e: 0 for e in self.engs}
        progress = True
        while progress:
            progress = False
            for e in self.engs:
                q = self.log[e]
                while pos[e] < len(q):
                    waits, inc = q[pos[e]]
                    if all(cnt[s_] >= v for s_, v in waits):
                        if inc is not None:
                            cnt[inc[0]] += inc[1]
                        pos[e] += 1
                        progress = True
                    else:
                        break
        stuck = {e: (pos[e], len(self.log[e]), self.log[e][pos[e]][0], {s_: cnt[s_] for s_, _ in self.log[e][pos[e]][0]})
                 for e in self.engs if pos[e] < len(self.log[e])}
        return stuck

    def _dma_src(self, key):
        sid = ("dma", key)
        if sid not in self.sems:
            nm = "d_" + "_".join(str(k) for k in (key if isinstance(key, tuple) else (key,)))
            self.sems[sid] = self.stack.enter_context(self.nc.semaphore(nm[:40]))
            self.count[sid] = 0
        return sid

    def _deps(self, e, reads, writes):
        need = {}

        def add(src, val):
            if src == e and (not SAME_ENGINE_SYNC or e in ("pe", "sp") or val > self.count[e]):
                return
            if need.get(src, 0) < val:
                need[src] = val

        for k in reads:
            if k in self.last_write:
                add(*self.last_write[k])
        for k in writes:
            if k in self.last_write:
                if not (SES_RAW_ONLY and self.last_write[k][0] == e):
                    add(*self.last_write[k])
            for src, val in self.reads.get(k, {}).items():
                if not (SES_RAW_ONLY and src == e):
                    add(src, val)
        out = []
        for src, val in need.items():
            if self.seen[e].get(src, 0) >= val:
                continue
            self.seen[e][src] = val
            out.append((src, val))
        return out

    def _emit_waits(self, e, deps):
        eng = self.engs[e]
        for src, val in deps:
            eng.wait_ge(self.sems[src], val)
            self.n_waits += 1
            self._pending_waits[e].append((src, val))

    def op(self, e, fn, reads=(), writes=(), inc=True):
        deps = self._deps(e, reads, writes)
        self._emit_waits(e, deps)
        ins = fn()
        self.n_inst += 1
        val = self.count[e] + 1
        self.log[e].append((self._pending_waits[e], (e, 1) if inc else None))
        self._pending_waits[e] = []
        if inc:
            ins.then_inc(self.sems[e], 1)
            self.count[e] = val
        for k in reads:
            d = self.reads.setdefault(k, {})
            d[e] = max(d.get(e, 0), val)
        for k in writes:
            self.last_write[k] = (e, val)
            self.reads[k] = {}
        return ins

    def dma(self, e, out, in_, reads=(), writes=(), **kw):
        deps = self._deps(e, reads, writes)
        self._emit_waits(e, deps)
        key = writes[0] if writes else ("rd",) + tuple(reads[:1])
        sid = self._dma_src(key)
        ins = self.engs[e].dma_start(out=out, in_=in_, **kw)
        self.log[e].append((self._pending_waits[e], (sid, 16)))
        self._pending_waits[e] = []
        self.count[sid] += 16
        val = self.count[sid]
        ins.then_inc(self.sems[sid], 16)
        self.n_inst += 1
        for k in reads:
            self.reads.setdefault(k, {})[sid] = val
        for k in writes:
            self.last_write[k] = (sid, val)
            self.reads[k] = {}
        return ins

    def barrier(self):
        for e in self.engs:
            for src, val in self.count.items():
                if src == e or val == 0:
                    continue
                if self.seen[e].get(src, 0) >= val:
                    continue
                self.seen[e][src] = val
                self.engs[e].wait_ge(self.sems[src], val)
                self.n_waits += 1
                self._pending_waits[e].append((src, val))
        self.last_write = {}
        self.reads = {}

    def finish(self, e="sp"):
        for src, val in self.count.items():
            if src == e or val == 0:
                continue
            if self.seen[e].get(src, 0) >= val:
                continue
            self.seen[e][src] = val
            self.engs[e].wait_ge(self.sems[src], val)


def bccol(ap, n):
    dims = [list(d) for d in ap.ap]
    dims[-1] = [0, n]
    return bass.AP(tensor=ap.tensor, offset=ap.offset, ap=dims)


def bcast(ap, pos, n):
    dims = [list(d) for d in ap.ap]
    dims.insert(1 + pos, [0, n])
    return bass.AP(tensor=ap.tensor, offset=ap.offset, ap=dims)


SWAP64 = np.concatenate([np.arange(32, 64), np.arange(0, 32)])


def _w_blocks():
    blks = []
    a = np.arange

    def add(name, cols):
        blks.append((name, "w_in", np.asarray(cols), 8))

    for j in range(4):
        add(f"sbv{j}", 1024 + j * 128 + a(128))
    for c in range(4):
        add(f"sbq{c}", 0 + c * 128 + a(128))
        add(f"sbk{c}", 512 + c * 128 + a(128))
        add(f"sbz{c}", 1536 + c * 128 + a(128))
    add("kc", 2560 + a(128))
    add("vc", 2688 + a(128))
    for c in range(4):
        hs = (c, c + 4)
        add(f"nq{c}", np.concatenate([2048 + h * 64 + a(64) for h in hs]))
        add(f"nqs{c}", np.concatenate([2048 + h * 64 + SWAP64 for h in hs]))
    add("ks", 2816 + a(128))
    add("kss", np.concatenate([2816 + g * 64 + SWAP64 for g in range(2)]))
    add("kw", 3072 + a(128))
    add("kws", np.concatenate([3072 + g * 64 + SWAP64 for g in range(2)]))
    add("vs", 2944 + a(128))
    add("vw", 3200 + a(128))
    add("ng", 3840 + a(24))
    for j in range(4):
        add(f"nz{j}", 3328 + j * 128 + a(128))
    for n in range(8):
        blks.append((f"wpa{n}", "w_proj_a", n * 128 + a(128), 4))
        blks.append((f"wpb{n}", "w_proj_b", n * 128 + a(128), 4))
        add(f"ma{n}", 3864 + n * 128 + a(128))
        add(f"mb{n}", 4888 + n * 128 + a(128))
    for n in range(8):
        blks.append((f"wo{n}", "w_out", n * 128 + a(128), 8))
    return blks


W_BLOCKS = _w_blocks()
W_OFF = {}
_off = 0
for _name, _src, _cols, _nk in W_BLOCKS:
    W_OFF[_name] = (_off, _nk, len(_cols))
    _off += _nk * len(_cols)
W_TOT = _off


def _consts():
    c = {}
    half = 32
    freq = (np.float32(10000.0) ** (-np.arange(half, dtype=np.float32) / np.float32(half))).astype(np.float32)
    pos = np.arange(S, dtype=np.float32)
    ang = (pos[:, None] * freq[None, :]).astype(np.float32)
    cos = np.cos(ang).astype(np.float32).T
    sin = np.sin(ang).astype(np.float32).T
    p = np.arange(128)
    sign = np.where((p % 64) < 32, -1.0, 1.0).astype(np.float32)[:, None]
    cos2 = cos[p % 32]
    sin2 = sin[p % 32] * sign
    c["cos"] = np.ascontiguousarray(cos2)
    c["sin"] = np.ascontiguousarray(sin2)
    posc = 16 * np.arange(N_CMP) + 31
    cc = np.zeros((128, 128), np.float32)
    sc = np.zeros((128, 128), np.float32)
    cc[:, :N_CMP] = cos2[:, posc]
    sc[:, :N_CMP] = sin2[:, posc]
    c["cosc"] = cc
    c["sinc"] = sc
    ident = np.eye(128, dtype=np.float32)
    ones = np.ones((128, 128), np.float32)
    jj, ss = np.meshgrid(np.arange(128), np.arange(128), indexing="ij")
    tri = (jj >= ss).astype(np.float32)
    bd = ((jj // 64) == (ss // 64)).astype(np.float32)
    w2m = np.where(jj > ss, 0.0, NEGB).astype(np.float32)
    c["cmat"] = np.concatenate([ident, ones, tri, bd, w2m, -8.0 * tri, -8.0 * ones], axis=1)
    u = np.arange(896)[None, :]
    s_ = np.arange(128)[:, None]
    c["big"] = np.where(s_ < u - 384, 0.0, NEGB).astype(np.float32)
    n_ = np.arange(128)[:, None]
    t_ = np.arange(S)[None, :]
    c["vbias"] = np.where((16 * n_ + 31 <= t_) & (n_ < N_CMP), 0.0, NEGB).astype(np.float32)
    ci = np.arange(128)[:, None]
    sj = np.arange(32)[None, :]
    ov = ((16 * ci < 64 * (sj + 1)) & (16 * ci + 32 > 64 * sj) & (ci < N_CMP)).astype(np.float32)
    c["ov1"] = np.concatenate([np.ones((128, 1), np.float32), ov], axis=1)
    t = np.arange(S)
    cur = (t // 64)[:, None]
    jb = np.arange(32)[None, :]
    m1 = np.ones((S, 32), np.float32)
    ad = np.zeros((S, 32), np.float32)
    fut = jb > cur
    m1[fut] = 0.0
    ad[fut] = -1e30
    frc = ((jb == 0) | (jb == cur - 1)) & ~fut
    m1[frc] = 0.0
    ad[frc] = 1e4
    cu = jb == cur
    m1[cu] = 0.0
    ad[cu] = 2e4
    c["impm"] = np.ascontiguousarray(m1.reshape(NT, 128, 32).transpose(1, 0, 2)).reshape(128, NT * 32)
    c["impa"] = np.ascontiguousarray(ad.reshape(NT, 128, 32).transpose(1, 0, 2)).reshape(128, NT * 32)
    return c


def prep_inputs(inp):
    f = lambda a: np.ascontiguousarray(np.asarray(a, dtype=np.float32))
    L = L_DEPTH
    shared = {}
    wst = np.empty((L, 128, W_TOT), np.float32)
    srcs = {k: f(inp[k]) for k in ("w_in", "w_proj_a", "w_proj_b", "w_out")}
    for name, src, cols, nk in W_BLOCKS:
        off, _, nc_ = W_OFF[name]
        w = srcs[src][:, :, cols]
        w = w.reshape(L, nk, 128, nc_).transpose(0, 2, 1, 3).reshape(L, 128, nk * nc_)
        wst[:, :, off:off + nk * nc_] = w
    shared["wst"] = wst
    ada_w = f(inp["ada_w"])
    shared["ada_w"] = np.ascontiguousarray(
        ada_w.reshape(L, 8, 128, 6, 512).transpose(0, 3, 2, 1, 4).reshape(L, 6, 128, 8 * 512))
    shared["ada_b"] = np.ascontiguousarray(f(inp["ada_b"]).reshape(L, 24, 128).transpose(0, 2, 1))
    shared["norm_g"] = np.ascontiguousarray(f(inp["norm_g"]).reshape(L, 8, 128).transpose(0, 2, 1))
    p = np.arange(128)
    gv = np.zeros((L, 128, 8), np.float32)
    for i, k in enumerate(("q_norm_g", "ks_norm_g", "kw_norm_g", "kc_norm_g")):
        g = f(inp[k])
        gv[:, :, 2 * i] = g[:, p % 64]
        gv[:, :, 2 * i + 1] = g[:, SWAP64[p % 64]]
    shared["gvec"] = gv
    for nm in ("k", "v"):
        w1 = f(inp[f"cmp_w1_{nm}"])
        w1 = w1.reshape(L, 32, 64, 128).transpose(0, 2, 1, 3).reshape(L, 64, 32 * 128)
        shared[f"w1{nm}"] = np.ascontiguousarray(np.concatenate([w1, w1], axis=1))
        pe = f(inp[f"cmp_pe_{nm}"]).transpose(0, 2, 1)
        shared[f"pe{nm}"] = np.ascontiguousarray(np.concatenate([pe, pe], axis=1))
    w2k = f(inp["cmp_w2_k"])
    shared["w2k"] = np.ascontiguousarray(np.concatenate([w2k, w2k[:, :, SWAP64]], axis=2))
    shared["w2v"] = f(inp["cmp_w2_v"])
    shared.update(_consts())
    x = f(inp["x"])
    c = f(inp["c"])
    maps = []
    for b in range(8):
        m = dict(shared)
        m["x"] = x[b]
        m["c"] = np.ascontiguousarray(c[b].reshape(8, 128).T)
        maps.append(m)
    return maps


IN_SHAPES = {
    "x": [S, D], "c": [128, 8], "wst": [L_DEPTH, 128, W_TOT], "ada_w": [L_DEPTH, 6, 128, 4096],
    "ada_b": [L_DEPTH, 128, 24], "norm_g": [L_DEPTH, 128, 8], "gvec": [L_DEPTH, 128, 8],
    "w1k": [L_DEPTH, 128, 4096], "w1v": [L_DEPTH, 128, 4096], "pek": [L_DEPTH, 128, 32],
    "pev": [L_DEPTH, 128, 32], "w2k": [L_DEPTH, 128, 128], "w2v": [L_DEPTH, 128, 64],
    "cos": [128, S], "sin": [128, S], "cosc": [128, 128], "sinc": [128, 128],
    "cmat": [128, 896], "big": [128, 896], "vbias": [128, S], "ov1": [128, 33],
    "impm": [128, NT * 32], "impa": [128, NT * 32],
}


def build_program(n_layers=L_DEPTH, debug=(), stop_after=None):
    nc = bass.Bass("TRN2", target_bir_lowering=False)
    I = {k: nc.dram_tensor(k, shp, F32, kind="ExternalInput").ap() for k, shp in IN_SHAPES.items()}
    out_d = nc.dram_tensor("out", [S, D], F32, kind="ExternalOutput").ap()
    dbg_out = {}

    with ExitStack() as st:
        T = Tracker(nc, st)

        uid = [0]

        def sb(stack, name, shape, dt):
            uid[0] += 1
            return stack.enter_context(nc.sbuf_tensor(f"sb{uid[0]}_{name}", shape, dt))

        PS = [st.enter_context(nc.psum_tensor(f"ps{i}", [128, 512], F32)) for i in range(8)]
        PSK = [("ps", i) for i in range(8)]

        def mm(out, lhsT, rhs, start, stop, reads, writes, inc=None):
            if inc is None:
                inc = stop
            return T.op("pe", lambda: nc.tensor.matmul(out, lhsT=lhsT, rhs=rhs, start=start, stop=stop,
                                                       skip_group_check=True),
                        reads, writes, inc=inc)

        def act(out, in_, func, reads, writes, **kw):
            return T.op("act", lambda: nc.scalar.activation(out, in_, func, **kw), reads, writes)

        def E(eng):
            return nc.vector if eng == "dve" else nc.gpsimd

        def tt(out, in0, in1, op, reads, writes, eng="dve"):
            return T.op(eng, lambda: E(eng).tensor_tensor(out, in0, in1, op), reads, writes)

        def ts(out, in0, s1, s2, op0, op1, reads, writes, eng="dve"):
            if s2 is None:
                return T.op(eng, lambda: E(eng).tensor_scalar(out, in0, s1, None, op0), reads, writes)
            return T.op(eng, lambda: E(eng).tensor_scalar(out, in0, s1, s2, op0, op1), reads, writes)

        def stt(out, in0, scalar, in1, op0, op1, reads, writes):
            return T.op("dve", lambda: nc.vector.scalar_tensor_tensor(out, in0, scalar, in1, op0, op1), reads, writes)

        def cp(out, in_, reads, writes, eng="dve"):
            if eng == "act":
                return T.op("act", lambda: nc.scalar.copy(out, in_), reads, writes)
            return T.op(eng, lambda: E(eng).tensor_copy(out, in_), reads, writes)

        def recip(out, in_, reads, writes):
            return T.op("dve", lambda: nc.vector.reciprocal(out, in_), reads, writes)

        def dump(name, ap, reads):
            if name not in debug:
                return
            shp = list(ap.shape)
            d = nc.dram_tensor("dbg_" + name, shp, ap.dtype, kind="ExternalOutput").ap()
            dbg_out[name] = d
            T.dma("sp", d, ap, reads=reads)

        xT = sb(st, "xT", [128, 8, S], F32)
        hT = sb(st, "hT", [128, 8, S], BF16)
        aT = sb(st, "aT", [128, 4, S], BF16)
        bT = sb(st, "bT", [128, 4, S], BF16)
        NSLOT = 8
        wpool = sb(st, "wpool", [128, NSLOT, 1024], BF16)
        cmat_f = sb(st, "cmat_f", [128, 128], F32)
        cmat = sb(st, "cmat", [128, 896], BF16)
        big = sb(st, "big", [128, 896], BF16)
        ov1 = sb(st, "ov1", [128, 33], BF16)
        cosc = sb(st, "cosc", [128, 128], F32)
        sinc = sb(st, "sinc", [128, 128], F32)
        c_sb = sb(st, "c_sb", [128, 8], F32)
        siluc = sb(st, "siluc", [128, 8, 2], F32)
        small = sb(st, "small", [128, 128], F32)
        siluc_bf = sb(st, "siluc_bf", [128, 8, 2], BF16)
        ident_bf = cmat[:, 0:128]
        ones_bf = cmat[:, 128:256]
        tri_bf = cmat[:, 256:384]
        bd_bf = cmat[:, 384:512]
        w2m_bf = cmat[:, 512:640]
        tri8_bf = cmat[:, 640:768]
        ones8_bf = cmat[:, 768:896]
        def smv(par):
            o = 64 * par
            return (small[:, o:o + 8], small[:, o + 8:o + 16], small[:, o + 16:o + 24], small[:, o + 24:o + 32],
                    small[:, o + 32:o + 40], small[:, o + 40:o + 64])

        def emit_mod(lm, stack):
            par = lm % 2
            gsv, shiftv, gatev, normg, gvec, modT = smv(par)
            adab = [sb(stack, f"adab{i}", [128, 8, 512], BF16) for i in range(2)]
            adab_b = sb(stack, "adab_b", [128, 24], F32)
            T.dma("sp", adab_b[:], I["ada_b"][lm], writes=["adab_b"])
            T.dma("sp", normg, I["norm_g"][lm], writes=[("small_ng", par)])
            T.dma("sp", gvec, I["gvec"][lm], writes=[("small_gv", par)])
            for nb in range(6):
                ab = adab[nb % 2]
                ak = ("adab", nb % 2)
                T.dma("pool", ab[:], I["ada_w"][lm, nb].rearrange("p (a b) -> p a b", a=8), writes=[ak], **CAST)
                for jj in range(4):
                    j = nb * 4 + jj
                    for kc in range(8):
                        mm(PS[7][:, 2 * j:2 * j + 2], ab[:, kc, jj * 128:(jj + 1) * 128], siluc_bf[:, kc, :],
                           kc == 0, kc == 7, [ak, "siluc_bf"], [PSK[7]])
            tt(modT, PS[7][:, 0:48:2], adab_b[:], ALU.add, [PSK[7], "adab_b"], [("modT", par)])
            stt(gsv, modT[:, 8:16], 1.0, normg, ALU.add, ALU.mult, [("modT", par), ("small_ng", par)], [("gsv", par)])
            cp(shiftv, modT[:, 0:8], [("modT", par)], [("shiftv", par)])
            cp(gatev, modT[:, 16:24], [("modT", par)], [("gatev", par)])
            dump(f"mod{lm}", modT, [("modT", par)])

        CAST = dict(max_dma_last_dim=4096)
        T.dma("sp", cmat_f[:], I["cmat"][:, 0:128], writes=["cmat_f"])
        T.dma("pool", cmat[:], I["cmat"], writes=["cmat"], **CAST)
        T.dma("pool", big[:], I["big"], writes=["big"], **CAST)
        T.dma("pool", ov1[:], I["ov1"], writes=["ov1"], **CAST)
        T.dma("sp", cosc[:], I["cosc"], writes=["cosc"])
        T.dma("sp", sinc[:], I["sinc"], writes=["sinc"])
        T.dma("sp", c_sb[:], I["c"], writes=["c_sb"])

        wstate = {"half": 0}

        def load_group(l, names):
            half = wstate["half"]
            wstate["half"] ^= 1
            res = {}
            for i, nm in enumerate(names):
                off, nk, ncol = W_OFF[nm]
                slot = half * 4 + i
                T.dma("pool", wpool[:, slot, 0:nk * ncol], I["wst"][l, :, off:off + nk * ncol],
                      writes=[("w", slot)], **CAST)
                res[nm] = (slot, nk, ncol)
            return res

        pref = {}

        def wview(info, kc):
            slot, nk, ncol = info
            return wpool[:, slot, kc * ncol:(kc + 1) * ncol]

        def proj_F(psum_ap, pkey, info, tq, extra_reads=()):
            slot, nk, ncol = info
            for kc in range(8):
                mm(psum_ap, wview(info, kc), hT[:, kc, tq * 512:(tq + 1) * 512], kc == 0, kc == 7,
                   [("w", slot), "hT"] + list(extra_reads), [pkey])

        with ExitStack() as ps_:
            xin = [sb(ps_, f"xin{i}", [128, D], F32) for i in range(2)]
            for t in range(NT):
                xi = xin[t % 2]
                xk = ("xin", t % 2)
                T.dma("sp", xi[:], I["x"][t * 128:(t + 1) * 128, :], writes=[xk])
                for half in range(2):
                    pb = PS[(2 * t + half) % 4]
                    pk = PSK[(2 * t + half) % 4]
                    for j in range(4):
                        c = half * 4 + j
                        T.op("pe", lambda: nc.tensor.transpose(pb[:, j * 128:(j + 1) * 128],
                                                               xi[:, c * 128:(c + 1) * 128], cmat_f[:]),
                             [xk, "cmat_f"], [pk], inc=(j == 3))
                    cp(xT[:, half * 4:half * 4 + 4, t * 128:(t + 1) * 128],
                       pb[:].rearrange("p (a b) -> p a b", a=4), [pk], ["xT"],
                       eng="dve" if half == 0 else "act")
            act(siluc[:, :, 0], c_sb[:], AF.Exp, ["c_sb"], ["siluc"], scale=-1.0)
            ts(siluc[:, :, 0], siluc[:, :, 0], 1.0, None, ALU.add, None, ["siluc"], ["siluc"])
            recip(siluc[:, :, 0], siluc[:, :, 0], ["siluc"], ["siluc"])
            tt(siluc[:, :, 0], siluc[:, :, 0], c_sb[:], ALU.mult, ["siluc", "c_sb"], ["siluc"])
            cp(siluc[:, :, 1], siluc[:, :, 0], ["siluc"], ["siluc"])
            cp(siluc_bf[:], siluc[:], ["siluc"], ["siluc_bf"])
            if stop_after != ("pro", 0):
                emit_mod(0, ps_)
            T.barrier()

        for l in range(n_layers if stop_after != ("pro", 0) else 0):
            with ExitStack() as s1:
                par = l % 2
                pref["s2"] = load_group(l, ["sbq0", "sbk0", "sbz0", "sbv0"])
                gsv, shiftv, gatev, normg, gvec, modT = smv(par)
                sqt = [sb(s1, f"sqt{i}", [128, 512], BF16) for i in range(2)]
                lnv = sb(s1, "lnv", [128, 512], F32)
                rstd = sb(s1, "rstd", [128, 512], F32)
                tmpf = [sb(s1, f"tmpf{i}", [128, 512], F32) for i in range(2)]
                for tq in range(4 if stop_after != ("s1a", l) else 0):
                    tsl = slice(tq * 512, (tq + 1) * 512)
                    for c in range(8):
                        sq = sqt[c % 2]
                        if c % 2 == 0:
                            tt(sq[:], xT[:, c, tsl], xT[:, c, tsl], ALU.mult, ["xT"], [("sqt", c % 2)])
                        else:
                            act(sq[:], xT[:, c, tsl], AF.Square, ["xT"], [("sqt", c % 2)])
                        mm(PS[1][:], ones_bf, sq[:], c == 0, c == 7, [("sqt", c % 2), "cmat"], [PSK[1]], inc=True)
                    act(lnv[:], PS[1][:], AF.Ln, [PSK[1]], ["lnv"], scale=1.0 / D, bias=EPS)
                    act(rstd[:], lnv[:], AF.Exp, ["lnv"], ["rstd"], scale=-0.5)
                    for c in range(8):
                        tf = tmpf[c % 2]
                        stt(tf[:], xT[:, c, tsl], gsv[:, c:c + 1], rstd[:], ALU.mult, ALU.mult,
                            ["xT", ("gsv", par), "rstd"], [("tmpf", c % 2)])
                        act(hT[:, c, tsl], tf[:], AF.Identity, [("tmpf", c % 2), ("shiftv", par)], ["hT"],
                            bias=shiftv[:, c:c + 1], scale=1.0)
                dump(f"hT{l}", hT[:, :, 0:256], ["hT"])
                T.barrier()
            if stop_after in (("s1", l), ("s1a", l)):
                break

            with ExitStack() as s2:
                sbvm = [[sb(s2, f"sbvm{pp}{i}", [128, NT, 128], BF16) for i in range(2)] for pp in range(2)]
                qmb = [[sb(s2, f"qm{pp}{i}", [128, S], BF16) for i in range(2)] for pp in range(2)]
                kcb = [sb(s2, f"kc_{pp}", [128, S], BF16) for pp in range(2)]
                NB = 3
                e_t = [sb(s2, f"e_t{i}", [128, 512], F32) for i in range(2)]
                sp_t = [sb(s2, f"sp_t{i}", [128, 512], BF16) for i in range(NB)]
                w_t = [sb(s2, f"w_t{i}", [128, 512], BF16) for i in range(NB)]
                lacc = [sb(s2, f"lacc{i}", [128, 512], BF16) for i in range(2)]
                zr = [sb(s2, f"zr{i}", [128, 512], F32) for i in range(2)]
                for pp in range(2):
                    T.op("pool", lambda: nc.gpsimd.memset(qmb[pp][0][64:128, :], 0.0), [], [f"qm{pp}0"])
                    T.op("pool", lambda: nc.gpsimd.memset(qmb[pp][1][0:64, :], 0.0), [], [f"qm{pp}1"])
                    T.op("pool", lambda: nc.gpsimd.memset(sbvm[pp][0][:, :, 64:128], 0.0), [], [f"sbvm{pp}0"])
                    T.op("pool", lambda: nc.gpsimd.memset(sbvm[pp][1][:, :, 0:64], 0.0), [], [f"sbvm{pp}1"])
                nxt = pref.pop("s2")
                qi = 0

                ucnt = [0]

                def v_unit(cn, gn, t):
                    pp = cn % 2
                    info = gn[f"sbv{cn}"]
                    ucnt[0] += 1
                    pb, pk = PS[6 + ucnt[0] % 2], PSK[6 + ucnt[0] % 2]
                    for kc in range(8):
                        mm(pb[:, 0:128], hT[:, kc, t * 128:(t + 1) * 128], wview(info, kc), kc == 0, kc == 7,
                           ["hT", ("w", info[0])], [pk])
                    cp(sbvm[pp][0][:, t, 0:64], pb[:, 0:64], [pk], [f"sbvm{pp}0"], eng="dve")
                    cp(sbvm[pp][1][:, t, 64:128], pb[:, 64:128], [pk], [f"sbvm{pp}1"], eng="dve")

                def proj_unit(cn, gn, tq, which):
                    pp = cn % 2
                    tsl = slice(tq * 512, (tq + 1) * 512)
                    ucnt[0] += 1
                    pb, pk = PS[6 + ucnt[0] % 2], PSK[6 + ucnt[0] % 2]
                    if which == "q":
                        proj_F(pb[:], pk, gn[f"sbq{cn}"], tq)
                        cp(qmb[pp][0][0:64, tsl], pb[0:64, :], [pk], [f"qm{pp}0"], eng="dve")
                        cp(qmb[pp][1][64:128, tsl], pb[64:128, :], [pk], [f"qm{pp}1"], eng="dve")
                    elif which == "k":
                        proj_F(pb[:], pk, gn[f"sbk{cn}"], tq)
                        cp(kcb[pp][:, tsl], pb[:], [pk], [f"kc_{pp}"], eng="dve")
                    else:
                        v_unit(cn, gn, tq)

                for tq in range(4):
                    proj_unit(0, nxt, tq, "q")
                    proj_unit(0, nxt, tq, "k")
                for t in range(NT):
                    v_unit(0, nxt, t)
                for c in range(4):
                    g = nxt
                    pending = []
                    if c < 3:
                        nxt = load_group(l, [f"sbq{c + 1}", f"sbk{c + 1}", f"sbz{c + 1}", f"sbv{c + 1}"])
                        pending = [(c + 1, nxt, tq, w_) for tq in range(4) for w_ in "qk"]
                        pending += [(c + 1, nxt, t, "v") for t in range(NT)]
                    else:
                        pref["3a"] = load_group(l, ["kc", "vc"])
                    qm = qmb[c % 2]
                    kc_ = kcb[c % 2]
                    qmk = [f"qm{c % 2}0", f"qm{c % 2}1"]
                    kck = f"kc_{c % 2}"
                    tiles = []
                    for Q in range(4):
                        nkb = 4 * Q + 4
                        for idx in range(nkb):
                            for hh in range(2):
                                kb = nkb - 1 - idx
                                tiles.append(dict(Q=Q, hh=hh, kb=kb, first=(idx == 0), last=(kb == 0),
                                                  c0=(128 * (kb - 4 * Q) if kb >= 4 * Q else 0),
                                                  diag=(kb >= 4 * Q)))
                    ntl = len(tiles)
                    accb = lambda Q: (PS[4 + (qi + Q) % 2], PSK[4 + (qi + Q) % 2])

                    def zgate(Q):
                        z = zr[(qi + Q) % 2]
                        zk = ("zr", (qi + Q) % 2)
                        proj_F(PS[6][:], PSK[6], g[f"sbz{c}"], Q)
                        act(z[:], PS[6][:], AF.Exp, [PSK[6]], [zk], scale=-1.0)
                        act(z[:], z[:], AF.Ln, [zk], [zk], bias=1.0)
                        act(z[:], z[:], AF.Exp, [zk], [zk], scale=-1.0)
                        tt(z[:], PS[6][:], z[:], ALU.mult, [PSK[6], zk], [zk])

                    def P1(j):
                        t_ = tiles[j]
                        Q, hh, kb, c0 = t_["Q"], t_["hh"], t_["kb"], t_["c0"]
                        zs, zsk = PS[j % 4], PSK[j % 4]
                        if t_["first"] and hh == 0:
                            zgate(Q)
                        cols = slice(Q * 512 + c0, (Q + 1) * 512)
                        mm(zs[:, c0:512], kc_[:, kb * 128:(kb + 1) * 128], qm[hh][:, cols], True, not t_["diag"],
                           [kck, qmk[hh]], [zsk], inc=True)
                        if t_["diag"]:
                            mm(zs[:, c0:512], ident_bf, big[:, 384:896 - c0], False, True, ["cmat", "big"], [zsk])

                    def A1a(j):
                        t_ = tiles[j]
                        c0 = t_["c0"]
                        zs, zsk = PS[j % 4], PSK[j % 4]
                        e, ek = e_t[j % 2], ("e_t", j % 2)
                        act(e[:, c0:512], zs[:, c0:512], AF.Exp, [zsk], [ek], scale=0.125)

                    def A1b(j):
                        t_ = tiles[j]
                        c0 = t_["c0"]
                        e, ek = e_t[j % 2], ("e_t", j % 2)
                        sp, spk = sp_t[j % NB], ("sp_t", j % NB)
                        act(sp[:, c0:512], e[:, c0:512], AF.Ln, [ek], [spk], bias=1.0)

                    def P2(j):
                        t_ = tiles[j]
                        c0, hh = t_["c0"], t_["hh"]
                        zs, zsk = PS[j % 4], PSK[j % 4]
                        sp, spk = sp_t[j % NB], ("sp_t", j % NB)
                        lk = ("lacc", hh)
                        if t_["first"]:
                            T.op("dve", lambda: nc.vector.memset(lacc[hh][:], 0.0), [], [lk])
                            mm(zs[:, c0:512], tri8_bf, sp[:, c0:512], False, True, ["cmat", spk], [zsk])
                        else:
                            mm(zs[:, c0:512], tri8_bf, sp[:, c0:512], False, False, ["cmat", spk], [zsk], inc=False)
                            mm(zs[:, c0:512], ones8_bf, lacc[hh][:, c0:512], False, True, ["cmat", lk], [zsk])
                        if not t_["last"]:
                            tt(lacc[hh][:, c0:512], lacc[hh][:, c0:512], sp[:, c0:512], ALU.add, [lk, spk], [lk])

                    def A2(j):
                        t_ = tiles[j]
                        c0 = t_["c0"]
                        zs, zsk = PS[j % 4], PSK[j % 4]
                        w, wk_ = w_t[j % NB], ("w_t", j % NB)
                        act(w[:, c0:512], zs[:, c0:512], AF.Exp, [zsk], [wk_], scale=0.125)

                    def P3(j):
                        t_ = tiles[j]
                        Q, hh, kb, c0 = t_["Q"], t_["hh"], t_["kb"], t_["c0"]
                        h = 2 * c + hh
                        w, wk_ = w_t[j % NB], ("w_t", j % NB)
                        ab, abk = accb(Q)
                        mm(ab[:, c0:512], sbvm[c % 2][hh][:, kb, :], w[:, c0:512],
                           t_["first"] and hh == 0, t_["last"] and hh == 1, [f"sbvm{c % 2}{hh}", wk_], [abk], inc=True)
                        if t_["last"] and hh == 1:
                            qs = slice(Q * 512, (Q + 1) * 512)
                            if c == 0 and Q == 0:
                                cp(e_t[0][:], ab[:], [abk], [("e_t", 0)])
                                dump(f"sba{l}", e_t[0][:], [("e_t", 0)])
                            tt(aT[:, c, qs], ab[:], zr[(qi + Q) % 2][:], ALU.mult, [abk, ("zr", (qi + Q) % 2)], ["aT"])

                    for j in range(ntl + 4):
                        if j < ntl:
                            P1(j)
                        if 0 <= j - 1 < ntl:
                            A1a(j - 1)
                        if 0 <= j - 3 < ntl:
                            A2(j - 3)
                        if 0 <= j - 1 < ntl:
                            A1b(j - 1)
                        if 0 <= j - 2 < ntl:
                            P2(j - 2)
                        if 0 <= j - 4 < ntl:
                            P3(j - 4)
                        if pending and j >= 4 and (j - 4) % 3 == 0:
                            proj_unit(*pending.pop(0))
                    while pending:
                        proj_unit(*pending.pop(0))
                    qi += 4
                dump(f"aT{l}", aT[:, :, 0:256], ["aT"])
                T.barrier()
            if stop_after == ("s2", l):
                break

            with ExitStack() as s3:
                vc1 = sb(s3, "vc1", [128, 2, 98], BF16)
                kcT = sb(s3, "kcT", [128, 128], BF16)
                sigg = sb(s3, "sigg", [128, NT, 32], F32)
                pb16 = [sb(s3, f"pb16_{i}", [128, 512], BF16) for i in range(3)]
                sml = sb(s3, "sml", [128, 256], F32)
                negsel = sb(s3, "negsel", [128, 2, 32], BF16)
                for g_ in range(2):
                    cp(vc1[:, g_, 64:97], ov1[:, :], ["ov1"], ["vc1"], eng="pool")
                with ExitStack() as s3a:
                    wk = [sb(s3a, f"wk{i}", [128, 512], F32) for i in range(4)]
                    kcr = sb(s3a, "kcr", [128, S], BF16)
                    vcr = sb(s3a, "vcr", [128, S], BF16)
                    w1k = sb(s3a, "w1k", [128, 32, 128], BF16)
                    w1v = sb(s3a, "w1v", [128, 32, 128], BF16)
                    pek = sb(s3a, "pek", [128, 32], BF16)
                    pev = sb(s3a, "pev", [128, 32], BF16)
                    w2k = sb(s3a, "w2k", [128, 128], BF16)
                    w2v = sb(s3a, "w2v", [128, 64], BF16)
                    hid = sb(s3a, "hid", [128, 128], BF16)
                    T.dma("pool", w1k[:], I["w1k"][l].rearrange("p (a b) -> p a b", a=32), writes=["w1k"], **CAST)
                    T.dma("pool", w1v[:], I["w1v"][l].rearrange("p (a b) -> p a b", a=32), writes=["w1v"], **CAST)
                    T.dma("pool", pek[:], I["pek"][l], writes=["pek"], **CAST)
                    T.dma("pool", pev[:], I["pev"][l], writes=["pev"], **CAST)
                    T.dma("pool", w2k[:], I["w2k"][l], writes=["w2k"], **CAST)
                    T.dma("pool", w2v[:], I["w2v"][l], writes=["w2v"], **CAST)
                    g = pref.pop("3a")
                    pref["3b"] = load_group(l, ["nq0", "nqs0"])
                    for tq in range(4):
                        proj_F(PS[0][:], PSK[0], g["kc"], tq)
                        cp(kcr[:, tq * 512:(tq + 1) * 512], PS[0][:], [PSK[0]], ["kcr"], eng="dve")
                        proj_F(PS[1][:], PSK[1], g["vc"], tq)
                        cp(vcr[:, tq * 512:(tq + 1) * 512], PS[1][:], [PSK[1]], ["vcr"], eng="act")
                    for kv, raw, rawk, w1, w1key, pe_, pekey in (("k", kcr, "kcr", w1k, "w1k", pek, "pek"),
                                                                 ("v", vcr, "vcr", w1v, "w1v", pev, "pev")):
                        for g_ in range(2):
                            po = slice(64 * g_, 64 * g_ + 64)
                            pre = PS[2][:, 0:N_CMP]
                            for li in range(32):
                                mm(pre, w1[po, li, :], raw[po, li:li + 16 * (N_CMP - 1) + 1:16], li == 0, False,
                                   [w1key, rawk], [PSK[2]], inc=False)
                                mm(pre, w1[po, li, :], bccol(pe_[po, li:li + 1], N_CMP),
                                   False, li == 31, [w1key, pekey], [PSK[2]], inc=(li == 31))
                            e0 = wk[0][:, 0:N_CMP]
                            act(e0, pre, AF.Exp, [PSK[2]], [("wk", 0)], scale=-1.0)
                            ts(e0, e0, 1.0, None, ALU.add, None, [("wk", 0)], [("wk", 0)])
                            recip(e0, e0, [("wk", 0)], [("wk", 0)])
                            tt(hid[:, 0:N_CMP], pre, e0, ALU.mult, [PSK[2], ("wk", 0)], ["hid"])
                            if kv == "k":
                                pk_, pk2 = PS[3][po, 0:N_CMP], PS[3][po, 128:128 + N_CMP]
                                mm(pk_, w2k[:, 0:64], hid[:, 0:N_CMP], True, True, ["w2k", "hid"], [PSK[3]])
                                mm(pk2, w2k[:, 64:128], hid[:, 0:N_CMP], True, True, ["w2k", "hid"], [PSK[3]])
                                sq = pb16[0][po, 0:N_CMP]
                                act(sq, pk_, AF.Square, [PSK[3]], [("pb16", 0)])
                                ssq = PS[5][po, 0:N_CMP]
                                mm(ssq, ones_bf[po, 0:64], sq, True, True, ["cmat", ("pb16", 0)], [PSK[5]])
                                lv = wk[1][po, 0:N_CMP]
                                act(lv, ssq, AF.Ln, [PSK[5]], [("wk", 1)], scale=1.0 / 64, bias=EPS)
                                act(lv, lv, AF.Exp, [("wk", 1)], [("wk", 1)], scale=-0.5)
                                kn = wk[2][po, 0:N_CMP]
                                kns = wk[3][po, 0:N_CMP]
                                stt(kn, pk_, gvec[po, 6:7], lv, ALU.mult, ALU.mult, [PSK[3], ("small_gv", par), ("wk", 1)],
                                    [("wk", 2)])
                                stt(kns, pk2, gvec[po, 7:8], lv, ALU.mult, ALU.mult, [PSK[3], ("small_gv", par), ("wk", 1)],
                                    [("wk", 3)])
                                tt(kn, kn, cosc[po, 0:N_CMP], ALU.mult, [("wk", 2), "cosc"], [("wk", 2)])
                                tt(kns, kns, sinc[po, 0:N_CMP], ALU.mult, [("wk", 3), "sinc"], [("wk", 3)], eng="pool")
                                tt(kcT[po, 0:N_CMP], kn, kns, ALU.add, [("wk", 2), ("wk", 3)], ["kcT"])
                            else:
                                pv_ = PS[3][0:N_CMP, 256:320]
                                mm(pv_, hid[:, 0:N_CMP], w2v[:, :], True, True, ["hid", "w2v"], [PSK[3]])
                                cp(vc1[0:N_CMP, g_, 0:64], pv_, [PSK[3]], ["vc1"])
                    dump(f"kcT{l}", kcT[:], ["kcT"])
                    dump(f"vc1{l}", vc1[:], ["vc1"])
                    T.barrier()
                qT = sb(s3, "qT", [128, 4, S], BF16)
                ksT = sb(s3, "ksT", [128, S], BF16)
                kwT = sb(s3, "kwT", [128, S], BF16)
                v1 = sb(s3, "v1", [128, NT, 2, 2, 66], BF16)
                T.op("pool", lambda: nc.gpsimd.memset(v1[:, :, :, :, 64:66], 1.0), [], ["v1"])
                with ExitStack() as s3b:
                    wk = [sb(s3b, f"wk{i}", [128, 512], F32) for i in range(4)]
                    cos_t = sb(s3b, "cos_t", [128, 512], F32)
                    sin_t = sb(s3b, "sin_t", [128, 512], F32)
                    jobs = [(f"nq{c}", f"nqs{c}", qT[:, c, :], "qT", 0) for c in range(4)]
                    jobs += [("ks", "kss", ksT[:], "ksT", 2), ("kw", "kws", kwT[:], "kwT", 4)]
                    nxt = pref.pop("3b")
                    for ji, (na, nb_, dest, dkey, gi) in enumerate(jobs):
                        g = nxt
                        if ji + 1 < len(jobs):
                            nxt = load_group(l, [jobs[ji + 1][0], jobs[ji + 1][1]])
                        else:
                            nxt = load_group(l, ["vs", "vw", "ng"])
                        for tq in range(4):
                            tsl = slice(tq * 512, (tq + 1) * 512)
                            T.dma("sp", cos_t[:], I["cos"][:, tsl], writes=["cos_t"])
                            T.dma("sp", sin_t[:], I["sin"][:, tsl], writes=["sin_t"])
                            pa, pak = PS[tq % 2], PSK[tq % 2]
                            pbb, pbk = PS[2 + tq % 2], PSK[2 + tq % 2]
                            pq, pqk = PS[4 + tq % 2], PSK[4 + tq % 2]
                            proj_F(pa[:], pak, g[na], tq)
                            proj_F(pbb[:], pbk, g[nb_], tq)
                            sq = pb16[tq % 2]
                            sqk = ("pb16", tq % 2)
                            act(sq[:], pa[:], AF.Square, [pak], [sqk])
                            mm(pq[:], bd_bf, sq[:], True, True, ["cmat", sqk], [pqk])
                            act(wk[0][:], pq[:], AF.Ln, [pqk], [("wk", 0)], scale=1.0 / 64, bias=EPS)
                            act(wk[0][:], wk[0][:], AF.Exp, [("wk", 0)], [("wk", 0)], scale=-0.5)
                            stt(wk[1][:], pa[:], gvec[:, gi:gi + 1], wk[0][:], ALU.mult, ALU.mult,
                                [pak, ("small_gv", par), ("wk", 0)], [("wk", 1)])
                            stt(wk[2][:], pbb[:], gvec[:, gi + 1:gi + 2], wk[0][:], ALU.mult, ALU.mult,
                                [pbk, ("small_gv", par), ("wk", 0)], [("wk", 2)])
                            tt(wk[1][:], wk[1][:], cos_t[:], ALU.mult, [("wk", 1), "cos_t"], [("wk", 1)],
                               eng="pool")
                            tt(wk[2][:], wk[2][:], sin_t[:], ALU.mult, [("wk", 2), "sin_t"], [("wk", 2)],
                               eng="pool")
                            tt(dest[:, tsl], wk[1][:], wk[2][:], ALU.add, [("wk", 1), ("wk", 2)], [dkey])
                    dump(f"qT{l}", qT[:, :, 0:256], ["qT"])
                    dump(f"ksT{l}", ksT[:, 0:512], ["ksT"])
                    g = nxt
                    nzg = load_group(l, [f"nz{j}" for j in range(4)])
                    slot0 = g["vs"][0]
                    for t in range(NT):
                        pb, pk = PS[t % 2], PSK[t % 2]
                        for kc in range(8):
                            mm(pb[:, 0:256], hT[:, kc, t * 128:(t + 1) * 128],
                               wpool[:, slot0:slot0 + 2, kc * 128:(kc + 1) * 128],
                               kc == 0, kc == 7, ["hT", ("w", slot0), ("w", slot0 + 1)], [pk])
                        cp(v1[:, t, :, :, 0:64], pb[:, 0:256].rearrange("p (a b c) -> p a b c", a=2, b=2), [pk], ["v1"],
                           eng="dve" if t % 2 == 0 else "act")
                    ngi = g["ng"]
                    for t in range(NT):
                        for kc in range(8):
                            mm(PS[2][:, t * 32:t * 32 + 24], hT[:, kc, t * 128:(t + 1) * 128], wview(ngi, kc),
                               kc == 0, kc == 7, ["hT", ("w", ngi[0])], [PSK[2]], inc=(kc == 7 and t == NT - 1))
                    sg = sigg[:].rearrange("p a b -> p (a b)")
                    act(sg, PS[2][:], AF.Exp, [PSK[2]], ["sigg"], scale=-1.0)
                    ts(sg, sg, 1.0, None, ALU.add, None, ["sigg"], ["sigg"])
                    recip(sg, sg, ["sigg"], ["sigg"])
                    dump(f"sigg{l}", sigg[:], ["sigg"])
                    dump(f"v1{l}", v1[:, 0:2], ["v1"])
                    T.barrier()
                nzslot = nzg["nz0"][0]
                pref["s4"] = load_group(l, ["wpa0", "wpb0", "ma0", "mb0"])
                with ExitStack() as s3c:
                    szs = [sb(s3c, "sz0", [128, 512], F32)] * 2
                    qmg = [sb(s3c, f"qmg{i}", [128, 4, 128], BF16) for i in range(2)]
                    T.op("pool", lambda: nc.gpsimd.memset(qmg[0][64:128], 0.0), [], [("qmg", 0)])
                    T.op("pool", lambda: nc.gpsimd.memset(qmg[1][0:64], 0.0), [], [("qmg", 1)])
                    negx = sb(s3c, "negx", [128, 32 * 64], BF16)
                    vbias = sb(s3c, "vbias", [128, S], BF16)
                    impm = sb(s3c, "impm", [128, NT, 32], BF16)
                    impa = sb(s3c, "impa", [128, NT, 32], BF16)
                    T.dma("pool", vbias[:], I["vbias"], writes=["vbias"], **CAST)
                    T.dma("pool", impm[:], I["impm"].rearrange("p (a b) -> p a b", a=NT), writes=["impm"], **CAST)
                    T.dma("pool", impa[:], I["impa"].rearrange("p (a b) -> p a b", a=NT), writes=["impa"], **CAST)
                    ocomb = sb(s3c, "ocomb", [128, 512], F32)
                    b16 = sb(s3c, "b16", [128, 512], BF16)
                    tiles = []
                    for tb in range(NT):
                        kbs = [kb for kb in (tb - 2, tb - 1, tb) if kb >= 0]
                        nW = len(kbs)
                        tiles.append(dict(tb=tb, g=0, br="c", kb=0, first=True, last=True))
                        tiles.append(dict(tb=tb, g=0, br="nz", kb=0))
                        for i, kb in enumerate(kbs):
                            tiles.append(dict(tb=tb, g=0, br="w", kb=kb, first=(i == 0), last=(i == nW - 1)))
                        tiles.append(dict(tb=tb, g=1, br="c", kb=0, first=True, last=True))
                        if tb > 0:
                            tiles.append(dict(tb=tb - 1, g=0, br="tr", kb=0))
                        for kb in range(tb + 1):
                            tiles.append(dict(tb=tb, g=0, br="s", kb=kb, first=(kb == 0), last=(kb == tb), tail=(kb == tb)))
                        for i, kb in enumerate(kbs):
                            tiles.append(dict(tb=tb, g=1, br="w", kb=kb, first=(i == 0), last=(i == nW - 1),
                                              expand=(nW > 1 and i == 1)))
                        for kb in range(tb + 1):
                            tiles.append(dict(tb=tb, g=1, br="s", kb=kb, first=(kb == 0), last=(kb == tb), tail=(kb == tb),
                                              expand=(nW == 1 and kb == 0)))
                    ntl = len(tiles)
                    NTR = 3
                    tmpI = sml[:, 128:256].rearrange("p (h j) -> p h j", h=4)
                    tmpO = sb(s3c, "tmpO", [128, 4, 64], F32)

                    def acc_of(tb, g_, br):
                        b_ = {"c": 3, "w": 4 + g_, "s": 6 + g_}[br]
                        return PS[b_][:].rearrange("p (a b) -> p a b", a=4), PSK[b_]

                    def nz_P1(tb, zs, zsk):
                        tbs = slice(tb * 128, (tb + 1) * 128)
                        for kc in range(8):
                            mm(zs[:], hT[:, kc, tbs], wpool[:, nzslot:nzslot + 4, kc * 128:(kc + 1) * 128],
                               kc == 0, kc == 7, ["hT"] + [("w", nzslot + i) for i in range(4)], [zsk])

                    def nz_A1(tb, zs, zsk):
                        sz, szk = szs[0], ("sz", 0)
                        act(sz[:], zs[:], AF.Exp, [zsk], [szk], scale=-1.0)
                        act(sz[:], sz[:], AF.Ln, [szk], [szk], bias=1.0)
                        act(sz[:], sz[:], AF.Exp, [szk], [szk], scale=-1.0)
                        tt(sz[:], zs[:], sz[:], ALU.mult, [zsk, szk], [szk])

                    def expand(tb, g_):
                        nbk = 2 * (tb + 1)
                        cp(negx[:, 0:nbk * 64].rearrange("p (a b) -> p a b", b=64),
                           bcast(negsel[:, g_, 0:nbk], 1, 64), [("negsel", g_)], ["negx"])

                    def P1(j):
                        t_ = tiles[j]
                        tb, g_, br, kb = t_["tb"], t_["g"], t_["br"], t_["kb"]
                        tbs = slice(tb * 128, (tb + 1) * 128)
                        po = slice(64 * g_, 64 * g_ + 64)
                        qg = qmg[g_][:]
                        qk_ = ("qmg", g_)
                        zs, zsk = PS[j % NTR], PSK[j % NTR]
                        if br == "nz":
                            nz_P1(tb, zs, zsk)
                            return
                        if br == "tr":
                            for c in range(4):
                                mm(zs[:, c * 128:(c + 1) * 128], b16[:, c * 128:(c + 1) * 128], ident_bf, True, True,
                                   ["b16", "cmat"], [zsk], inc=(c == 3))
                            return
                        if t_.get("expand"):
                            expand(tb, 1)
                        if br == "c":
                            cp(qmg[g_][po], qT[po, :, tbs], ["qT"], [qk_], eng="pool")
                            mm(zs[0:N_CMP, :], kcT[:, 0:N_CMP], qg, True, False, ["kcT", qk_], [zsk], inc=False)
                            mm(zs[0:N_CMP, :], ident_bf[0:N_CMP, 0:N_CMP], bcast(vbias[0:N_CMP, tbs], 0, 4), False, True,
                               ["cmat", "vbias"], [zsk])
                        elif br == "w":
                            nob = (kb == tb - 1)
                            mm(zs[:], kwT[:, kb * 128:(kb + 1) * 128], qg, True, nob, ["kwT", qk_], [zsk], inc=nob)
                            if kb == tb:
                                mm(zs[:], ident_bf, bcast(big[:, 385:513], 0, 4), False, True, ["cmat", "big"], [zsk])
                            elif kb == tb - 2:
                                mm(zs[:], ident_bf, bcast(w2m_bf, 0, 4), False, True, ["cmat"], [zsk])
                        else:
                            mm(zs[:], ksT[:, kb * 128:(kb + 1) * 128], qg, True, False, ["ksT", qk_], [zsk], inc=False)
                            mm(zs[:], negx[:, 2 * kb * 64:(2 * kb + 2) * 64], bcast(ident_bf, 0, 4), False, kb != tb,
                               ["negx", "cmat"], [zsk], inc=(kb != tb))
                            if kb == tb:
                                mm(zs[:], ident_bf, bcast(big[:, 385:513], 0, 4), False, True, ["cmat", "big"], [zsk])

                    def A1(j):
                        t_ = tiles[j]
                        nr = N_CMP if t_["br"] == "c" else 128
                        zs, zsk = PS[j % NTR], PSK[j % NTR]
                        if t_["br"] == "nz":
                            nz_A1(t_["tb"], zs, zsk)
                            return
                        if t_["br"] == "tr":
                            tbs = slice(t_["tb"] * 128, (t_["tb"] + 1) * 128)
                            cp(bT[:, :, tbs], zs[:].rearrange("p (a b) -> p a b", a=4), [zsk], ["bT"], eng="act")
                            return
                        p, pk = pb16[j % 3], ("pb16", j % 3)
                        act(p[0:nr, :], zs[0:nr, :], AF.Exp, [zsk], [pk], scale=0.125)

                    def P2(j):
                        t_ = tiles[j]
                        tb, g_, br, kb = t_["tb"], t_["g"], t_["br"], t_["kb"]
                        if br in ("nz", "tr"):
                            return
                        p, pk = pb16[j % 3], ("pb16", j % 3)
                        a3, ak = acc_of(tb, g_, br)
                        for h in range(4):
                            st_ = t_["first"] and h == 0
                            sp_ = t_["last"] and h == 3
                            if br == "c":
                                mm(a3[:, h, 0:97], p[0:N_CMP, h * 128:(h + 1) * 128], vc1[0:N_CMP, g_, 0:97],
                                   st_, sp_, [pk, "vc1"], [ak], inc=(h == 3))
                            else:
                                mm(a3[:, h, 0:65], p[:, h * 128:(h + 1) * 128],
                                   v1[:, kb, 0 if br == "s" else 1, g_, 0:65], st_, sp_, [pk, "v1"], [ak], inc=(h == 3))
                        if br == "c":
                            select(tb, g_)
                        if t_.get("tail"):
                            combine(tb, g_)

                    def select(tb, g_):
                        a3, ak = acc_of(tb, g_, "c")
                        rc = sml[:, 16 * g_:16 * g_ + 4]
                        rck = ("sml_rc", g_)
                        ts(rc, a3[:, :, 64], 1e-30, None, ALU.max, None, [ak], [rck])
                        recip(rc, rc, [rck], [rck])
                        imp = sml[:, 32 + 32 * g_:64 + 32 * g_]
                        ik = ("sml_imp", g_)
                        tt(tmpI[:], a3[:, :, 65:97], bcast(rc, 1, 32), ALU.mult, [ak, rck], ["tmpI"])
                        T.op("dve", lambda: nc.vector.tensor_reduce(out=imp, in_=tmpI[:].rearrange("p h j -> p j h"),
                                                                    axis=mybir.AxisListType.X, op=ALU.add),
                             ["tmpI"], [ik])
                        tt(imp, imp, impa[:, tb, :], ALU.add, [ik, "impa"], [ik])
                        top8 = sml[:, 96 + 8 * g_:104 + 8 * g_]
                        tk = ("sml_top", g_)
                        T.op("dve", lambda: nc.vector.max(out=top8, in_=imp), [ik], [tk])
                        ts(negsel[:, g_, :], imp, top8[:, 7:8], NEGB, ALU.is_lt, ALU.mult, [ik, tk], [("negsel", g_)])
                        if g_ == 0:
                            expand(tb, 0)
                        tt(rc, rc, sigg[:, tb, 0 + 4 * g_:4 + 4 * g_], ALU.mult, [rck, "sigg"], [rck])
                        ok = ("ocomb", g_)
                        tt(ocomb[:, g_ * 256:(g_ + 1) * 256].rearrange("p (h d) -> p h d", h=4), a3[:, :, 0:64],
                           bcast(rc, 1, 64), ALU.mult, [ak, rck], [ok])

                    def combine(tb, g_):
                        tbs = slice(tb * 128, (tb + 1) * 128)
                        ok = ("ocomb", g_)
                        ocv = ocomb[:, g_ * 256:(g_ + 1) * 256].rearrange("p (h d) -> p h d", h=4)
                        for br, gi, off in (("w", 16, 4), ("s", 8, 8)):
                            a3, ak = acc_of(tb, g_, br)
                            r_ = sml[:, 16 * g_ + off:16 * g_ + off + 4]
                            rk = ("sml_r" + br, g_)
                            recip(r_, a3[:, :, 64], [ak], [rk])
                            tt(r_, r_, sigg[:, tb, gi + 4 * g_:gi + 4 + 4 * g_], ALU.mult, [rk, "sigg"], [rk])
                            tt(tmpO[:], a3[:, :, 0:64], bcast(r_, 1, 64), ALU.mult, [ak, rk], ["tmpO"])
                            tt(ocv, ocv, tmpO[:], ALU.add, [ok, "tmpO"], [ok])
                        if g_ == 1:
                            if tb == 5:
                                dump(f"ocomb{l}", ocomb[:], [("ocomb", 0), ("ocomb", 1)])
                            tt(b16[:], ocomb[:], szs[0][:], ALU.mult, [("ocomb", 0), ("ocomb", 1), ("sz", 0)],
                               ["b16"])

                    for j in range(ntl + 2):
                        if j < ntl:
                            P1(j)
                        if 0 <= j - 1 < ntl:
                            A1(j - 1)
                        if 0 <= j - 2 < ntl:
                            P2(j - 2)
                    tiles.append(dict(tb=NT - 1, g=0, br="tr", kb=0))
                    P1(ntl)
                    A1(ntl)
                    dump(f"bT{l}", bT[:, :, 0:256], ["bT"])
                    T.barrier()
            if stop_after == ("s3", l):
                break

            with ExitStack() as s4:
                yT = sb(s4, "yT", [128, 8, S], BF16)
                ra = [sb(s4, f"ra{i}", [128, 512], F32) for i in range(2)]
                rb = [sb(s4, f"rb{i}", [128, 512], F32) for i in range(2)]
                nxt = pref.pop("s4")
                it = 0
                for n in range(8):
                    g = nxt
                    if n < 7:
                        nxt = load_group(l, [f"wpa{n + 1}", f"wpb{n + 1}", f"ma{n + 1}", f"mb{n + 1}"])
                    else:
                        nxt = load_group(l, ["wo0", "wo1", "wo2", "wo3"])
                    for tq in range(4):
                        tsl = slice(tq * 512, (tq + 1) * 512)
                        r = it % 2
                        it += 1
                        p_a, p_ak = PS[0 + r], PSK[0 + r]
                        p_b, p_bk = PS[2 + r], PSK[2 + r]
                        p_ma, p_mak = PS[4 + r], PSK[4 + r]
                        p_mb, p_mbk = PS[6 + r], PSK[6 + r]
                        ia, ib = g[f"wpa{n}"], g[f"wpb{n}"]
                        for fc in range(4):
                            mm(p_a[:], wview(ia, fc), aT[:, fc, tsl], fc == 0, fc == 3, [("w", ia[0]), "aT"], [p_ak])
                        for fc in range(4):
                            mm(p_b[:], wview(ib, fc), bT[:, fc, tsl], fc == 0, fc == 3, [("w", ib[0]), "bT"], [p_bk])
                        proj_F(p_ma[:], p_mak, g[f"ma{n}"], tq)
                        proj_F(p_mb[:], p_mbk, g[f"mb{n}"], tq)
                        act(ra[r][:], p_ma[:], AF.Exp, [p_mak], [("ra", r)], scale=-1.0)
                        act(rb[r][:], p_mb[:], AF.Exp, [p_mbk], [("rb", r)], scale=-1.0)
                        act(ra[r][:], ra[r][:], AF.Ln, [("ra", r)], [("ra", r)], bias=1.0)
                        act(rb[r][:], rb[r][:], AF.Ln, [("rb", r)], [("rb", r)], bias=1.0)
                        act(ra[r][:], ra[r][:], AF.Exp, [("ra", r)], [("ra", r)], scale=-1.0)
                        act(rb[r][:], rb[r][:], AF.Exp, [("rb", r)], [("rb", r)], scale=-1.0)
                        tt(ra[r][:], p_a[:], ra[r][:], ALU.mult, [p_ak, ("ra", r)], [("ra", r)])
                        tt(rb[r][:], p_b[:], rb[r][:], ALU.mult, [p_bk, ("rb", r)], [("rb", r)])
                        tt(yT[:, n, tsl], ra[r][:], rb[r][:], ALU.add, [("ra", r), ("rb", r)], ["yT"])
                dump(f"yT{l}", yT[:, :, 0:256], ["yT"])
                if l + 1 < n_layers:
                    emit_mod(l + 1, s4)
                for half in range(2):
                    g = nxt
                    if half == 0:
                        nxt = load_group(l, ["wo4", "wo5", "wo6", "wo7"])
                    for nn in range(4):
                        n = half * 4 + nn
                        info = g[f"wo{n}"]
                        for tq in range(4):
                            tsl = slice(tq * 512, (tq + 1) * 512)
                            r = it % 2
                            it += 1
                            po_, pok = PS[r], PSK[r]
                            for kc in range(8):
                                mm(po_[:], wview(info, kc), yT[:, kc, tsl], kc == 0, kc == 7, [("w", info[0]), "yT"],
                                   [pok])
                            stt(xT[:, n, tsl], po_[:], gatev[:, n:n + 1], xT[:, n, tsl], ALU.mult, ALU.add,
                                [pok, ("gatev", par), "xT"], ["xT"])
                T.barrier()

        if stop_after is None or True:
            with ExitStack() as pe_:
                xo = [sb(pe_, f"xo{i}", [128, D], F32) for i in range(2)]
                for t in range(NT):
                    xi = xo[t % 2]
                    xk = ("xo", t % 2)
                    for half in range(2):
                        pb = PS[(2 * t + half) % 4]
                        pk = PSK[(2 * t + half) % 4]
                        for j in range(4):
                            c = half * 4 + j
                            T.op("pe", lambda: nc.tensor.transpose(pb[:, j * 128:(j + 1) * 128],
                                                                   xT[:, c, t * 128:(t + 1) * 128], cmat_f[:]),
                                 ["xT", "cmat_f"], [pk], inc=(j == 3))
                        cp(xi[:, half * 512:(half + 1) * 512], pb[:], [pk], [xk], eng="dve" if half == 0 else "act")
                    T.dma("sp", out_d[t * 128:(t + 1) * 128, :], xi[:], reads=[xk])
                T.finish("sp")
        build_program.stats = (T.n_inst, T.n_waits, len(T.sems))
        build_program.stuck = T.check_deadlock()
    return nc, dbg_out


_CACHE = {}


def kernel(**inputs):
    maps = prep_inputs(inputs)
    if "nc" not in _CACHE:
        _CACHE["nc"] = build_program()[0]
    nc = _CACHE["nc"]
    res = run_bass_kernel_spmd(nc, maps, core_ids=list(range(8)))
    out = np.stack([np.asarray(r["out"], dtype=np.float32) for r in res.results], axis=0)
    return out
```
